# Optimizing a Trainium2 kernel written in Bass

```python
import math
import jax
import jax.numpy as jnp
from jax import lax
import numpy as np

D_MODEL = 1024
BATCH = 4
SEQ = 4096
DEPTH = 4

GRID_W = 64
CTX_LEN = 256
N_MIXERS = 3
FFN_DIM = 256 * ((8 * D_MODEL // 3 + 255) // 256)
ADA_CHUNKS = 9
NORM_EPS = 1e-6
RET_HEADS = 4
RET_QK_HEAD = D_MODEL // RET_HEADS
RET_V_HEAD = 2 * D_MODEL // RET_HEADS
RET_QK = RET_HEADS * RET_QK_HEAD
RET_V = RET_HEADS * RET_V_HEAD
RET_CHUNK = 128
ROPE_BASE = 10000.0
POOL_WINDOWS = (2, 4, 8, 16)
POOL_GROUPS = len(POOL_WINDOWS)
POOL_GROUP_DIM = D_MODEL // POOL_GROUPS
HYENA_ORDER = 2
HYENA_EMB = 33
HYENA_BANDS = (HYENA_EMB - 1) // 2
HYENA_FILTER_WIDTH = 64
HYENA_SHORT = 3
HYENA_TARGET = 1e-2
HYENA_FAST = 0.3
HYENA_SLOW = 1.5
HYENA_MAX_DECAY = math.log(HYENA_TARGET) / HYENA_FAST
HYENA_MIN_DECAY = math.log(HYENA_TARGET) / HYENA_SLOW
N_RET = (DEPTH + 2) // 3
N_POOL = (DEPTH + 1) // 3
N_HYENA = DEPTH // 3

kernel_name = 'hybrid_retention_pool_hyena_macaron_dit'

F32 = jnp.float32


def _rmsnorm(x, g):
    xf = x.astype(F32)
    y = xf * lax.rsqrt(jnp.mean(xf * xf, axis=-1, keepdims=True) + NORM_EPS)
    return (y * g.astype(F32)).astype(x.dtype)


def _ada(cond, w, b):
    m = jax.nn.silu(cond) @ w + b
    return jnp.split(m[..., None, :], ADA_CHUNKS, axis=-1)


def _adaln(h, g, shift, scale):
    return _rmsnorm(h, g) * (1.0 + scale) + shift


def _half_ffn(h, mods, g, w1, w3, w2):
    shift, scale, gate = mods
    u = _adaln(h, g, shift, scale)
    return h + 0.5 * gate * ((jax.nn.silu(u @ w1) * (u @ w3)) @ w2)


def _rotary(x, pos):
    half = x.shape[-1] // 2
    inv = 1.0 / (ROPE_BASE ** jnp.linspace(0.0, 1.0, half, dtype=F32))
    ang = pos[:, None] * inv[None, :]
    cos = jnp.cos(ang)[None, :, None, :]
    sin = jnp.sin(ang)[None, :, None, :]
    x1, x2 = x[..., :half], x[..., half:]
    return jnp.concatenate([x1 * cos - x2 * sin, x2 * cos + x1 * sin], axis=-1)


def _retention_scan(q, k, v, log_gamma, state0, strict):
    bsz, nh, t, _ = q.shape
    dv = v.shape[-1]
    nc = t // RET_CHUNK

    def chunks(a):
        return jnp.moveaxis(a.reshape(bsz, nh, nc, RET_CHUNK, a.shape[-1]), 2, 0)

    n = jnp.arange(RET_CHUNK, dtype=F32)
    rel = n[:, None] - n[None, :]
    mask = (rel > 0) if strict else (rel >= 0)
    dmat = jnp.where(mask, jnp.exp(log_gamma[:, None, None] * jnp.maximum(rel, 0.0)), 0.0)
    q_dec = jnp.exp(log_gamma[:, None] * (n + 1.0))[..., None]
    k_dec = jnp.exp(log_gamma[:, None] * (RET_CHUNK - 1.0 - n))[..., None]
    c_dec = jnp.exp(log_gamma * RET_CHUNK)[:, None, None]

    def step(state, blk):
        qb, kb, vb = blk
        scores = jnp.einsum('bhnd,bhmd->bhnm', qb, kb) * dmat
        out = (jnp.einsum('bhnm,bhme->bhne', scores, vb)
               + jnp.einsum('bhnd,bhde->bhne', qb, state) * q_dec)
        state = state * c_dec + jnp.einsum('bhmd,bhme->bhde', kb * k_dec, vb)
        return state, out

    state, out = lax.scan(step, state0, (chunks(q), chunks(k), chunks(v)))
    out = jnp.moveaxis(out, 0, 2).reshape(bsz, nh, t, dv)
    return out, state


def _retention_readout(o, g, w_out):
    mu = jnp.mean(o, axis=-1, keepdims=True)
    var = jnp.mean(jnp.square(o - mu), axis=-1, keepdims=True)
    o = (o - mu) * lax.rsqrt(var + NORM_EPS)
    bsz, nh, t, dv = o.shape
    y = jnp.transpose(o, (0, 2, 1, 3)).reshape(bsz, t, nh * dv)
    y = jax.nn.silu(g) * y
    return y.astype(w_out.dtype) @ w_out


def _retention_mixer(uc, ul, w_in, w_out, decay, ctx_out):
    bsz, ctx_len, _ = uc.shape
    log_gamma = jax.nn.log_sigmoid(decay.astype(F32))

    def project(u, offset):
        t = u.shape[1]
        p = (u @ w_in).astype(F32)
        q, k, v, g = jnp.split(p, [RET_QK, 2 * RET_QK, 2 * RET_QK + RET_V], axis=-1)
        pos = jnp.arange(t, dtype=F32) + offset
        q = _rotary(q.reshape(bsz, t, RET_HEADS, RET_QK_HEAD), pos)
        k = _rotary(k.reshape(bsz, t, RET_HEADS, RET_QK_HEAD), pos) * (RET_QK_HEAD ** -0.5)
        v = v.reshape(bsz, t, RET_HEADS, RET_V_HEAD)
        heads = lambda a: jnp.transpose(a, (0, 2, 1, 3))
        return heads(q), heads(k), heads(v), g

    flip = lambda a: jnp.flip(a, axis=2)
    zero = jnp.zeros((bsz, RET_HEADS, RET_QK_HEAD, RET_V_HEAD), F32)
    qc, kc, vc, gc = project(uc, 0.0)
    oc_f, s_f = _retention_scan(qc, kc, vc, log_gamma[0], zero, False)
    oc_b, s_b = _retention_scan(flip(qc), flip(kc), flip(vc), log_gamma[1], zero, True)
    ql, kl, vl, gl = project(ul, float(ctx_len))
    ol_f, _ = _retention_scan(ql, kl, vl, log_gamma[0], s_f, False)
    ol_b, _ = _retention_scan(flip(ql), flip(kl), flip(vl), log_gamma[1], s_b, True)
    yl = _retention_readout(ol_f + flip(ol_b), gl, w_out).astype(ul.dtype)
    yc = _retention_readout(oc_f + flip(oc_b), gc, w_out).astype(uc.dtype) if ctx_out else None
    return yc, yl


def _multiscale_pool(u, w_grp, b_grp, scale):
    bsz, r, w, d = u.shape
    uf = u.astype(F32)
    cs = jnp.pad(lax.cumsum(uf, axis=2), ((0, 0), (0, 0), (1, 0), (0, 0)))
    pos = jnp.arange(w)
    diffs = []
    for gi, win in enumerate(POOL_WINDOWS):
        lo = jnp.clip(pos - win // 2, 0, w)
        hi = jnp.clip(pos - win // 2 + win, 0, w)
        sl = slice(gi * POOL_GROUP_DIM, (gi + 1) * POOL_GROUP_DIM)
        csg = cs[..., sl]
        mean = (jnp.take(csg, hi, axis=2) - jnp.take(csg, lo, axis=2)) / (hi - lo).astype(F32)[:, None]
        diffs.append(mean - uf[..., sl])
    dlt = jnp.stack(diffs, axis=-2)
    y = jnp.einsum('brwgc,gce->brwge', dlt, w_grp.astype(F32)).reshape(bsz, r, w, d)
    return ((y + b_grp.astype(F32)) * scale.astype(F32)).astype(u.dtype)


def _hyena_spectra(length, w_pos, b_pos, w_mid, b_mid, freq, w_filt):
    t = jnp.linspace(0.0, 1.0, length, dtype=F32)[:, None]
    bands = jnp.linspace(1e-4, HYENA_BANDS - 1, HYENA_BANDS, dtype=F32)
    ang = (2.0 * math.pi / length) * jnp.arange(length, dtype=F32)[:, None] * bands[None, :]
    feat = jnp.concatenate([t, jnp.cos(ang), -jnp.sin(ang)], axis=-1)
    fr = freq.astype(F32)
    hdn = jnp.sin(fr * (feat @ w_pos.astype(F32) + b_pos.astype(F32)))
    hdn = jnp.sin(fr * (hdn @ w_mid.astype(F32) + b_mid.astype(F32)))
    h = (hdn @ w_filt.astype(F32)).reshape(length, HYENA_ORDER, 2, D_MODEL)
    deltas = jnp.abs(jnp.linspace(HYENA_MIN_DECAY, HYENA_MAX_DECAY, D_MODEL, dtype=F32))
    h = h * jnp.exp(-t * deltas[None, :])[:, None, None, :]
    h_fwd, h_bwd = h[:, :, 0], h[:, :, 1]
    two_sided = jnp.concatenate(
        [h_fwd, jnp.zeros((1, HYENA_ORDER, D_MODEL), F32), h_bwd[:0:-1]], axis=0)
    two_sided = two_sided * lax.rsqrt(jnp.sum(jnp.square(two_sided), axis=0, keepdims=True) + NORM_EPS)
    return jnp.fft.rfft(two_sided, axis=0)


def _long_conv(u, spec, bias):
    length = u.shape[1]
    uf = jnp.fft.rfft(u, n=2 * length, axis=1)
    y = jnp.fft.irfft(uf * spec[None], n=2 * length, axis=1)[:, :length]
    return y + u * bias


def _hyena_mixer(u, w_in, b_in, w_short, b_short, w_pos, b_pos, w_mid, b_mid, freq, w_filt, fbias, w_out, b_out):
    bsz, t, d = u.shape
    p = u @ w_in + b_in
    p = lax.conv_general_dilated(
        p, w_short[:, None, :], window_strides=(1,), padding=[(1, 1)],
        dimension_numbers=('NWC', 'WIO', 'NWC'), feature_group_count=3 * d) + b_short
    v, x1, x2 = jnp.split(p.astype(F32), 3, axis=-1)
    spec = _hyena_spectra(t, w_pos, b_pos, w_mid, b_mid, freq, w_filt)
    fb = fbias.astype(F32)
    z = x1 * _long_conv(v, spec[:, 0], fb[0])
    z = x2 * _long_conv(z, spec[:, 1], fb[1])
    return z.astype(u.dtype) @ w_out + b_out


def setup_inputs(seed: int = 0) -> dict:
    key = jax.random.key(seed)
    keys = list(jax.random.split(key, 32))

    def nrm(shape, scale):
        return scale * jax.random.normal(keys.pop(), shape, F32)

    d = D_MODEL
    decay_init = jnp.log(2.0 ** (5.0 + jnp.arange(RET_HEADS, dtype=F32)) - 1.0)
    return {
        'x': nrm((BATCH, SEQ, d), 1.0),
        'c': nrm((BATCH, d), 1.0),
        'ctx': nrm((BATCH, CTX_LEN, d), 1.0),
        'c_ctx': nrm((d,), 1.0),
        'ada_w': nrm((DEPTH, d, ADA_CHUNKS * d), 0.5 * d ** -0.5),
        'ada_b': nrm((DEPTH, ADA_CHUNKS * d), 0.02),
        'norm_g': 1.0 + nrm((DEPTH, 3, d), 0.02),
        'ffn_w1': nrm((DEPTH, 2, d, FFN_DIM), d ** -0.5),
        'ffn_w3': nrm((DEPTH, 2, d, FFN_DIM), d ** -0.5),
        'ffn_w2': nrm((DEPTH, 2, FFN_DIM, d), FFN_DIM ** -0.5),
        'ret_w_in': nrm((N_RET, d, 2 * RET_QK + 2 * RET_V), d ** -0.5),
        'ret_w_out': nrm((N_RET, RET_V, d), RET_V ** -0.5),
        'ret_decay': decay_init + nrm((N_RET, 2, RET_HEADS), 0.1),
        'pool_w': nrm((N_POOL, POOL_GROUPS, POOL_GROUP_DIM, POOL_GROUP_DIM), POOL_GROUP_DIM ** -0.5),
        'pool_b': nrm((N_POOL, d), 0.02),
        'pool_scale': 1.0 + nrm((N_POOL, d), 0.02),
        'hy_w_in': nrm((N_HYENA, d, 3 * d), d ** -0.5),
        'hy_b_in': nrm((N_HYENA, 3 * d), 0.02),
        'hy_w_short': nrm((N_HYENA, HYENA_SHORT, 3 * d), HYENA_SHORT ** -0.5),
        'hy_b_short': nrm((N_HYENA, 3 * d), 0.02),
        'hy_w_pos': nrm((N_HYENA, HYENA_EMB, HYENA_FILTER_WIDTH), HYENA_EMB ** -0.5),
        'hy_b_pos': nrm((N_HYENA, HYENA_FILTER_WIDTH), 0.5),
        'hy_w_mid': nrm((N_HYENA, HYENA_FILTER_WIDTH, HYENA_FILTER_WIDTH), HYENA_FILTER_WIDTH ** -0.5),
        'hy_b_mid': nrm((N_HYENA, HYENA_FILTER_WIDTH), 0.5),
        'hy_freq': 1.0 + nrm((N_HYENA, HYENA_FILTER_WIDTH), 0.02),
        'hy_w_filt': nrm((N_HYENA, HYENA_FILTER_WIDTH, HYENA_ORDER * 2 * d), HYENA_FILTER_WIDTH ** -0.5),
        'hy_bias': nrm((N_HYENA, HYENA_ORDER, d), 0.5),
        'hy_w_out': nrm((N_HYENA, d, d), d ** -0.5),
        'hy_b_out': nrm((N_HYENA, d), 0.02),
        'final_g': 1.0 + nrm((d,), 0.02),
    }


def reference(x, c, ctx, c_ctx, ada_w, ada_b, norm_g, ffn_w1, ffn_w3, ffn_w2,
              ret_w_in, ret_w_out, ret_decay, pool_w, pool_b, pool_scale,
              hy_w_in, hy_b_in, hy_w_short, hy_b_short, hy_w_pos, hy_b_pos, hy_w_mid, hy_b_mid,
              hy_freq, hy_w_filt, hy_bias, hy_w_out, hy_b_out, final_g):
    bsz, seq, d = x.shape
    rows = seq // GRID_W
    h_lat, h_ctx = x, ctx
    for layer in range(DEPTH):
        kind = layer % N_MIXERS
        slot = layer // N_MIXERS
        last = layer == DEPTH - 1
        ctx_out = not last
        ctx_live = ctx_out or kind == 0
        ml = _ada(c, ada_w[layer], ada_b[layer])
        mc = _ada(c_ctx, ada_w[layer], ada_b[layer])
        h_lat = _half_ffn(h_lat, ml[0:3], norm_g[layer, 0], ffn_w1[layer, 0], ffn_w3[layer, 0], ffn_w2[layer, 0])
        if ctx_live:
            h_ctx = _half_ffn(h_ctx, mc[0:3], norm_g[layer, 0], ffn_w1[layer, 0], ffn_w3[layer, 0], ffn_w2[layer, 0])
        ul = _adaln(h_lat, norm_g[layer, 1], ml[3], ml[4])
        uc = _adaln(h_ctx, norm_g[layer, 1], mc[3], mc[4]) if ctx_live else None
        if kind == 0:
            yc, yl = _retention_mixer(uc, ul, ret_w_in[slot], ret_w_out[slot], ret_decay[slot], ctx_out)
        elif kind == 1:
            pp = (pool_w[slot], pool_b[slot], pool_scale[slot])
            yl = _multiscale_pool(ul.reshape(bsz, rows, GRID_W, d), *pp).reshape(bsz, seq, d)
            yc = _multiscale_pool(uc[:, None], *pp)[:, 0] if ctx_out else None
        else:
            hp = (hy_w_in[slot], hy_b_in[slot], hy_w_short[slot], hy_b_short[slot], hy_w_pos[slot],
                  hy_b_pos[slot], hy_w_mid[slot], hy_b_mid[slot], hy_freq[slot], hy_w_filt[slot],
                  hy_bias[slot], hy_w_out[slot], hy_b_out[slot])
            yl = _hyena_mixer(ul, *hp)
            yc = _hyena_mixer(uc, *hp) if ctx_out else None
        h_lat = h_lat + ml[5] * yl
        if ctx_out:
            h_ctx = h_ctx + mc[5] * yc
        h_lat = _half_ffn(h_lat, ml[6:9], norm_g[layer, 2], ffn_w1[layer, 1], ffn_w3[layer, 1], ffn_w2[layer, 1])
        if ctx_out:
            h_ctx = _half_ffn(h_ctx, mc[6:9], norm_g[layer, 2], ffn_w1[layer, 1], ffn_w3[layer, 1], ffn_w2[layer, 1])
    return _rmsnorm(h_lat, final_g)
```

```python
import contextlib
import math
import numpy as np
import ml_dtypes
import concourse.bass as bass
import concourse.mybir as mybir
from concourse.bass_utils import run_bass_kernel_spmd

F32 = mybir.dt.float32
BF16 = mybir.dt.bfloat16
AF = mybir.ActivationFunctionType
ALU = mybir.AluOpType
AX = mybir.AxisListType

D = 1024
SEQ = 4096
CTX = 256
NT = SEQ + CTX
DEPTH = 4
FFN = 2816
NFC = FFN // 128
EPS = 1e-6
NCORES = 4
TEST_CORES = None
DEBUG_DUMP = False
DEBUG_OUT = {}
ENGS = ("tensor", "vector", "scalar", "gpsimd", "sync")

CTX_TILE = (0, CTX, 1)
LAT_TILES = [(CTX + 512 * i, 512, 0) for i in range(8)]


class _Op:
    __slots__ = ("eng", "fn", "waits", "tick", "needs_inc", "is_dma", "sem", "seq")


class Emitter:
    def __init__(self, nc):
        self.nc = nc
        self.ops = {e: [] for e in ENGS}
        self.last_w = {}
        self.readers = {}
        self.dma_sems = {}
        self.eng_sem = {}
        self.pending_dma = []
        self.cur_map = {}
        arena = nc.alloc_sbuf_tensor("arena", [128, 52000], F32)
        base = nc.lookup_mloc(arena).addr
        self.sb_base = base
        self.sb_top = base
        self.sb_cnt = 0
        self.sb_limit = base + 52000 * 4

    def _alloc(self, name, shape, dtype, off):
        self.sb_cnt += 1
        return self.nc.alloc_sbuf_tensor_at("%s_%d" % (name, self.sb_cnt), list(shape), dtype, offset=off)

    @staticmethod
    def _bytes(shape, dtype):
        n = 1
        for s in shape[1:]:
            n *= s
        return n * (2 if dtype == BF16 else 4)

    def persist(self, name, shape, dtype):
        assert self.sb_top == self.sb_base
        off = self.sb_base
        self.sb_base += (self._bytes(shape, dtype) + 63) // 64 * 64
        self.sb_top = self.sb_base
        assert self.sb_base <= self.sb_limit
        return self._alloc(name, shape, dtype, off)

    def tmp(self, name, shape, dtype):
        off = self.sb_top
        self.sb_top += (self._bytes(shape, dtype) + 63) // 64 * 64
        assert self.sb_top <= self.sb_limit, (name, self.sb_top)
        return self._alloc(name, shape, dtype, off)

    def phase(self):
        self.barrier()
        self.sb_top = self.sb_base

    def _deps(self, op, reads, writes):
        deps = []
        for k in reads:
            w = self.last_w.get(k)
            if w is not None:
                deps.append(w)
        for k in writes:
            w = self.last_w.get(k)
            if w is not None:
                deps.append(w)
            deps.extend(self.readers.get(k, ()))
        self._add_waits(op, deps)
        for k in reads:
            self.readers.setdefault(k, []).append(op)
        for k in writes:
            self.last_w[k] = op
            self.readers[k] = []

    def _add_waits(self, op, deps):
        seen = set(id(d) for d in op.waits)
        latest = {}
        rest = []
        for d in deps:
            if d.is_dma:
                rest.append(d)
            elif d.eng not in latest or d.seq > latest[d.eng].seq:
                latest[d.eng] = d
        deps = rest + list(latest.values())
        for d in deps:
            if d is op or id(d) in seen:
                continue
            seen.add(id(d))
            if d.eng == op.eng and d.eng == "tensor" and not d.is_dma and not op.is_dma:
                continue
            if not d.is_dma:
                d.needs_inc = True
            op.waits.append(d)

    def op(self, eng, fn, reads=(), writes=()):
        o = _Op()
        o.eng = eng; o.fn = fn; o.waits = []; o.tick = None; o.needs_inc = False
        o.is_dma = False; o.sem = None
        o.seq = len(self.ops[eng])
        self._deps(o, reads, writes)
        self.ops[eng].append(o)
        return o

    def dma(self, eng, fns, semkey, reads=(), writes=()):
        o = _Op()
        o.eng = eng; o.fn = fns; o.waits = []; o.needs_inc = False
        o.is_dma = True
        slot = self.cur_map.setdefault(semkey, len(self.cur_map))
        ent = self.dma_sems.setdefault(slot, [None, 0])
        ent[1] += 16 * len(fns)
        o.sem = slot; o.tick = ent[1]
        o.seq = len(self.ops[eng])
        self._deps(o, reads, writes)
        self.ops[eng].append(o)
        self.pending_dma.append(o)
        return o

    def barrier(self):
        lasts = []
        for e in ENGS:
            for o in reversed(self.ops[e]):
                if not o.is_dma:
                    if o.fn is not None:
                        lasts.append(o)
                    break
        deps = lasts + self.pending_dma
        self.pending_dma = []
        self.cur_map = {}
        for e in ENGS:
            o = _Op()
            o.eng = e; o.fn = None; o.waits = []; o.tick = None; o.needs_inc = False
            o.is_dma = False; o.sem = None
            o.seq = len(self.ops[e])
            self._add_waits(o, deps)
            self.ops[e].append(o)
        self.last_w = {}
        self.readers = {}

    def emit(self, stack):
        nc = self.nc
        for e in ENGS:
            self.eng_sem[e] = stack.enter_context(nc.semaphore("s_" + e))
        for i, (k, ent) in enumerate(self.dma_sems.items()):
            ent[0] = stack.enter_context(nc.semaphore("d_%d" % i))
        for e in ENGS:
            t = 0
            for o in self.ops[e]:
                if not o.is_dma and o.needs_inc:
                    assert o.fn is not None
                    t += 1
                    o.tick = t
        block = stack.enter_context(nc.Block())
        em = self

        def run(engname):
            def body(eng):
                waited = {}
                for o in em.ops[engname]:
                    for d in o.waits:
                        if d.is_dma:
                            sem = em.dma_sems[d.sem][0]; key = ("d", d.sem)
                        else:
                            sem = em.eng_sem[d.eng]; key = ("e", d.eng)
                        if waited.get(key, 0) >= d.tick:
                            continue
                        waited[key] = d.tick
                        eng.wait_ge(sem, d.tick)
                    if o.is_dma:
                        sem = em.dma_sems[o.sem][0]
                        for f in o.fn:
                            f(eng).then_inc(sem, 16)
                    elif o.fn is not None:
                        ins = o.fn(eng)
                        if o.needs_inc:
                            ins.then_inc(em.eng_sem[engname], 1)
            return body

        block.tensor(run("tensor"))
        block.vector(run("vector"))
        block.scalar(run("scalar"))
        block.gpsimd(run("gpsimd"))
        block.sync(run("sync"))


def _fm(vec):
    v = np.asarray(vec, np.float32)
    lead = v.shape[:-1]
    n = v.shape[-1] // 128
    v = v.reshape(*lead, n, 128)
    return np.ascontiguousarray(np.moveaxis(v, -1, 0))


def _rot_tables():
    half = 128
    inv = 1.0 / (10000.0 ** np.linspace(0.0, 1.0, half, dtype=np.float32).astype(np.float64))
    pos = np.arange(NT, dtype=np.float64)
    ang = (pos[None, :].astype(np.float32) * inv[:, None].astype(np.float32)).astype(np.float64)
    return np.cos(ang).astype(np.float32), np.sin(ang).astype(np.float32)


def _ret_consts():
    n = np.arange(128)
    m = np.arange(128)[:, None]
    nn = n[None, :]
    c = {}
    c["rel"] = np.stack([np.maximum(nn - m, 0), np.maximum(m - nn, 0)]).astype(np.float32)
    c["mask"] = np.stack([(nn >= m), (m > nn)]).astype(np.float32) / 16.0
    c["qexp"] = np.stack([np.broadcast_to(n + 1.0, (128, 128)), np.broadcast_to(128.0 - n, (128, 128))]).astype(np.float32)
    c["kexp"] = np.stack([127.0 - n, n * 1.0]).astype(np.float32)
    return c


class Prog:
    def __init__(self, stop_after=None):
        self.stop_after = stop_after
        nc = self.nc = bass.Bass("TRN2", target_bir_lowering=False)
        self.E = Emitter(nc)
        self.inputs = {}
        self.stopped = False

    _SPECS = {
        "h0": ([D, NT], F32), "vecs": ([128, None], F32), "ada_w": ([DEPTH, D, 9 * D], F32),
        "ffn_w1": ([DEPTH, 2, D, FFN], F32), "ffn_w3": ([DEPTH, 2, D, FFN], F32), "ffn_w2": ([DEPTH, 2, FFN, D], F32),
        "ret_w_in": ([2, D, 6144], F32), "ret_w_out": ([2, 2048, D], F32), "pool_w": ([4, 256, 256], F32),
        "rot": ([2, 128, NT], F32), "retc": ([128, None], F32), "poolrc": ([128, 4, 768], F32), "identb": ([128, 128], BF16),
        "hy_w_in": ([D, 3 * D], F32), "hy_w_out": ([D, D], F32), "hy_w_pos": ([33, 64], F32), "hy_w_mid": ([64, 64], F32),
        "hy_w_filt": ([64, 4 * D], F32), "hyv64": ([64, 8], F32), "hy_delta": ([128, D], F32), "hyb_bc": ([128, 2 * D], F32),
    }

    def __getattr__(self, name):
        specs = type(self)._SPECS
        if name in specs:
            shape, dt = specs[name]
            shape = [NVEC if (x is None and name == "vecs") else (RETC_N if x is None else x) for x in shape]
            t = self.din(name, shape, dt)
            self.__dict__[name] = t
            return t
        raise AttributeError(name)

    def get_hyd(self, nm):
        if nm not in self.hyd:
            nc = self.nc
            L = SEQ if nm == "l" else CTX
            nsc = L // 128
            dk = "ExternalOutput" if DEBUG_DUMP else "Internal"
            self.hyd[nm] = dict(
                featT=self.din("featT_" + nm, [33, L]),
                tneg=self.din("tneg_" + nm, [128, nsc]),
                FT=self.din("FT_" + nm, [2 * nsc, 128, nsc, 128], BF16),
                GT=self.din("GT_" + nm, [nsc, 128, 2 * nsc, 128], BF16),
                HSD=nc.dram_tensor("HSD_" + nm, [nsc, 128, 2, 2 * D], BF16, kind=dk).ap(),
                KS=nc.dram_tensor("KS_" + nm, [2, 2, L, D], F32, kind=dk).ap(),
                VX=nc.dram_tensor("VX_" + nm, [3, 8, 128, nsc, 128], BF16, kind=dk).ap(),
                ZT=nc.dram_tensor("ZT_" + nm, [D, L], BF16, kind=dk).ap(),
            )
        return self.hyd[nm]

    def din(self, name, shape, dtype=F32):
        t = self.nc.dram_tensor(name, list(shape), dtype, kind="ExternalInput").ap()
        self.inputs[name] = t
        return t

    def build(self):
        nc, E = self.nc, self.E
        _ = (self.h0, self.vecs, self.identb)
        self.hyd = {}
        self.outT = nc.dram_tensor("outT", [D, SEQ], F32, kind="ExternalOutput").ap()
        dk = "ExternalOutput" if DEBUG_DUMP else "Internal"
        self.H = nc.dram_tensor("Hres", [D, NT], F32, kind=dk).ap()
        self.QT = nc.dram_tensor("QT", [D, NT], BF16).ap()
        self.KT = nc.dram_tensor("KT", [D, NT], BF16).ap()
        self.VT = nc.dram_tensor("VT", [NT, 2048], BF16).ap()
        self.GT = nc.dram_tensor("GT", [NT, 2048], BF16).ap()
        self.OF = nc.dram_tensor("OF", [NT, 2048], F32).ap()

        self.ps = [nc.alloc_psum_tensor("ps%d" % i, [128, 512], F32) for i in range(8)]

        self.V = E.persist("vecs", [128, NVEC], F32)
        self.ones = E.persist("ones", [128, 128], BF16)
        self.ident = E.persist("ident", [128, 128], BF16)
        self.AD = E.persist("AD", [128, DEPTH, 3, 3, 8, 2], F32)
        self.finA = E.persist("finA", [128, 8], F32)
        self.epsD = E.persist("epsD", [128, 1], F32)
        self.eps1 = E.persist("eps1", [128, 1], F32)

        E.dma("sync", [lambda e: e.dma_start(out=self.V[:], in_=self.vecs)], "vecs", writes=["vecs"])
        E.dma("sync", [lambda e: e.dma_start(out=self.ident[:], in_=self.identb)], "ident", writes=["ident"])
        E.op("vector", lambda e: e.memset(self.ones[:], 1.0), writes=["ones"])
        E.op("vector", lambda e: e.memset(self.epsD[:], D * EPS), writes=["epsD"])
        E.op("vector", lambda e: e.memset(self.eps1[:], EPS), writes=["eps1"])
        E.op("vector", lambda e: e.tensor_scalar(out=self.finA[:], in0=self.vsl("final_g"), scalar1=32.0, scalar2=None, op0=ALU.mult),
             reads=["vecs"], writes=["finA"])
        E.barrier()

        self.mods_all()
        if self.stop_after is not None and self.stop_after.startswith("HY"):
            stage = int(self.stop_after[2])
            big = len(self.stop_after) > 3
            L_, col_, cond_, nm_ = (SEQ, CTX, 0, "l") if big else (CTX, 0, 1, "c")
            E.phase()
            tb_ = E.tmp("cp", [128, 8, 512], F32)
            for t0 in range(0, L_, 512):
                n = min(512, L_ - t0)
                E.dma("sync", [lambda e, t0=t0, n=n: e.dma_start(out=tb_[:, :, 0:n], in_=self.h0.rearrange("(c p) t -> p c t", p=128)[:, :, col_ + t0:col_ + t0 + n])], "cp", writes=["cp"])
                E.dma("sync", [lambda e, t0=t0, n=n: e.dma_start(out=self.H.rearrange("(c p) t -> p c t", p=128)[:, :, col_ + t0:col_ + t0 + n], in_=tb_[:, :, 0:n])], "cp2", reads=["cp"], writes=["Hcp"])
            self._hy_filter(L_, nm_)
            if stage >= 2:
                self._hy_proj(2, L_, col_, cond_, nm_)
            if stage >= 3:
                self._hy_conv(L_, nm_)
            if stage >= 4:
                self._hy_out(2, L_, col_, cond_, nm_)
            self.finish()
            return
        src = self.h0
        for layer in range(DEPTH):
            kind = layer % 3
            slot = layer // 3
            last = layer == DEPTH - 1
            ctx_out = not last
            tiles_a = [CTX_TILE] + LAT_TILES
            tiles_b = ([CTX_TILE] if ctx_out else []) + LAT_TILES
            self.ffn_half(layer, 0, tiles_a, src)
            src = self.H
            if self.check_stop("L%da" % layer):
                return
            if kind == 0:
                self.retention(layer, slot, ctx_out)
            elif kind == 1:
                self.pool(layer, slot, tiles_b)
            else:
                self.hyena(layer, slot, ctx_out)
            if self.check_stop("L%db" % layer):
                return
            self.ffn_half(layer, 1, tiles_b, self.H)
            if self.check_stop("L%dc" % layer):
                return
        self.final_norm()
        self.finish()

    def check_stop(self, tag):
        if self.stop_after == tag:
            self.dump_h()
            self.finish()
            self.stopped = True
            return True
        return False

    def finish(self):
        E = self.E
        E.barrier()

    def dump_h(self):
        E = self.E
        E.phase()
        bufs = [E.tmp("dump%d" % i, [128, 8, 512], F32) for i in range(2)]
        for i, (c0, n, cond) in enumerate(LAT_TILES):
            t = bufs[i % 2]
            E.dma("sync", [lambda e, t=t, c0=c0, n=n: e.dma_start(out=t[:], in_=self.H.rearrange("(c p) t -> p c t", p=128)[:, :, c0:c0 + n])],
                  ("dl", i % 2), writes=[("dump", i % 2)])
            E.dma("sync", [lambda e, t=t, c0=c0, n=n: e.dma_start(out=self.outT.rearrange("(c p) t -> p c t", p=128)[:, :, c0 - CTX:c0 - CTX + n], in_=t[:])],
                  ("ds", i % 2), reads=[("dump", i % 2)], writes=[("dumpo", i)])

    def vsl(self, name, *idx):
        off, shape = VEC_LAYOUT[name]
        n = int(np.prod(shape))
        ap = self.V[:, off:off + n]
        return ap

    def vcol(self, name, flat_index):
        off, shape = VEC_LAYOUT[name]
        return self.V[:, off + flat_index:off + flat_index + 1]

    def mods_all(self):
        E = self.E
        E.phase()
        sc = E.tmp("sc", [128, 8, 2], BF16)
        mods = E.tmp("mods", [128, 72, 2], F32)
        wb = [E.tmp("adaw%d" % i, [128, 8, 1152], BF16) for i in range(2)]
        E.op("scalar", lambda e: e.activation(out=sc[:].rearrange("p c k -> p (c k)"), in_=self.vsl("cond"), func=AF.Silu),
             reads=["vecs"], writes=["sc"])
        gi = 0
        for layer in range(DEPTH):
            if self.stop_after is not None and self.stop_after.startswith("HY") and layer != 2:
                continue
            for g in range(8):
                b = gi % 2
                gi += 1
                src = self.ada_w[layer].rearrange("(kc p) f -> p kc f", p=128)[:, :, g * 1152:(g + 1) * 1152]
                E.dma("gpsimd", [lambda e, b=b, src=src: e.dma_start(out=wb[b][:], in_=src)], ("adaw", b), writes=[("adaw", b)])
                for j in range(9):
                    oc = g * 9 + j
                    for kc in range(8):
                        E.op("tensor", lambda e, b=b, j=j, kc=kc, oc=oc: e.matmul(self.ps[0][:, oc * 2:oc * 2 + 2], lhsT=wb[b][:, kc, j * 128:(j + 1) * 128],
                                                                                 rhs=sc[:, kc, :], start=(kc == 0), stop=(kc == 7)),
                             reads=[("adaw", b), "sc"], writes=["psmods"])
            off, _ = VEC_LAYOUT["ada_b"]
            E.op("vector", lambda e, layer=layer, off=off: e.tensor_tensor(out=mods[:].rearrange("p c k -> p (c k)"), in0=self.ps[0][:, 0:144],
                                                                           in1=self.V[:, off + layer * 144: off + (layer + 1) * 144], op=ALU.add),
                 reads=["psmods", "vecs"], writes=["mods"])
            ngo, _ = VEC_LAYOUT["norm_g"]
            for n in range(3):
                shift = mods[:, (3 * n) * 8:(3 * n + 1) * 8, :]
                scale = mods[:, (3 * n + 1) * 8:(3 * n + 2) * 8, :]
                gate = mods[:, (3 * n + 2) * 8:(3 * n + 3) * 8, :]
                for cond in range(2):
                    gsl = self.V[:, ngo + (layer * 3 + n) * 8: ngo + (layer * 3 + n + 1) * 8]
                    E.op("vector", lambda e, layer=layer, n=n, cond=cond, scale=scale, gsl=gsl: e.scalar_tensor_tensor(
                        out=self.AD[:, layer, n, 0, :, cond], in0=scale[:, :, cond], scalar=1.0, in1=gsl, op0=ALU.add, op1=ALU.mult),
                        reads=["mods", "vecs"], writes=["AD"])
                E.op("vector", lambda e, layer=layer, n=n: e.tensor_scalar(out=self.AD[:, layer, n, 0, :, :], in0=self.AD[:, layer, n, 0, :, :],
                                                                           scalar1=32.0, scalar2=None, op0=ALU.mult), reads=["AD"], writes=["AD"])
                E.op("vector", lambda e, layer=layer, n=n, shift=shift: e.tensor_copy(out=self.AD[:, layer, n, 1, :, :], in_=shift), reads=["mods"], writes=["AD"])
                gmul = 1.0 if n == 1 else 0.5
                E.op("vector", lambda e, layer=layer, n=n, gate=gate, gmul=gmul: e.tensor_scalar(out=self.AD[:, layer, n, 2, :, :], in0=gate,
                                                                                                 scalar1=gmul, scalar2=None, op0=ALU.mult), reads=["mods"], writes=["AD"])
        E.barrier()

    def adaln(self, h, n, layer, nrm, cond, out, key_in, key_out, sq, rstd, tmp, psb, A=None, shift=None, idx=0, out3=None):
        E = self.E
        sqk = ("sq", idx % 2)
        for c in range(8):
            E.op("scalar", lambda e, c=c: e.activation(out=sq[:, c, 0:n], in_=h[:, c, 0:n], func=AF.Square), reads=[key_in], writes=[sqk])
        for c in range(8):
            E.op("tensor", lambda e, c=c: e.matmul(self.ps[psb][:, 0:n], lhsT=self.ones[:], rhs=sq[:, c, 0:n], start=(c == 0), stop=(c == 7)),
                 reads=[sqk], writes=[("ps", psb)])
        rk = ("rstd", idx % 2)
        E.op("scalar", lambda e: e.activation(out=rstd[:, 0:n], in_=self.ps[psb][:, 0:n], func=AF.Ln, bias=self.epsD[:, 0:1], scale=1.0),
             reads=[("ps", psb)], writes=[rk])
        E.op("scalar", lambda e: e.activation(out=rstd[:, 0:n], in_=rstd[:, 0:n], func=AF.Exp, scale=-0.5),
             reads=[rk], writes=[rk])
        for c in range(8):
            tk = ("tmpn", c % 2)
            tt = tmp[c % 2]
            Ac = A(c) if A is not None else self.AD[:, layer, nrm, 0, c, cond:cond + 1]
            E.op("vector", lambda e, c=c, tt=tt: e.tensor_tensor(out=tt[:, 0:n], in0=h[:, c, 0:n], in1=rstd[:, 0:n], op=ALU.mult),
                 reads=[key_in, rk], writes=[tk])
            if shift is None:
                sh = self.AD[:, layer, nrm, 1, c, cond:cond + 1]
            else:
                sh = shift(c)
            src_ap = tt[:, 0:n] if out3 is None else tt[:, 0:n].rearrange("p (r w) -> p r w", w=out3[1])
            E.op("scalar", lambda e, c=c, src_ap=src_ap, Ac=Ac, sh=sh: e.activation(out=out(c), in_=src_ap, func=AF.Identity, bias=sh, scale=Ac),
                 reads=[tk, "AD"], writes=[key_out])

    def ffn_half(self, layer, which, tiles, src):
        E = self.E
        nrm = 0 if which == 0 else 2
        sts = []
        cur, cnt = [], 0
        for t in tiles:
            if cnt + t[1] > 2304:
                sts.append(cur)
                cur, cnt = [], 0
            cur.append(t)
            cnt += t[1]
        sts.append(cur)
        srcv = src.rearrange("(c p) t -> p c t", p=128)
        dstv = self.H.rearrange("(c p) t -> p c t", p=128)
        w1v = self.ffn_w1[layer, which].rearrange("(kc p) f -> p kc f", p=128)
        w3v = self.ffn_w3[layer, which].rearrange("(kc p) f -> p kc f", p=128)
        w2v = self.ffn_w2[layer, which].rearrange("(fc p) d -> p fc d", p=128)
        for st in sts:
            self._ffn_st(layer, nrm, st, srcv, dstv, w1v, w3v, w2v)

    def _ffn_st(self, layer, nrm, st, srcv, dstv, w1v, w3v, w2v):
        E = self.E
        if True:
            E.phase()
            ntok = sum(t[1] for t in st)
            hbuf = E.tmp("h", [128, 8, ntok], F32)
            ubuf = E.tmp("u", [128, 8, ntok], BF16)
            sq = E.tmp("sq", [128, 8, 512], BF16)
            rstd = E.tmp("rstd", [128, 512], F32)
            tmpn = [E.tmp("tmpn%d" % i, [128, 512], F32) for i in range(2)]
            GS = 4
            groups = [(f0, min(GS, NFC - f0)) for f0 in range(0, NFC, GS)]
            w13 = [E.tmp("w13_%d" % i, [128, 2, 8, GS * 128], BF16) for i in range(2)]
            w2 = [E.tmp("w2_%d" % i, [128, GS, D], BF16) for i in range(2)]
            gb = [E.tmp("g%d" % i, [128, GS, 512], BF16) for i in range(2)]
            sl = [E.tmp("sl%d" % i, [128, 512], F32) for i in range(2)]
            offs = []
            o = 0
            for t in st:
                offs.append(o)
                o += t[1]
            for ti, (c0, n, cond) in enumerate(st):
                o = offs[ti]
                E.dma("sync", [lambda e, o=o, c0=c0, n=n: e.dma_start(out=hbuf[:, :, o:o + n], in_=srcv[:, :, c0:c0 + n])],
                      ("hld", ti), writes=[("h", ti)])

            def do_adaln(ti):
                c0, n, cond = st[ti]
                o = offs[ti]
                self.adaln(hbuf[:, :, o:o + n], n, layer, nrm, cond, lambda c, o=o, n=n: ubuf[:, c, o:o + n],
                           ("h", ti), ("u", ti), sq, rstd, tmpn, 6, idx=ti)

            do_adaln(0)
            items = [(gi_, ti) for gi_ in range(len(groups)) for ti in range(len(st))]

            def load_w(gi_):
                b = gi_ % 2
                f0, gsz = groups[gi_]
                E.dma("gpsimd", [lambda e: e.dma_start(out=w13[b][:, 0, :, 0:gsz * 128], in_=w1v[:, :, f0 * 128:(f0 + gsz) * 128]),
                                 lambda e: e.dma_start(out=w13[b][:, 1, :, 0:gsz * 128], in_=w3v[:, :, f0 * 128:(f0 + gsz) * 128]),
                                 lambda e: e.dma_start(out=w2[b][:, 0:gsz, :], in_=w2v[:, f0:f0 + gsz, :])],
                      ("ffw", b), writes=[("ffw", b)])

            def stage_a(it, k):
                gi_, ti = it
                b = gi_ % 2
                f0, gsz = groups[gi_]
                c0, n, cond = st[ti]
                o = offs[ti]
                gk = k % 2
                for j in range(gsz):
                    pa, pb = (2 * j) % 4, (2 * j + 1) % 4
                    for kc in range(8):
                        E.op("tensor", lambda e, kc=kc, j=j, pa=pa: e.matmul(self.ps[pa][:, 0:n], lhsT=w13[b][:, 0, kc, j * 128:(j + 1) * 128],
                                                                             rhs=ubuf[:, kc, o:o + n], start=(kc == 0), stop=(kc == 7)),
                             reads=[("ffw", b), ("u", ti)], writes=[("ps", pa)])
                    for kc in range(8):
                        E.op("tensor", lambda e, kc=kc, j=j, pb=pb: e.matmul(self.ps[pb][:, 0:n], lhsT=w13[b][:, 1, kc, j * 128:(j + 1) * 128],
                                                                             rhs=ubuf[:, kc, o:o + n], start=(kc == 0), stop=(kc == 7)),
                             reads=[("ffw", b), ("u", ti)], writes=[("ps", pb)])
                    E.op("scalar", lambda e, j=j, pa=pa: e.activation(out=sl[j % 2][:, 0:n], in_=self.ps[pa][:, 0:n], func=AF.Silu),
                         reads=[("ps", pa)], writes=[("sl", j % 2)])
                    E.op("vector", lambda e, j=j, pb=pb: e.tensor_tensor(out=gb[gk][:, j, 0:n], in0=sl[j % 2][:, 0:n], in1=self.ps[pb][:, 0:n], op=ALU.mult),
                         reads=[("sl", j % 2), ("ps", pb)], writes=[("g", gk, j)])

            def stage_b(it, k):
                gi_, ti = it
                b = gi_ % 2
                f0, gsz = groups[gi_]
                c0, n, cond = st[ti]
                o = offs[ti]
                gk = k % 2
                for dc in range(8):
                    py = 4 + dc % 4
                    for j in range(gsz):
                        E.op("tensor", lambda e, dc=dc, j=j, py=py: e.matmul(self.ps[py][:, 0:n], lhsT=w2[b][:, j, dc * 128:(dc + 1) * 128],
                                                                             rhs=gb[gk][:, j, 0:n], start=(j == 0), stop=(j == gsz - 1)),
                             reads=[("ffw", b), ("g", gk, j)], writes=[("ps", py)])
                    gate = self.AD[:, layer, nrm, 2, dc, cond:cond + 1]
                    E.op("vector", lambda e, dc=dc, py=py, gate=gate: e.scalar_tensor_tensor(out=hbuf[:, dc, o:o + n], in0=self.ps[py][:, 0:n], scalar=gate,
                                                                                             in1=hbuf[:, dc, o:o + n], op0=ALU.mult, op1=ALU.add),
                         reads=[("ps", py), "AD"], writes=[("hd", ti, dc)])

            load_w(0)
            prev = None
            for k, it in enumerate(items):
                stage_a(it, k)
                if it[0] == 0 and it[1] + 1 < len(st):
                    do_adaln(it[1] + 1)
                if prev is not None:
                    stage_b(prev[0], prev[1])
                if it[1] == 0 and it[0] + 1 < len(groups):
                    load_w(it[0] + 1)
                prev = (it, k)
            stage_b(prev[0], prev[1])
            for ti, (c0, n, cond) in enumerate(st):
                o = offs[ti]
                E.dma("sync", [lambda e, o=o, c0=c0, n=n: e.dma_start(out=dstv[:, :, c0:c0 + n], in_=hbuf[:, :, o:o + n])],
                      ("hst", ti % 2), reads=[("hd", ti, dc) for dc in range(8)] + [("h", ti)], writes=[("Hd", ti)])

    def final_norm(self):
        E = self.E
        E.phase()
        hv = self.H.rearrange("(c p) t -> p c t", p=128)
        ov = self.outT.rearrange("(c p) t -> p c t", p=128)
        hb = [E.tmp("fh%d" % i, [128, 8, 512], F32) for i in range(2)]
        ob = [E.tmp("fo%d" % i, [128, 8, 512], F32) for i in range(2)]
        sq = E.tmp("sq", [128, 8, 512], BF16)
        rstd = E.tmp("rstd", [128, 512], F32)
        tmpn = [E.tmp("tmpn%d" % i, [128, 512], F32) for i in range(2)]
        zero = E.tmp("zero", [128, 1], F32)
        E.op("vector", lambda e: e.memset(zero[:], 0.0), writes=["zero"])
        for ti, (c0, n, cond) in enumerate(LAT_TILES):
            b = ti % 2
            E.dma("sync", [lambda e, b=b, c0=c0, n=n: e.dma_start(out=hb[b][:], in_=hv[:, :, c0:c0 + n])], ("fh", b), writes=[("fh", b)])
            self.adaln(hb[b], n, 0, 0, 0, lambda c, b=b: ob[b][:, c, :], ("fh", b), ("fo", b), sq, rstd, tmpn, 6,
                       A=lambda c: self.finA[:, c:c + 1], shift=lambda c: zero[:, 0:1], idx=ti)
            E.dma("sync", [lambda e, b=b, c0=c0, n=n: e.dma_start(out=ov[:, :, c0 - CTX:c0 - CTX + n], in_=ob[b][:])], ("fo", b),
                  reads=[("fo", b)], writes=[("out", ti)])

    def load_norm_tiles(self, tiles, layer, nrm, out_fn, hb, sq, rstd, tmpn, keyp):
        E = self.E
        hv = self.H.rearrange("(c p) t -> p c t", p=128)
        for ti, (c0, n, cond) in enumerate(tiles):
            b = ti % 2
            E.dma("sync", [lambda e, b=b, c0=c0, n=n: e.dma_start(out=hb[b][:, :, 0:n], in_=hv[:, :, c0:c0 + n])], (keyp, b), writes=[(keyp, b)])
            self.adaln(hb[b], n, layer, nrm, cond, (lambda c, ti=ti: out_fn(ti, c)), (keyp, b), (keyp + "o", ti), sq, rstd, tmpn, 6, idx=ti)

    def retention(self, layer, slot, ctx_out):
        E = self.E
        _ = (self.rot, self.retc, self.ret_w_in, self.ret_w_out)
        tiles = [CTX_TILE] + LAT_TILES
        E.phase()
        u_all = E.tmp("u_all", [128, 8, NT], BF16)
        cos = E.tmp("cos", [128, NT], F32)
        sin = E.tmp("sin", [128, NT], F32)
        hb = [E.tmp("hb%d" % i, [128, 8, 512], F32) for i in range(2)]
        sq = E.tmp("sq", [128, 8, 512], BF16)
        rstd = E.tmp("rstd", [128, 512], F32)
        tmpn = [E.tmp("tmpn%d" % i, [128, 512], F32) for i in range(2)]
        wg = [E.tmp("wg%d" % i, [128, 8, 512], BF16) for i in range(2)]
        rt = [E.tmp("rt%d" % i, [128, 512], F32) for i in range(4)]
        ob = [E.tmp("ob%d" % i, [128, 512], BF16) for i in range(4)]
        E.dma("sync", [lambda e: e.dma_start(out=cos[:], in_=self.rot[0]), lambda e: e.dma_start(out=sin[:], in_=self.rot[1])], "rot", writes=["rot"])
        self.load_norm_tiles(tiles, layer, 1, lambda ti, c: u_all[:, c, tiles[ti][0]:tiles[ti][0] + tiles[ti][1]], hb, sq, rstd, tmpn, "rh")
        ukeys = [("rho", ti) for ti in range(len(tiles))]
        wv = self.ret_w_in[slot].rearrange("(kc p) f -> p kc f", p=128)
        cnt = {"ob": 0, "t": 0}

        def loadw(g):
            b = g % 2
            E.dma("gpsimd", [lambda e, b=b, g=g: e.dma_start(out=wg[b][:], in_=wv[:, :, g * 512:(g + 1) * 512])], ("rwg", b), writes=[("rwg", b)])

        loadw(0)
        for g in range(12):
            b = g % 2
            if g + 1 < 12:
                loadw(g + 1)
            if g < 4:
                dst = (self.QT if g < 2 else self.KT).rearrange("(c p) t -> p c t", p=128)
                for hh in range(2):
                    for ti, (c0, n, cond) in enumerate(tiles):
                        pa, pb = (0, 1) if cnt["t"] % 2 == 0 else (2, 3)
                        cnt["t"] += 1
                        for kc in range(8):
                            E.op("tensor", lambda e, kc=kc, hh=hh, c0=c0, n=n, pa=pa, b=b: e.matmul(
                                self.ps[pa][:, 0:n], lhsT=wg[b][:, kc, hh * 256:hh * 256 + 128], rhs=u_all[:, kc, c0:c0 + n], start=(kc == 0), stop=(kc == 7)),
                                reads=[("rwg", b), ukeys[ti]], writes=[("ps", pa)])
                        for kc in range(8):
                            E.op("tensor", lambda e, kc=kc, hh=hh, c0=c0, n=n, pb=pb, b=b: e.matmul(
                                self.ps[pb][:, 0:n], lhsT=wg[b][:, kc, hh * 256 + 128:hh * 256 + 256], rhs=u_all[:, kc, c0:c0 + n], start=(kc == 0), stop=(kc == 7)),
                                reads=[("rwg", b), ukeys[ti]], writes=[("ps", pb)])
                        o1 = cnt["ob"] % 4
                        o2 = (cnt["ob"] + 1) % 4
                        cnt["ob"] += 2
                        E.op("vector", lambda e, c0=c0, n=n, pa=pa: e.tensor_tensor(out=rt[0][:, 0:n], in0=self.ps[pa][:, 0:n], in1=cos[:, c0:c0 + n], op=ALU.mult),
                             reads=[("ps", pa), "rot"], writes=[("rt", 0)])
                        E.op("vector", lambda e, c0=c0, n=n, pb=pb: e.tensor_tensor(out=rt[1][:, 0:n], in0=self.ps[pb][:, 0:n], in1=sin[:, c0:c0 + n], op=ALU.mult),
                             reads=[("ps", pb), "rot"], writes=[("rt", 1)])
                        E.op("gpsimd", lambda e, n=n, o1=o1: e.tensor_tensor(out=ob[o1][:, 0:n], in0=rt[0][:, 0:n], in1=rt[1][:, 0:n], op=ALU.subtract),
                             reads=[("rt", 0), ("rt", 1)], writes=[("ob", o1)])
                        E.op("vector", lambda e, c0=c0, n=n, pb=pb: e.tensor_tensor(out=rt[2][:, 0:n], in0=self.ps[pb][:, 0:n], in1=cos[:, c0:c0 + n], op=ALU.mult),
                             reads=[("ps", pb), "rot"], writes=[("rt", 2)])
                        E.op("vector", lambda e, c0=c0, n=n, pa=pa: e.tensor_tensor(out=rt[3][:, 0:n], in0=self.ps[pa][:, 0:n], in1=sin[:, c0:c0 + n], op=ALU.mult),
                             reads=[("ps", pa), "rot"], writes=[("rt", 3)])
                        E.op("gpsimd", lambda e, n=n, o2=o2: e.tensor_tensor(out=ob[o2][:, 0:n], in0=rt[2][:, 0:n], in1=rt[3][:, 0:n], op=ALU.add),
                             reads=[("rt", 2), ("rt", 3)], writes=[("ob", o2)])
                        fc = (g % 2) * 4 + hh * 2
                        E.dma("sync", [lambda e, o1=o1, fc=fc, c0=c0, n=n, dst=dst: e.dma_start(out=dst[:, fc, c0:c0 + n], in_=ob[o1][:, 0:n])],
                              ("obs", o1), reads=[("ob", o1)], writes=[("qk", g, hh, ti, 0)])
                        E.dma("sync", [lambda e, o2=o2, fc=fc, c0=c0, n=n, dst=dst: e.dma_start(out=dst[:, fc + 1, c0:c0 + n], in_=ob[o2][:, 0:n])],
                              ("obs", o2), reads=[("ob", o2)], writes=[("qk", g, hh, ti, 1)])
            else:
                dstT = self.VT if g < 8 else self.GT
                col0 = ((g - 4) % 4) * 512
                for tc in range(NT // 128):
                    tok0 = tc * 128
                    pa = cnt["t"] % 4
                    cnt["t"] += 1
                    ti = 0 if tok0 < CTX else 1 + (tok0 - CTX) // 512
                    for kc in range(8):
                        E.op("tensor", lambda e, kc=kc, tok0=tok0, pa=pa, b=b: e.matmul(
                            self.ps[pa][:, 0:512], lhsT=u_all[:, kc, tok0:tok0 + 128], rhs=wg[b][:, kc, :], start=(kc == 0), stop=(kc == 7)),
                            reads=[("rwg", b), ukeys[ti]], writes=[("ps", pa)])
                    o1 = cnt["ob"] % 4
                    cnt["ob"] += 1
                    fn = AF.Copy if g < 8 else AF.Silu
                    E.op("scalar", lambda e, pa=pa, o1=o1, fn=fn: e.activation(out=ob[o1][:], in_=self.ps[pa][:, 0:512], func=fn),
                         reads=[("ps", pa)], writes=[("ob", o1)])
                    E.dma("sync", [lambda e, o1=o1, tok0=tok0, col0=col0, dstT=dstT: e.dma_start(out=dstT[tok0:tok0 + 128, col0:col0 + 512], in_=ob[o1][:])],
                          ("obs", o1), reads=[("ob", o1)], writes=[("vg", g, tc)])

        E.phase()
        RC = E.tmp("retc", [128, RETC_N], F32)
        E.dma("sync", [lambda e: e.dma_start(out=RC[:], in_=self.retc)], "retc", writes=["retc"])
        rel = [RC[:, 0:128], RC[:, 128:256]]
        mask = [RC[:, 256:384], RC[:, 384:512]]
        qexp = [RC[:, 512:640], RC[:, 640:768]]
        kexp = [RC[:, 768:769], RC[:, 769:770]]
        lg = E.tmp("lg", [128, 8], F32)
        onec = E.tmp("onec", [128, 1], F32)
        dm = E.tmp("dm", [128, 8, 128], F32)
        qd = E.tmp("qd", [128, 8, 128], F32)
        kd = E.tmp("kd", [128, 8], F32)
        cd = E.tmp("cd", [128, 8], F32)
        E.op("vector", lambda e: e.memset(onec[:], 1.0), writes=["onec"])
        do, _ = VEC_LAYOUT["ret_decay"]
        dsl = self.V[:, do + slot * 8: do + slot * 8 + 8]
        E.op("scalar", lambda e: e.activation(out=lg[:], in_=dsl, func=AF.Exp, scale=-1.0), reads=["retc"], writes=["lg"])
        E.op("scalar", lambda e: e.activation(out=lg[:], in_=lg[:], func=AF.Ln, bias=onec[:, 0:1], scale=1.0), reads=["lg", "onec"], writes=["lg"])
        E.op("vector", lambda e: e.tensor_scalar(out=lg[:], in0=lg[:], scalar1=-1.0, scalar2=None, op0=ALU.mult), reads=["lg"], writes=["lg"])
        for d_ in range(2):
            for h in range(4):
                i = d_ * 4 + h
                E.op("scalar", lambda e, i=i, d_=d_: e.activation(out=dm[:, i, :], in_=rel[d_], func=AF.Exp, scale=lg[:, i:i + 1]), reads=["lg", "retc"], writes=["dm"])
                E.op("vector", lambda e, i=i, d_=d_: e.tensor_tensor(out=dm[:, i, :], in0=dm[:, i, :], in1=mask[d_], op=ALU.mult), reads=["dm", "retc"], writes=["dm"])
                E.op("scalar", lambda e, i=i, d_=d_: e.activation(out=qd[:, i, :], in_=qexp[d_], func=AF.Exp, scale=lg[:, i:i + 1]), reads=["lg", "retc"], writes=["qd"])
                E.op("scalar", lambda e, i=i, d_=d_: e.activation(out=kd[:, i:i + 1], in_=kexp[d_], func=AF.Exp, scale=lg[:, i:i + 1]), reads=["lg", "retc"], writes=["kd"])
        E.op("vector", lambda e: e.tensor_scalar(out=kd[:], in0=kd[:], scalar1=1.0 / 16.0, scalar2=None, op0=ALU.mult), reads=["kd"], writes=["kd"])
        E.op("scalar", lambda e: e.activation(out=cd[:], in_=lg[:], func=AF.Exp, scale=128.0), reads=["lg"], writes=["cd"])

        S = E.tmp("S", [128, 4, 2, 512], F32)
        Sb = E.tmp("Sb", [128, 4, 2, 512], BF16)
        qt = [E.tmp("qt%d" % i, [128, 8, 128], BF16) for i in range(3)]
        kt = [E.tmp("kt%d" % i, [128, 8, 128], BF16) for i in range(3)]
        vt = [E.tmp("vt%d" % i, [128, 2048], BF16) for i in range(3)]
        gt = [E.tmp("gt%d" % i, [128, 2048], BF16) for i in range(2)]
        ofs = [E.tmp("of%d" % i, [128, 2048], F32) for i in range(2)]
        sT = [E.tmp("sT%d" % i, [128, 128], BF16) for i in range(2)]
        kp = [E.tmp("kp%d" % i, [128, 256], BF16) for i in range(2)]
        qp = [E.tmp("qp%d" % i, [128, 2, 128], BF16) for i in range(2)]
        ot = E.tmp("ot", [128, 2048], F32)
        ysq = E.tmp("ysq", [128, 512], F32)
        st4 = E.tmp("st4", [128, 8, 4], F32)
        yb = E.tmp("yb", [128, 2048], BF16)
        yT = E.tmp("yT", [128, 16, 128], BF16)
        wo = E.tmp("wo", [128, 16, D], BF16)
        hch = [E.tmp("hch%d" % i, [128, 8, 128], F32) for i in range(2)]
        epsc = E.tmp("epsc", [128, 1], F32)
        E.op("vector", lambda e: e.memset(epsc[:], EPS), writes=["epsc"])
        zeroc = E.tmp("zeroc", [128, 1], F32)
        E.op("vector", lambda e: e.memset(zeroc[:], 0.0), writes=["zeroc"])
        E.dma("gpsimd", [lambda e: e.dma_start(out=wo[:], in_=self.ret_w_out[slot].rearrange("(fc p) d -> p fc d", p=128))], "wo", writes=["wo"])
        qv = self.QT.rearrange("(c p) t -> p c t", p=128)
        kv = self.KT.rearrange("(c p) t -> p c t", p=128)
        hv = self.H.rearrange("(c p) t -> p c t", p=128)
        nchunk = NT // 128
        for d_ in range(2):
            order = list(range(nchunk)) if d_ == 0 else [1, 0] + list(range(nchunk - 1, 1, -1))
            for ci, tc in enumerate(order):
                self._ret_chunk(layer, d_, ci, tc, ci == 0, ci == len(order) - 1, (tc >= 2) or ctx_out,
                                dict(qt=qt, kt=kt, vt=vt, gt=gt, ofs=ofs, sT=sT, kp=kp, qp=qp, ot=ot, ysq=ysq, st4=st4, yb=yb, yT=yT, wo=wo,
                                     hch=hch, S=S, Sb=Sb, zeroc=zeroc, dm=dm, qd=qd, kd=kd, cd=cd, qv=qv, kv=kv, hv=hv, epsc=epsc))

    def _ret_chunk(self, layer, d_, ci, tc, first, lastc, need_out, B):
        E = self.E
        tok0 = tc * 128
        cond = 1 if tc < 2 else 0
        gi = d_ * 64 + ci
        b3 = gi % 3
        b2 = gi % 2
        qt, kt, vt = B["qt"][b3], B["kt"][b3], B["vt"][b3]
        E.dma("sync", [lambda e: e.dma_start(out=qt[:], in_=B["qv"][:, :, tok0:tok0 + 128])], ("qt", b3), writes=[("qt", b3)])
        E.dma("sync", [lambda e: e.dma_start(out=kt[:], in_=B["kv"][:, :, tok0:tok0 + 128])], ("kt", b3), writes=[("kt", b3)])
        E.dma("sync", [lambda e: e.dma_start(out=vt[:], in_=self.VT[tok0:tok0 + 128, :])], ("vt", b3), writes=[("vt", b3)])
        readout = need_out and d_ == 1
        if readout:
            gt, ofl, hch = B["gt"][b2], B["ofs"][b2], B["hch"][b2]
            E.dma("sync", [lambda e: e.dma_start(out=gt[:], in_=self.GT[tok0:tok0 + 128, :])], ("gt", b2), writes=[("gt", b2)])
            E.dma("sync", [lambda e: e.dma_start(out=ofl[:], in_=self.OF[tok0:tok0 + 128, :])], ("ofl", b2), writes=[("of", b2)])
            E.dma("sync", [lambda e: e.dma_start(out=hch[:], in_=B["hv"][:, :, tok0:tok0 + 128])], ("hch", b2), writes=[("hch", b2)])
        elif need_out:
            ofl = B["ofs"][b2]
        S, Sb, ot = B["S"], B["Sb"], B["ot"]

        def do_head(h):
            i = d_ * 4 + h
            hb_ = h % 2
            bs, bo = hb_, 2 + hb_
            sT, kp, qp = B["sT"][hb_], B["kp"][hb_], B["qp"][hb_]
            vh = vt[:, h * 512:(h + 1) * 512]
            if need_out:
                for c in range(2):
                    E.op("tensor", lambda e, c=c: e.matmul(self.ps[bs][:, 0:128], lhsT=kt[:, 2 * h + c, :], rhs=qt[:, 2 * h + c, :], start=(c == 0), stop=(c == 1)),
                         reads=[("qt", b3), ("kt", b3)], writes=[("pss", bs)])
                E.op("vector", lambda e: e.tensor_tensor(out=sT[:], in0=self.ps[bs][:, 0:128], in1=B["dm"][:, i, :], op=ALU.mult),
                     reads=[("pss", bs), "dm"], writes=[("sT", hb_)])
            if not lastc:
                for c in range(2):
                    E.op("tensor", lambda e, c=c: e.matmul(self.ps[bs][:, 128 + c * 128:256 + c * 128], lhsT=kt[:, 2 * h + c, :], rhs=self.ident[:], start=True, stop=True),
                         reads=[("kt", b3)], writes=[("pst", bs)])
                E.op("vector", lambda e: e.tensor_scalar(out=kp[:], in0=self.ps[bs][:, 128:384], scalar1=B["kd"][:, i:i + 1], scalar2=None, op0=ALU.mult),
                     reads=[("pst", bs), "kd"], writes=[("kp", hb_)])
            if need_out:
                if not first:
                    for c in range(2):
                        E.op("gpsimd", lambda e, c=c: e.tensor_tensor(out=qp[:, c, :], in0=qt[:, 2 * h + c, :], in1=B["qd"][:, i, :], op=ALU.mult),
                             reads=[("qt", b3), "qd"], writes=[("qp", hb_)])
                E.op("tensor", lambda e: e.matmul(self.ps[bo][:, 0:512], lhsT=sT[:], rhs=vh, start=True, stop=first),
                     reads=[("sT", hb_), ("vt", b3)], writes=[("ps", bo)])
                if not first:
                    for c in range(2):
                        E.op("tensor", lambda e, c=c: e.matmul(self.ps[bo][:, 0:512], lhsT=qp[:, c, :], rhs=Sb[:, h, c, :], start=False, stop=(c == 1)),
                             reads=[("qp", hb_), ("Sb", h, c)], writes=[("ps", bo)])
                if d_ == 0:
                    E.op("scalar", lambda e: e.activation(out=ofl[:, h * 512:(h + 1) * 512], in_=self.ps[bo][:, 0:512], func=AF.Copy),
                         reads=[("ps", bo)], writes=[("of", b2)])
                else:
                    E.op("vector", lambda e: e.tensor_tensor(out=ot[:, h * 512:(h + 1) * 512], in0=self.ps[bo][:, 0:512], in1=ofl[:, h * 512:(h + 1) * 512], op=ALU.add),
                         reads=[("ps", bo), ("of", b2)], writes=[("ot", h)])
            if not lastc:
                for c in range(2):
                    E.op("tensor", lambda e, c=c: e.matmul(self.ps[4 + c][:, 0:512], lhsT=kp[:, c * 128:(c + 1) * 128], rhs=vh, start=True, stop=True),
                         reads=[("kp", hb_), ("vt", b3)], writes=[("ps", 4 + c)])
                    if first:
                        E.op("vector", lambda e, c=c: e.tensor_copy(out=S[:, h, c, :], in_=self.ps[4 + c][:, 0:512]), reads=[("ps", 4 + c)], writes=[("S", h, c)])
                    else:
                        E.op("vector", lambda e, c=c: e.scalar_tensor_tensor(out=S[:, h, c, :], in0=S[:, h, c, :], scalar=B["cd"][:, i:i + 1], in1=self.ps[4 + c][:, 0:512],
                                                                             op0=ALU.mult, op1=ALU.add), reads=[("ps", 4 + c), ("S", h, c), "cd"], writes=[("S", h, c)])
                    E.op("scalar", lambda e, c=c: e.activation(out=Sb[:, h, c, :], in_=S[:, h, c, :], func=AF.Copy), reads=[("S", h, c)], writes=[("Sb", h, c)])
        for h_ in range(4):
            do_head(h_)
        if need_out and d_ == 0:
            E.dma("sync", [lambda e: e.dma_start(out=self.OF[tok0:tok0 + 128, :], in_=ofl[:])], ("ofs", b2), reads=[("of", b2)], writes=[("OFd", tc)])
        if not readout:
            return
        st4, ysq, yb, yT, wo = B["st4"], B["ysq"], B["yb"], B["yT"], B["wo"]
        for h in range(4):
            oh = ot[:, h * 512:(h + 1) * 512]
            E.op("vector", lambda e, h=h, oh=oh: e.tensor_reduce(out=st4[:, 0, h:h + 1], in_=oh, axis=AX.X, op=ALU.add), reads=[("ot", h)], writes=[("st", 0, h)])
            E.op("scalar", lambda e, oh=oh: e.activation(out=ysq[:], in_=oh, func=AF.Square), reads=[("ot", h)], writes=["ysq"])
            E.op("vector", lambda e, h=h: e.tensor_reduce(out=st4[:, 1, h:h + 1], in_=ysq[:], axis=AX.X, op=ALU.add), reads=["ysq"], writes=[("st", 1, h)])
        allst = [("st", a, h) for a in range(2) for h in range(4)]
        E.op("vector", lambda e: e.tensor_scalar(out=st4[:, 0:2, :], in0=st4[:, 0:2, :], scalar1=1.0 / 512.0, scalar2=None, op0=ALU.mult), reads=allst, writes=["stA"])
        E.op("vector", lambda e: e.tensor_tensor(out=st4[:, 2, :], in0=st4[:, 0, :], in1=st4[:, 0, :], op=ALU.mult), reads=["stA"], writes=["stB"])
        E.op("vector", lambda e: e.tensor_tensor(out=st4[:, 3, :], in0=st4[:, 1, :], in1=st4[:, 2, :], op=ALU.subtract), reads=["stA", "stB"], writes=["stC"])
        E.op("scalar", lambda e: e.activation(out=st4[:, 4, :], in_=st4[:, 3, :], func=AF.Ln, bias=B["epsc"][:, 0:1], scale=1.0), reads=["stC", "epsc"], writes=["stD"])
        E.op("scalar", lambda e: e.activation(out=st4[:, 5, :], in_=st4[:, 4, :], func=AF.Exp, scale=-0.5), reads=["stD"], writes=["stE"])
        E.op("vector", lambda e: e.scalar_tensor_tensor(out=st4[:, 6, :], in0=st4[:, 0, :], scalar=-1.0, in1=st4[:, 5, :], op0=ALU.mult, op1=ALU.mult),
             reads=["stA", "stE"], writes=["stF"])
        for h in range(4):
            oh = ot[:, h * 512:(h + 1) * 512]
            E.op("scalar", lambda e, h=h, oh=oh: e.activation(out=oh, in_=oh, func=AF.Identity, bias=st4[:, 6, h:h + 1], scale=st4[:, 5, h:h + 1]),
                 reads=[("ot", h), "stE", "stF", "ysq"], writes=[("ot", h)])
            E.op("vector", lambda e, h=h, oh=oh: e.tensor_tensor(out=yb[:, h * 512:(h + 1) * 512], in0=oh, in1=gt[:, h * 512:(h + 1) * 512], op=ALU.mult),
                 reads=[("ot", h), ("gt", b2)], writes=[("yb", h)])
        for fq in range(4):
            for i2 in range(4):
                fc = fq * 4 + i2
                E.op("tensor", lambda e, fc=fc, i2=i2: e.matmul(self.ps[6][:, i2 * 128:(i2 + 1) * 128], lhsT=yb[:, fc * 128:(fc + 1) * 128], rhs=self.ident[:], start=True, stop=True),
                     reads=[("yb", fq)], writes=[("ps", 6)])
            E.op("scalar", lambda e, fq=fq: e.activation(out=yT[:, fq * 4:(fq + 1) * 4, :].rearrange("p a b -> p (a b)"), in_=self.ps[6][:, 0:512], func=AF.Copy),
                 reads=[("ps", 6)], writes=[("yT", fq)])
        for dc in range(8):
            for fc in range(16):
                E.op("tensor", lambda e, dc=dc, fc=fc: e.matmul(self.ps[7][:, 0:128], lhsT=wo[:, fc, dc * 128:(dc + 1) * 128], rhs=yT[:, fc, :], start=(fc == 0), stop=(fc == 15)),
                     reads=["wo"] + [("yT", q) for q in range(4)], writes=[("ps", 7)])
            gate = self.AD[:, layer, 1, 2, dc, cond:cond + 1]
            E.op("vector", lambda e, dc=dc, gate=gate: e.scalar_tensor_tensor(out=hch[:, dc, :], in0=self.ps[7][:, 0:128], scalar=gate, in1=hch[:, dc, :], op0=ALU.mult, op1=ALU.add),
                 reads=[("ps", 7), ("hch", b2), "AD"], writes=[("hcho", b2, dc)])
        E.dma("sync", [lambda e: e.dma_start(out=B["hv"][:, :, tok0:tok0 + 128], in_=hch[:])], ("hchs", b2),
              reads=[("hcho", b2, dc) for dc in range(8)] + [("hch", b2)], writes=[("Hc", tc)])

    def pool(self, layer, slot, tiles):
        ct = [t for t in tiles if t[2] == 1]
        lt = [t for t in tiles if t[2] == 0]
        if ct:
            self._pool_tiles(layer, ct, "c", 1, 256)
        self._pool_tiles(layer, lt, "l", 8, 64)

    def _pool_tiles(self, layer, tiles, nm0, rows0, w0):
        E = self.E
        _ = (self.poolrc, self.pool_w)
        E.phase()
        hv = self.H.rearrange("(c p) t -> p c t", p=128)
        hb = [E.tmp("hb%d" % i, [128, 8, 512], F32) for i in range(2)]
        sq = E.tmp("sq", [128, 8, 512], BF16)
        rstd = E.tmp("rstd", [128, 512], F32)
        tmpn = [E.tmp("tmpn%d" % i, [128, 512], F32) for i in range(2)]
        rc = E.tmp("rc", [128, 4, 768], F32)
        pw = E.tmp("pw", [128, 4, 2, 256], BF16)
        E.dma("sync", [lambda e: e.dma_start(out=rc[:], in_=self.poolrc)], "rc", writes=["rc"])
        E.dma("gpsimd", [lambda e: e.dma_start(out=pw[:], in_=self.pool_w.rearrange("g (cc p) e -> p g cc e", p=128))], "pw", writes=["pw"])
        bufs = {}
        for nm, rows, w in ((nm0, rows0, w0),):
            wp = w + 16
            arr = [E.tmp("up" + nm, [128, 8 * rows, wp], F32)] + [E.tmp("A%d%s" % (k, nm), [128, 8 * rows, wp], F32) for k in range(4)]
            for a_i, a in enumerate(arr):
                E.op("gpsimd", lambda e, a=a: e.memset(a[:], 0.0), writes=[("pad", nm, a_i)])
            bufs[nm] = (arr, rows, w, wp)
        dl = E.tmp("dl", [128, 8, 512], BF16)
        tm = E.tmp("tm", [128, 512], F32)
        yv = E.tmp("yv", [128, 512], F32)
        pbo, _ = VEC_LAYOUT["pool_b"]
        pso, _ = VEC_LAYOUT["pool_scale"]
        for ti, (c0, n, cond) in enumerate(tiles):
            nm = "c" if cond == 1 else "l"
            arr, rows, w, wp = bufs[nm]
            up = arr[0]
            b = ti % 2
            E.dma("sync", [lambda e, b=b, c0=c0, n=n: e.dma_start(out=hb[b][:, :, 0:n], in_=hv[:, :, c0:c0 + n])], ("ph", b), writes=[("ph", b)])
            self.adaln(hb[b], n, layer, 1, cond,
                       (lambda c, up=up, rows=rows, w=w: up[:, c * rows:(c + 1) * rows, 8:8 + w]),
                       ("ph", b), ("up", nm), sq, rstd, tmpn, 6, idx=ti, out3=(rows, w))
            A2, A4, A8, A16 = arr[1], arr[2], arr[3], arr[4]
            R = 8 * rows
            E.op("vector", lambda e, up=up, A2=A2, wp=wp: e.tensor_tensor(out=A2[:, :, 1:wp], in0=up[:, :, 0:wp - 1], in1=up[:, :, 1:wp], op=ALU.add),
                 reads=[("up", nm), ("pad", nm, 0)], writes=[("A", nm, 2)])
            E.op("gpsimd", lambda e, A2=A2, A4=A4, wp=wp, rows=rows: e.tensor_tensor(out=A4[:, 2 * rows:, 1:wp - 1], in0=A2[:, 2 * rows:, 0:wp - 2], in1=A2[:, 2 * rows:, 2:wp], op=ALU.add),
                 reads=[("A", nm, 2)], writes=[("A", nm, 4)])
            E.op("vector", lambda e, A4=A4, A8=A8, wp=wp, rows=rows: e.tensor_tensor(out=A8[:, 4 * rows:, 2:wp - 2], in0=A4[:, 4 * rows:, 0:wp - 4], in1=A4[:, 4 * rows:, 4:wp], op=ALU.add),
                 reads=[("A", nm, 4)], writes=[("A", nm, 8)])
            E.op("gpsimd", lambda e, A8=A8, A16=A16, wp=wp, rows=rows: e.tensor_tensor(out=A16[:, 6 * rows:, 4:wp - 4], in0=A8[:, 6 * rows:, 0:wp - 8], in1=A8[:, 6 * rows:, 8:wp], op=ALU.add),
                 reads=[("A", nm, 8)], writes=[("A", nm, 16)])
            rco = 0 if nm == "l" else 512
            for c in range(8):
                gi = c // 2
                Aw = arr[1 + gi]
                wk = (2, 4, 8, 16)[gi]
                E.op("vector", lambda e, c=c, Aw=Aw, gi=gi, rows=rows, w=w, rco=rco, n=n: e.tensor_tensor(
                    out=tm[:, 0:n].rearrange("p (r w) -> p r w", w=w), in0=Aw[:, c * rows:(c + 1) * rows, 8:8 + w],
                    in1=rc[:, gi, rco:rco + n].rearrange("p (r w) -> p r w", w=w), op=ALU.mult),
                    reads=[("A", nm, wk), "rc"], writes=["tm"])
                E.op("vector", lambda e, c=c, up=up, rows=rows, w=w, n=n: e.tensor_tensor(
                    out=dl[:, c, 0:n].rearrange("p (r w) -> p r w", w=w), in0=tm[:, 0:n].rearrange("p (r w) -> p r w", w=w),
                    in1=up[:, c * rows:(c + 1) * rows, 8:8 + w], op=ALU.subtract),
                    reads=["tm", ("up", nm)], writes=[("dl", c)])
            for gi in range(4):
                for ec in range(2):
                    pbk = ec
                    for cc in range(2):
                        E.op("tensor", lambda e, gi=gi, ec=ec, cc=cc, n=n, pbk=pbk: e.matmul(self.ps[pbk][:, 0:n], lhsT=pw[:, gi, cc, ec * 128:(ec + 1) * 128],
                                                                                             rhs=dl[:, 2 * gi + cc, 0:n], start=(cc == 0), stop=(cc == 1)),
                             reads=["pw", ("dl", 2 * gi + cc)], writes=[("ps", pbk)])
                    dc = 2 * gi + ec
                    E.op("vector", lambda e, dc=dc, n=n, pbk=pbk: e.tensor_scalar(out=yv[:, 0:n], in0=self.ps[pbk][:, 0:n], scalar1=self.V[:, pbo + dc:pbo + dc + 1],
                                                                                  scalar2=self.V[:, pso + dc:pso + dc + 1], op0=ALU.add, op1=ALU.mult),
                         reads=[("ps", pbk), "vecs"], writes=["yv"])
                    gate = self.AD[:, layer, 1, 2, dc, cond:cond + 1]
                    E.op("vector", lambda e, dc=dc, n=n, b=b, gate=gate: e.scalar_tensor_tensor(out=hb[b][:, dc, 0:n], in0=yv[:, 0:n], scalar=gate, in1=hb[b][:, dc, 0:n],
                                                                                                op0=ALU.mult, op1=ALU.add),
                         reads=["yv", ("ph", b), ("up", nm), "AD"], writes=[("pho", b, dc)])
            E.dma("sync", [lambda e, b=b, c0=c0, n=n: e.dma_start(out=hv[:, :, c0:c0 + n], in_=hb[b][:, :, 0:n])], ("phs", b),
                  reads=[("pho", b, dc) for dc in range(8)] + [("ph", b)], writes=[("Hp", ti)])

    def hyena(self, layer, slot, ctx_out):
        insts = [(SEQ, CTX, 0, "l")] + ([(CTX, 0, 1, "c")] if ctx_out else [])
        for (L, col0, cond, nm) in insts:
            self._hy_filter(L, nm)
            self._hy_proj(layer, L, col0, cond, nm)
            self._hy_conv(L, nm)
            self._hy_out(layer, L, col0, cond, nm)

    def _hy_filter(self, L, nm):
        E = self.E
        _ = (self.hy_w_pos, self.hy_w_mid, self.hy_w_filt, self.hyv64, self.hy_delta, self.hyb_bc)
        D_ = self.get_hyd(nm)
        nsc = L // 128
        E.phase()
        nrm = E.tmp("nrm", [128, 2 * D], F32)
        saved_base = E.sb_base
        E.sb_base = E.sb_top
        featT = E.tmp("featT", [33, L], F32)
        wpos = E.tmp("wpos", [33, 64], F32)
        wmid = E.tmp("wmid", [64, 64], F32)
        wfilt = E.tmp("wfilt", [64, 4096], F32)
        hv64 = E.tmp("hv64", [64, 8], F32)
        tneg = E.tmp("tneg", [128, nsc], F32)
        delta = E.tmp("delta", [128, D], F32)
        hdn = [E.tmp("hdn%d" % i, [64, L], F32) for i in range(2)]
        rr_f = E.tmp("rr_f", [64, L], F32)
        rr_i = E.tmp("rr_i", [64, L], mybir.dt.int32)
        ones32 = E.tmp("ones32", [128, 128], F32)
        win = E.tmp("win", [128, D], F32)
        hw = [E.tmp("hw%d" % i, [128, 2, D], F32) for i in range(2)]
        sqt = E.tmp("sqt", [128, 2, D], F32)
        hsd = [E.tmp("hsd%d" % i, [128, 2, 2 * D], BF16) for i in range(2)]
        E.dma("sync", [lambda e: e.dma_start(out=featT[:], in_=D_["featT"]), lambda e: e.dma_start(out=wpos[:], in_=self.hy_w_pos),
                       lambda e: e.dma_start(out=wmid[:], in_=self.hy_w_mid), lambda e: e.dma_start(out=wfilt[:], in_=self.hy_w_filt),
                       lambda e: e.dma_start(out=hv64[:], in_=self.hyv64), lambda e: e.dma_start(out=tneg[:], in_=D_["tneg"]),
                       lambda e: e.dma_start(out=delta[:], in_=self.hy_delta)], "hyf", writes=["hyf"])
        E.op("vector", lambda e: e.memset(ones32[:], 1.0), writes=["ones32"])
        E.op("vector", lambda e: e.tensor_tensor(out=hv64[:, 3:4], in0=hv64[:, 0:1], in1=hv64[:, 2:3], op=ALU.mult), reads=["hyf"], writes=["hv"])
        E.op("vector", lambda e: e.tensor_tensor(out=hv64[:, 4:5], in0=hv64[:, 1:2], in1=hv64[:, 2:3], op=ALU.mult), reads=["hyf", "hv"], writes=["hv"])
        E.op("vector", lambda e: e.memset(hv64[:, 5:6], -math.pi), reads=["hv"], writes=["hv"])
        srcs = [(wpos, featT, 3), (wmid, hdn[0], 4)]
        for li, (wl, src, bcol) in enumerate(srcs):
            for t0 in range(0, L, 512):
                n = min(512, L - t0)
                E.op("tensor", lambda e, wl=wl, src=src, t0=t0, n=n: e.matmul(self.ps[0][0:64, 0:n], lhsT=wl[:], rhs=src[:, t0:t0 + n], start=True, stop=True),
                     reads=["hyf", ("hdn", li - 1)], writes=[("ps", 0)])
                E.op("vector", lambda e, li=li, t0=t0, n=n, bcol=bcol: e.tensor_scalar(out=hdn[li][:, t0:t0 + n], in0=self.ps[0][0:64, 0:n], scalar1=hv64[:, 2:3],
                                                                                      scalar2=hv64[:, bcol:bcol + 1], op0=ALU.mult, op1=ALU.add),
                     reads=[("ps", 0), "hv"], writes=[("hdnA", li)])
            hl = hdn[li]
            E.op("vector", lambda e, hl=hl: e.tensor_scalar(out=hl[:], in0=hl[:], scalar1=8.0 * math.pi, scalar2=None, op0=ALU.add),
                 reads=[("hdnA", li)], writes=[("hdnB", li)])
            E.op("vector", lambda e, hl=hl: e.tensor_scalar(out=rr_f[:], in0=hl[:], scalar1=1.0 / (2.0 * math.pi), scalar2=None, op0=ALU.mult),
                 reads=[("hdnB", li)], writes=["rr_f"])
            E.op("vector", lambda e: e.tensor_copy(out=rr_i[:], in_=rr_f[:]), reads=["rr_f"], writes=["rr_i"])
            E.op("vector", lambda e: e.tensor_copy(out=rr_f[:], in_=rr_i[:]), reads=["rr_i"], writes=["rr_f"])
            E.op("vector", lambda e, hl=hl: e.scalar_tensor_tensor(out=hl[:], in0=rr_f[:], scalar=-2.0 * math.pi, in1=hl[:], op0=ALU.mult, op1=ALU.add),
                 reads=["rr_f", ("hdnB", li)], writes=[("hdnC", li)])
            E.op("vector", lambda e, hl=hl: e.tensor_scalar(out=rr_f[:], in0=hl[:], scalar1=math.pi, scalar2=2.0 * math.pi, op0=ALU.is_gt, op1=ALU.mult),
                 reads=[("hdnC", li)], writes=["rr_f"])
            E.op("vector", lambda e, hl=hl: e.tensor_tensor(out=hl[:], in0=hl[:], in1=rr_f[:], op=ALU.subtract),
                 reads=["rr_f", ("hdnC", li)], writes=[("hdnD", li)])
            E.op("scalar", lambda e, hl=hl: e.activation(out=hl[:], in_=hl[:], func=AF.Sin),
                 reads=[("hdnD", li)], writes=[("hdn", li)])
        for sc in range(nsc):
            E.op("scalar", lambda e, sc=sc: e.activation(out=win[:], in_=delta[:], func=AF.Exp, scale=tneg[:, sc:sc + 1]), reads=["hyf"], writes=["win"])
            b = sc % 2
            for o in range(2):
                for dr in range(2):
                    for hf in range(2):
                        cb = o * 2048 + dr * 1024 + hf * 512
                        pb = (dr * 2 + hf) % 4
                        E.op("tensor", lambda e, sc=sc, cb=cb, pb=pb: e.matmul(self.ps[pb][:, 0:512], lhsT=hdn[1][:, sc * 128:(sc + 1) * 128], rhs=wfilt[:, cb:cb + 512], start=True, stop=True),
                             reads=[("hdn", 1), "hyf"], writes=[("ps", pb)])
                        E.op("vector", lambda e, o=o, dr=dr, hf=hf, pb=pb: e.tensor_tensor(out=hw[o][:, dr, hf * 512:(hf + 1) * 512], in0=self.ps[pb][:, 0:512],
                                                                                           in1=win[:, hf * 512:(hf + 1) * 512], op=ALU.mult),
                             reads=[("ps", pb), "win"], writes=[("hw", o, dr)])
                if sc == 0:
                    E.op("vector", lambda e, o=o: e.memset(hw[o][0:1, 1, :], 0.0), reads=[("hw", o, 1)], writes=[("hw", o, 1)])
                E.op("scalar", lambda e, o=o: e.activation(out=sqt[:].rearrange("p a d -> p (a d)"), in_=hw[o][:].rearrange("p a d -> p (a d)"), func=AF.Square),
                     reads=[("hw", o, 0), ("hw", o, 1)], writes=["sqt"])
                for hf in range(2):
                    for dr in range(2):
                        E.op("tensor", lambda e, o=o, hf=hf, dr=dr, sc=sc: e.matmul(self.ps[4 + o * 2 + hf][:, 0:512], lhsT=ones32[:], rhs=sqt[:, dr, hf * 512:(hf + 1) * 512],
                                                                                   start=(sc == 0 and dr == 0), stop=(sc == nsc - 1 and dr == 1)),
                             reads=["sqt", "ones32"], writes=[("psn", o, hf)])
                E.op("gpsimd", lambda e, o=o, b=b: e.tensor_tensor(out=hsd[b][:, 0, o * D:(o + 1) * D], in0=hw[o][:, 0, :], in1=hw[o][:, 1, :], op=ALU.add),
                     reads=[("hw", o, 0), ("hw", o, 1)], writes=[("hsd", b, o)])
                E.op("gpsimd", lambda e, o=o, b=b: e.tensor_tensor(out=hsd[b][:, 1, o * D:(o + 1) * D], in0=hw[o][:, 0, :], in1=hw[o][:, 1, :], op=ALU.subtract),
                     reads=[("hw", o, 0), ("hw", o, 1)], writes=[("hsd", b, o)])
            E.dma("sync", [lambda e, sc=sc, b=b: e.dma_start(out=D_["HSD"][sc], in_=hsd[b][:])], ("hsds", b), reads=[("hsd", b, 0), ("hsd", b, 1)], writes=[("HSDd", sc)])
        for o in range(2):
            for hf in range(2):
                sl_ = nrm[:, o * D + hf * 512: o * D + (hf + 1) * 512]
                E.op("scalar", lambda e, o=o, hf=hf, sl_=sl_: e.activation(out=sl_, in_=self.ps[4 + o * 2 + hf][:, 0:512], func=AF.Ln, bias=self.eps1[:, 0:1], scale=1.0),
                     reads=[("psn", o, hf)], writes=[("nrm", o, hf)])
                E.op("scalar", lambda e, sl_=sl_: e.activation(out=sl_, in_=sl_, func=AF.Exp, scale=-0.5), reads=[("nrm", o, hf)], writes=[("nrm", o, hf)])
        E.phase()
        nb2 = L // 128
        hs = E.tmp("hs", [128, nsc, 512], BF16)
        hd = E.tmp("hd", [128, nsc, 512], BF16)
        ftb = [E.tmp("ftb%d" % i, [128, 2, nsc, 128], BF16) for i in range(2)]
        fbb = E.tmp("fbb", [128, 2 * D], F32)
        kro = [E.tmp("kro%d" % i, [128, 2, 512], F32) for i in range(2)]
        E.dma("sync", [lambda e: e.dma_start(out=fbb[:], in_=self.hyb_bc)], "fbb", writes=["fbb"])
        for cb in range(4):
            o, hf = cb // 2, cb % 2
            E.dma("sync", [lambda e, cb=cb, q0=q0, w_=w_, dst_=dst_: e.dma_start(out=dst_[:, q0:q0 + 8, :], in_=D_["HSD"][q0:q0 + 8, :, w_, cb * 512:(cb + 1) * 512].rearrange("c p d -> p c d"))
                           for q0 in range(0, nsc, 8) for (w_, dst_) in ((0, hs), (1, hd))][:None] if nsc >= 8 else
                  [lambda e, cb=cb: e.dma_start(out=hs[:], in_=D_["HSD"][:, :, 0, cb * 512:(cb + 1) * 512].rearrange("c p d -> p c d")),
                   lambda e, cb=cb: e.dma_start(out=hd[:], in_=D_["HSD"][:, :, 1, cb * 512:(cb + 1) * 512].rearrange("c p d -> p c d"))],
                  "hshd", writes=["hshd"])
            nsl = nrm[:, cb * 512:(cb + 1) * 512]
            fsl = fbb[:, cb * 512:(cb + 1) * 512]
            for j in range(nb2):
                b = j % 2
                E.dma("scalar", [lambda e, j=j, b=b: e.dma_start(out=ftb[b][:, 0], in_=D_["FT"][j])], ("ftbA", b), writes=[("ftb", b, 0)])
                E.dma("gpsimd", [lambda e, j=j, b=b: e.dma_start(out=ftb[b][:, 1], in_=D_["FT"][nb2 + j])], ("ftbB", b), writes=[("ftb", b, 1)])
                pr, pi = (0, 1) if b == 0 else (2, 3)
                for sc in range(nsc):
                    E.op("tensor", lambda e, sc=sc, b=b, pr=pr: e.matmul(self.ps[pr][:, 0:512], lhsT=ftb[b][:, 0, sc, :], rhs=hs[:, sc, :], start=(sc == 0), stop=(sc == nsc - 1)),
                         reads=[("ftb", b, 0), "hshd"], writes=[("ps", pr)])
                for sc in range(nsc):
                    E.op("tensor", lambda e, sc=sc, b=b, pi=pi: e.matmul(self.ps[pi][:, 0:512], lhsT=ftb[b][:, 1, sc, :], rhs=hd[:, sc, :], start=(sc == 0), stop=(sc == nsc - 1)),
                         reads=[("ftb", b, 1), "hshd"], writes=[("ps", pi)])
                if j == 0:
                    for sc in range(nsc):
                        E.op("tensor", lambda e, sc=sc, b=b: e.matmul(self.ps[6][0:1, 0:512], lhsT=ftb[b][:, 1, sc, 0:1], rhs=hs[:, sc, :], start=(sc == 0), stop=(sc == nsc - 1)),
                             reads=[("ftb", b, 1), "hshd"], writes=[("ps", 6)])
                E.op("vector", lambda e, b=b, pr=pr, nsl=nsl: e.tensor_tensor(out=kro[b][:, 0, :], in0=self.ps[pr][:, 0:512], in1=nsl, op=ALU.mult),
                     reads=[("ps", pr), ("nrm", o, hf)], writes=[("kro", b, 0)])
                E.op("gpsimd", lambda e, b=b, fsl=fsl: e.tensor_tensor(out=kro[b][:, 0, :], in0=kro[b][:, 0, :], in1=fsl, op=ALU.add),
                     reads=[("kro", b, 0), "fbb"], writes=[("kro", b, 0)])
                E.op("vector", lambda e, b=b, pi=pi, nsl=nsl: e.tensor_tensor(out=kro[b][:, 1, :], in0=self.ps[pi][:, 0:512], in1=nsl, op=ALU.mult),
                     reads=[("ps", pi), ("nrm", o, hf)], writes=[("kro", b, 1)])
                if j == 0:
                    E.op("vector", lambda e, b=b, nsl=nsl: e.tensor_tensor(out=kro[b][0:1, 1, :], in0=self.ps[6][0:1, 0:512], in1=nsl[0:1, :], op=ALU.mult),
                         reads=[("ps", 6), ("kro", b, 1)], writes=[("kro", b, 1)])
                    E.op("vector", lambda e, b=b, fsl=fsl: e.tensor_tensor(out=kro[b][0:1, 1, :], in0=kro[b][0:1, 1, :], in1=fsl[0:1, :], op=ALU.add),
                         reads=[("kro", b, 1), "fbb"], writes=[("kro", b, 1)])
                E.dma("sync", [lambda e, b=b, j=j, o=o, hf=hf: e.dma_start(out=D_["KS"][o, :, j * 128:(j + 1) * 128, hf * 512:(hf + 1) * 512].rearrange("a p d -> p a d"), in_=kro[b][:])],
                      ("kros", b), reads=[("kro", b, 0), ("kro", b, 1)], writes=[("KSd", cb, j)])
        E.barrier()
        E.sb_base = saved_base

    def _hy_proj(self, layer, L, col0, cond, nm):
        E = self.E
        _ = (self.hy_w_in,)
        D_ = self.get_hyd(nm)
        nsc = L // 128
        E.phase()
        u_all = E.tmp("u_all", [128, 8, L], BF16)
        hb = [E.tmp("hb%d" % i, [128, 8, 512], F32) for i in range(2)]
        sq = E.tmp("sq", [128, 8, 512], BF16)
        rstd = E.tmp("rstd", [128, 512], F32)
        tmpn = [E.tmp("tmpn%d" % i, [128, 512], F32) for i in range(2)]
        wg = [E.tmp("wg%d" % i, [128, 8, 512], BF16) for i in range(2)]
        ppad = [E.tmp("ppad%d" % i, [128, L + 2], F32) for i in range(1)] * 2
        acc = E.tmp("acc", [128, L], F32)
        pcb = [E.tmp("pcb%d" % i, [128, L], BF16) for i in range(1)] * 2
        tokm = [E.tmp("tokm%d" % i, [128, nsc, 128], BF16) for i in range(2)]
        tiles = [(col0 + t0, min(512, L - t0), cond) for t0 in range(0, L, 512)]
        self.load_norm_tiles(tiles, layer, 1, lambda ti, c: u_all[:, c, ti * 512:ti * 512 + tiles[ti][1]], hb, sq, rstd, tmpn, "yh")
        ukeys = [("yho", ti) for ti in range(len(tiles))]
        for i in range(2):
            E.op("gpsimd", lambda e, i=i: e.memset(ppad[i][:, 0:1], 0.0), writes=[("ppz", i)])
            E.op("gpsimd", lambda e, i=i: e.memset(ppad[i][:, L + 1:L + 2], 0.0), writes=[("ppz", i)])
        wv = self.hy_w_in.rearrange("(kc p) f -> p kc f", p=128)
        bio, _ = VEC_LAYOUT["hy_b_in"]
        wso, _ = VEC_LAYOUT["hy_w_short"]
        bso, _ = VEC_LAYOUT["hy_b_short"]

        def loadw(g):
            b = g % 2
            E.dma("gpsimd", [lambda e, b=b, g=g: e.dma_start(out=wg[b][:], in_=wv[:, :, g * 512:(g + 1) * 512])], ("ywg", b), writes=[("ywg", b)])

        loadw(0)
        cnt = 0
        for g in range(6):
            b = g % 2
            if g + 1 < 6:
                loadw(g + 1)
            for f4 in range(4):
                fc = g * 4 + f4
                pb_ = 0
                pp = ppad[pb_]
                for ti, (c0, n, _) in enumerate(tiles):
                    pa = cnt % 4
                    cnt += 1
                    for kc in range(8):
                        E.op("tensor", lambda e, kc=kc, f4=f4, ti=ti, n=n, pa=pa, b=b: e.matmul(self.ps[pa][:, 0:n], lhsT=wg[b][:, kc, f4 * 128:(f4 + 1) * 128],
                                                                                               rhs=u_all[:, kc, ti * 512:ti * 512 + n], start=(kc == 0), stop=(kc == 7)),
                             reads=[("ywg", b), ukeys[ti]], writes=[("ps", pa)])
                    E.op("vector", lambda e, ti=ti, n=n, pa=pa, pp=pp, fc=fc: e.tensor_scalar(out=pp[:, 1 + ti * 512:1 + ti * 512 + n], in0=self.ps[pa][:, 0:n],
                                                                                            scalar1=self.V[:, bio + fc:bio + fc + 1], scalar2=None, op0=ALU.add),
                         reads=[("ps", pa), "vecs", ("ppz", pb_)], writes=[("pp", pb_)])
                w0 = self.V[:, wso + fc:wso + fc + 1]
                w1 = self.V[:, wso + 24 + fc:wso + 24 + fc + 1]
                w2 = self.V[:, wso + 48 + fc:wso + 48 + fc + 1]
                bs_ = self.V[:, bso + fc:bso + fc + 1]
                E.op("vector", lambda e, pp=pp, w1=w1, bs_=bs_: e.tensor_scalar(out=acc[:], in0=pp[:, 1:L + 1], scalar1=w1, scalar2=bs_, op0=ALU.mult, op1=ALU.add),
                     reads=[("pp", pb_), "vecs"], writes=["acc"])
                E.op("vector", lambda e, pp=pp, w0=w0: e.scalar_tensor_tensor(out=acc[:], in0=pp[:, 0:L], scalar=w0, in1=acc[:], op0=ALU.mult, op1=ALU.add),
                     reads=[("pp", pb_), "acc"], writes=["acc"])
                E.op("vector", lambda e, pp=pp, w2=w2, pb_=pb_: e.scalar_tensor_tensor(out=pcb[pb_][:], in0=pp[:, 2:L + 2], scalar=w2, in1=acc[:], op0=ALU.mult, op1=ALU.add),
                     reads=[("pp", pb_), "acc"], writes=[("pcb", pb_)])
                tk = fc % 2
                for q in range(0, nsc, 4):
                    nq = min(4, nsc - q)
                    pa = 4 + (q // 4) % 2
                    for i in range(nq):
                        E.op("tensor", lambda e, q=q, i=i, pa=pa, pb_=pb_: e.matmul(self.ps[pa][:, i * 128:(i + 1) * 128], lhsT=pcb[pb_][:, (q + i) * 128:(q + i + 1) * 128], rhs=self.ident[:], start=True, stop=True),
                             reads=[("pcb", pb_)], writes=[("ps", pa)])
                    E.op("scalar", lambda e, q=q, nq=nq, pa=pa, tk=tk: e.activation(out=tokm[tk][:, q:q + nq, :].rearrange("p a b -> p (a b)"), in_=self.ps[pa][:, 0:nq * 128], func=AF.Copy),
                         reads=[("ps", pa)], writes=[("tokm", tk)])
                E.dma("sync", [lambda e, fc=fc, tk=tk: e.dma_start(out=D_["VX"][fc // 8, fc % 8], in_=tokm[tk][:])], ("tokms", tk), reads=[("tokm", tk)], writes=[("VXd", fc)])

    def _hy_conv(self, L, nm):
        E = self.E
        D_ = self.get_hyd(nm)
        nsc = L // 128
        nb2 = L // 128
        nrc = 2 * L // 128
        E.phase()
        zin = E.tmp("zin", [128, nsc, 512], BF16)
        Y = E.tmp("Y", [128, nrc, 512], BF16)
        ftb = [E.tmp("ftb%d" % i, [128, 2, nsc, 128], BF16) for i in range(2)]
        gtb = [E.tmp("gtb%d" % i, [128, nrc, 128], BF16) for i in range(2)]
        ks = [E.tmp("ks%d" % i, [128, 2, 512], F32) for i in range(2)]
        tt = [E.tmp("tt%d" % i, [128, 512], F32) for i in range(4)]
        xt = [E.tmp("xt%d" % i, [128, 512], BF16) for i in range(2)]
        z2 = [E.tmp("z2%d" % i, [128, 512], BF16) for i in range(2)]
        zT = [E.tmp("zT%d" % i, [128, 4, 128], BF16) for i in range(2)]
        ZTv = D_["ZT"].rearrange("(c p) t -> p c t", p=128)
        gcnt = {"ft": 0, "gt": 0, "x": 0}
        for dblk in range(2):
            qs = 8 if nsc >= 8 else nsc
            E.dma("sync", [lambda e, i=i, dblk=dblk, q0=q0: e.dma_start(out=zin[:, q0:q0 + qs, i * 128:(i + 1) * 128], in_=D_["VX"][0, dblk * 4 + i, :, q0:q0 + qs, :])
                           for i in range(4) for q0 in range(0, nsc, qs)],
                  "zin", writes=["zin"] + [("zin1", sc) for sc in range(nsc)])
            for o in range(2):
                zkey = (lambda sc: "zin") if o == 0 else (lambda sc: ("zin1", sc))
                for j in range(nb2):
                    b = gcnt["ft"] % 2
                    gcnt["ft"] += 1
                    E.dma("scalar", [lambda e, j=j, b=b: e.dma_start(out=ftb[b][:, 0], in_=D_["FT"][j])], ("ftbA", b), writes=[("ftb", b, 0)])
                    E.dma("gpsimd", [lambda e, j=j, b=b: e.dma_start(out=ftb[b][:, 1], in_=D_["FT"][nb2 + j])], ("ftbB", b), writes=[("ftb", b, 1)])
                    E.dma("sync", [lambda e, j=j, b=b, o=o, dblk=dblk: e.dma_start(out=ks[b][:], in_=D_["KS"][o, :, j * 128:(j + 1) * 128, dblk * 512:(dblk + 1) * 512].rearrange("a p d -> p a d"))],
                          ("ks", b), writes=[("ks", b)])
                    pr, pi = (0, 1) if b == 0 else (2, 3)
                    for sc in range(nsc):
                        E.op("tensor", lambda e, sc=sc, b=b, pr=pr: e.matmul(self.ps[pr][:, 0:512], lhsT=ftb[b][:, 0, sc, :], rhs=zin[:, sc, :], start=(sc == 0), stop=(sc == nsc - 1)),
                             reads=[("ftb", b, 0), zkey(sc)], writes=[("ps", pr)])
                    for sc in range(nsc):
                        E.op("tensor", lambda e, sc=sc, b=b, pi=pi: e.matmul(self.ps[pi][:, 0:512], lhsT=ftb[b][:, 1, sc, :], rhs=zin[:, sc, :], start=(sc == 0), stop=(sc == nsc - 1)),
                             reads=[("ftb", b, 1), zkey(sc)], writes=[("ps", pi)])
                    KR, KI = ks[b][:, 0, :], ks[b][:, 1, :]
                    E.op("vector", lambda e, pr=pr, KR=KR: e.tensor_tensor(out=tt[0][:], in0=self.ps[pr][:, 0:512], in1=KR, op=ALU.mult), reads=[("ps", pr), ("ks", b)], writes=[("tt", 0)])
                    E.op("vector", lambda e, pi=pi, KI=KI: e.tensor_tensor(out=tt[1][:], in0=self.ps[pi][:, 0:512], in1=KI, op=ALU.mult), reads=[("ps", pi), ("ks", b)], writes=[("tt", 1)])
                    E.op("gpsimd", lambda e, j=j: e.tensor_tensor(out=Y[:, j, :], in0=tt[0][:], in1=tt[1][:], op=ALU.subtract), reads=[("tt", 0), ("tt", 1)], writes=[("Y", j)])
                    E.op("vector", lambda e, pr=pr, KI=KI: e.tensor_tensor(out=tt[2][:], in0=self.ps[pr][:, 0:512], in1=KI, op=ALU.mult), reads=[("ps", pr), ("ks", b)], writes=[("tt", 2)])
                    E.op("vector", lambda e, pi=pi, KR=KR: e.tensor_tensor(out=tt[3][:], in0=self.ps[pi][:, 0:512], in1=KR, op=ALU.mult), reads=[("ps", pi), ("ks", b)], writes=[("tt", 3)])
                    E.op("gpsimd", lambda e, j=j: e.tensor_tensor(out=Y[:, nb2 + j, :], in0=tt[2][:], in1=tt[3][:], op=ALU.add), reads=[("tt", 2), ("tt", 3)], writes=[("Y", nb2 + j)])
                    if j == 0:
                        E.op("gpsimd", lambda e: e.tensor_copy(out=Y[0:1, 0, :], in_=tt[0][0:1, :]), reads=[("tt", 0), ("Y", 0)], writes=[("Y", 0)])
                        E.op("gpsimd", lambda e: e.tensor_copy(out=Y[0:1, nb2, :], in_=tt[1][0:1, :]), reads=[("tt", 1), ("Y", nb2)], writes=[("Y", nb2)])
                for tb in range(nsc):
                    b = gcnt["gt"] % 2
                    gcnt["gt"] += 1
                    xb = gcnt["x"] % 2
                    gcnt["x"] += 1
                    hr = nrc // 2
                    E.dma("scalar", [lambda e, tb=tb, b=b: e.dma_start(out=gtb[b][:, 0:hr, :], in_=D_["GT"][tb, :, 0:hr, :])], ("gtbA", b), writes=[("gtb", b, 0)])
                    E.dma("gpsimd", [lambda e, tb=tb, b=b: e.dma_start(out=gtb[b][:, hr:nrc, :], in_=D_["GT"][tb, :, hr:nrc, :])], ("gtbB", b), writes=[("gtb", b, 1)])
                    E.dma("sync", [lambda e, i=i, tb=tb, xb=xb, o=o, dblk=dblk: e.dma_start(out=xt[xb][:, i * 128:(i + 1) * 128], in_=D_["VX"][1 + o, dblk * 4 + i, :, tb, :]) for i in range(4)],
                          ("xt", xb), writes=[("xt", xb)])
                    po = 4 + b
                    for rc in range(nrc):
                        E.op("tensor", lambda e, rc=rc, b=b, po=po: e.matmul(self.ps[po][:, 0:512], lhsT=gtb[b][:, rc, :], rhs=Y[:, rc, :], start=(rc == 0), stop=(rc == nrc - 1)),
                             reads=[("gtb", b, 0 if rc < nrc // 2 else 1), ("Y", rc)], writes=[("ps", po)])
                    if o == 0:
                        E.op("vector", lambda e, tb=tb, po=po, xb=xb: e.tensor_tensor(out=zin[:, tb, :], in0=self.ps[po][:, 0:512], in1=xt[xb][:], op=ALU.mult),
                             reads=[("ps", po), ("xt", xb)], writes=[("zin1", tb)])
                    else:
                        E.op("vector", lambda e, po=po, xb=xb: e.tensor_tensor(out=z2[xb][:], in0=self.ps[po][:, 0:512], in1=xt[xb][:], op=ALU.mult),
                             reads=[("ps", po), ("xt", xb)], writes=[("z2", xb)])
                        for i in range(4):
                            E.op("tensor", lambda e, i=i, xb=xb: e.matmul(self.ps[6 + xb][:, i * 128:(i + 1) * 128], lhsT=z2[xb][:, i * 128:(i + 1) * 128], rhs=self.ident[:], start=True, stop=True),
                                 reads=[("z2", xb)], writes=[("ps", 6 + xb)])
                        E.op("scalar", lambda e, xb=xb: e.activation(out=zT[xb][:].rearrange("p a b -> p (a b)"), in_=self.ps[6 + xb][:, 0:512], func=AF.Copy),
                             reads=[("ps", 6 + xb)], writes=[("zT", xb)])
                        E.dma("sync", [lambda e, xb=xb, tb=tb, dblk=dblk: e.dma_start(out=ZTv[:, dblk * 4:(dblk + 1) * 4, tb * 128:(tb + 1) * 128], in_=zT[xb][:])],
                              ("zTs", xb), reads=[("zT", xb)], writes=[("ZTd", dblk, tb)])

    def _hy_out(self, layer, L, col0, cond, nm):
        E = self.E
        _ = (self.hy_w_out,)
        D_ = self.get_hyd(nm)
        E.phase()
        hv = self.H.rearrange("(c p) t -> p c t", p=128)
        ZTv = D_["ZT"].rearrange("(c p) t -> p c t", p=128)
        wo = E.tmp("hwo", [128, 8, D], BF16)
        hb = [E.tmp("hb%d" % i, [128, 8, 512], F32) for i in range(2)]
        zb = [E.tmp("zb%d" % i, [128, 8, 512], BF16) for i in range(2)]
        yv = E.tmp("yv", [128, 512], F32)
        E.dma("gpsimd", [lambda e: e.dma_start(out=wo[:], in_=self.hy_w_out.rearrange("(kc p) d -> p kc d", p=128))], "hwo", writes=["hwo"])
        boo, _ = VEC_LAYOUT["hy_b_out"]
        for ti, t0 in enumerate(range(0, L, 512)):
            n = min(512, L - t0)
            b = ti % 2
            E.dma("sync", [lambda e, b=b, t0=t0, n=n: e.dma_start(out=hb[b][:, :, 0:n], in_=hv[:, :, col0 + t0:col0 + t0 + n])], ("oh", b), writes=[("oh", b)])
            E.dma("sync", [lambda e, b=b, t0=t0, n=n: e.dma_start(out=zb[b][:, :, 0:n], in_=ZTv[:, :, t0:t0 + n])], ("oz", b), writes=[("oz", b)])
            for dc in range(8):
                pa = dc % 2
                for kc in range(8):
                    E.op("tensor", lambda e, dc=dc, kc=kc, b=b, n=n, pa=pa: e.matmul(self.ps[pa][:, 0:n], lhsT=wo[:, kc, dc * 128:(dc + 1) * 128], rhs=zb[b][:, kc, 0:n], start=(kc == 0), stop=(kc == 7)),
                         reads=["hwo", ("oz", b)], writes=[("ps", pa)])
                E.op("vector", lambda e, dc=dc, n=n, pa=pa: e.tensor_scalar(out=yv[:, 0:n], in0=self.ps[pa][:, 0:n], scalar1=self.V[:, boo + dc:boo + dc + 1], scalar2=None, op0=ALU.add),
                     reads=[("ps", pa), "vecs"], writes=["yv"])
                gate = self.AD[:, layer, 1, 2, dc, cond:cond + 1]
                E.op("vector", lambda e, dc=dc, n=n, b=b, gate=gate: e.scalar_tensor_tensor(out=hb[b][:, dc, 0:n], in0=yv[:, 0:n], scalar=gate, in1=hb[b][:, dc, 0:n], op0=ALU.mult, op1=ALU.add),
                     reads=["yv", ("oh", b), "AD"], writes=[("oho", b, dc)])
            E.dma("sync", [lambda e, b=b, t0=t0, n=n: e.dma_start(out=hv[:, :, col0 + t0:col0 + t0 + n], in_=hb[b][:, :, 0:n])], ("ohs", b),
                  reads=[("oho", b, dc) for dc in range(8)] + [("oh", b)], writes=[("Ho", ti)])


VEC_LAYOUT = {}
NVEC = 0


def _vreg(name, shape):
    global NVEC
    VEC_LAYOUT[name] = (NVEC, tuple(shape))
    NVEC += int(np.prod(shape))


_vreg("cond", (8, 2))
_vreg("ada_b", (DEPTH, 72, 2))
_vreg("norm_g", (DEPTH, 3, 8))
_vreg("final_g", (8,))
_vreg("pool_b", (8,))
_vreg("pool_scale", (8,))
_vreg("ret_decay", (2, 8))
_vreg("hy_b_in", (24,))
_vreg("hy_w_short", (3, 24))
_vreg("hy_b_short", (24,))
_vreg("hy_b_out", (8,))
RETC_N = 2 * 128 + 2 * 128 + 2 * 128 + 2


def _pack_vecs(inp, b):
    V = np.zeros((128, NVEC), np.float32)

    def put(name, arr):
        off, shape = VEC_LAYOUT[name]
        n = int(np.prod(shape))
        V[:, off:off + n] = np.asarray(arr, np.float32).reshape(128, n)

    cond = np.stack([_fm(inp["c"][b]), _fm(inp["c_ctx"])], axis=-1)
    put("cond", cond)
    ab = _fm(inp["ada_b"])
    put("ada_b", np.repeat(ab[..., None], 2, axis=-1))
    put("norm_g", _fm(inp["norm_g"]))
    put("final_g", _fm(inp["final_g"]))
    put("pool_b", _fm(inp["pool_b"][0]))
    put("pool_scale", _fm(inp["pool_scale"][0]))
    put("hy_b_in", _fm(inp["hy_b_in"][0]))
    put("hy_w_short", _fm(inp["hy_w_short"][0]))
    put("hy_b_short", _fm(inp["hy_b_short"][0]))
    put("hy_b_out", _fm(inp["hy_b_out"][0]))
    put("ret_decay", np.broadcast_to(np.asarray(inp["ret_decay"], np.float32).reshape(1, 2, 8), (128, 2, 8)))
    return V


def _pool_rc():
    rc = np.zeros((4, 512 + 256), np.float32)
    for gi, win in enumerate((2, 4, 8, 16)):
        for w, off, reps in ((64, 0, 8), (256, 512, 1)):
            pos = np.arange(w)
            lo = np.clip(pos - win // 2, 0, w)
            hi = np.clip(pos - win // 2 + win, 0, w)
            r = (1.0 / (hi - lo)).astype(np.float32)
            rc[gi, off:off + w * reps] = np.tile(r, reps)
    return np.ascontiguousarray(np.broadcast_to(rc[None], (128, 4, 768)))


def _hy_consts(L):
    N = 2 * L
    nsc = L // 128
    s_ = np.arange(L, dtype=np.int64)
    r = np.arange(N, dtype=np.int64)
    f = np.where(r < L, r, r - L)
    ang = (2.0 * np.pi / N) * ((s_[:, None] * f[None, :]) % N)
    FT = np.where((r <= L)[None, :], np.cos(ang), -np.sin(ang))
    FT[:, L] = np.where(s_ % 2 == 0, 1.0, -1.0)
    cf = np.where((f == 0) | (r == L), 1.0, 2.0) / N
    GT = (FT * cf[None, :]).T
    FTt = FT.reshape(nsc, 128, 2 * nsc, 128).transpose(2, 1, 0, 3)
    GTt = GT.reshape(2 * nsc, 128, nsc, 128).transpose(2, 1, 0, 3)
    t = np.linspace(0.0, 1.0, L, dtype=np.float32)
    bands = np.linspace(1e-4, 15.0, 16, dtype=np.float32)
    a2 = (np.float32(2.0 * math.pi / L) * np.arange(L, dtype=np.float32)[:, None] * bands[None, :]).astype(np.float32)
    feat = np.concatenate([t[:, None], np.cos(a2), -np.sin(a2)], axis=-1).astype(np.float32)
    tneg = (-t).reshape(nsc, 128).T
    return dict(FT=np.ascontiguousarray(FTt).astype(ml_dtypes.bfloat16), GT=np.ascontiguousarray(GTt).astype(ml_dtypes.bfloat16),
                featT=np.ascontiguousarray(feat.T), tneg=np.ascontiguousarray(tneg, np.float32))


_CACHE = {}


def _get_prog(stop_after=None):
    key = stop_after
    if key not in _CACHE:
        p = Prog(stop_after)
        with contextlib.ExitStack() as st:
            p.build()
            p.E.emit(st)
        _CACHE[key] = p
    return _CACHE[key]


def kernel(stop_after=None, **inp):
    inp = {k: np.asarray(v) for k, v in inp.items()}
    p = _get_prog(stop_after)
    cosT, sinT = _rot_tables()
    rc = _ret_consts()
    retc = np.concatenate([rc["rel"][0], rc["rel"][1], rc["mask"][0], rc["mask"][1], rc["qexp"][0], rc["qexp"][1],
                           rc["kexp"].T], axis=1).astype(np.float32)
    shared = {
        "ada_w": np.ascontiguousarray(inp["ada_w"], np.float32),
        "ffn_w1": np.ascontiguousarray(inp["ffn_w1"], np.float32),
        "ffn_w3": np.ascontiguousarray(inp["ffn_w3"], np.float32),
        "ffn_w2": np.ascontiguousarray(inp["ffn_w2"], np.float32),
        "ret_w_in": np.ascontiguousarray(inp["ret_w_in"], np.float32),
        "ret_w_out": np.ascontiguousarray(inp["ret_w_out"], np.float32),
        "pool_w": np.ascontiguousarray(inp["pool_w"][0], np.float32),
        "rot": np.stack([cosT, sinT]),
        "retc": np.ascontiguousarray(retc),
        "poolrc": _pool_rc(),
        "identb": np.eye(128, dtype=np.float32).astype(ml_dtypes.bfloat16),
        "hy_w_in": np.ascontiguousarray(inp["hy_w_in"][0], np.float32),
        "hy_w_out": np.ascontiguousarray(inp["hy_w_out"][0], np.float32),
        "hy_w_pos": np.ascontiguousarray(inp["hy_w_pos"][0], np.float32),
        "hy_w_mid": np.ascontiguousarray(inp["hy_w_mid"][0], np.float32),
        "hy_w_filt": np.ascontiguousarray(inp["hy_w_filt"][0], np.float32),
    }
    hv = np.zeros((64, 8), np.float32)
    hv[:, 0] = inp["hy_b_pos"][0]
    hv[:, 1] = inp["hy_b_mid"][0]
    hv[:, 2] = inp["hy_freq"][0]
    shared["hyv64"] = hv
    min_decay = math.log(1e-2) / 1.5
    max_decay = math.log(1e-2) / 0.3
    shared["hy_delta"] = np.ascontiguousarray(np.broadcast_to(np.abs(np.linspace(min_decay, max_decay, D, dtype=np.float32))[None], (128, D)))
    shared["hyb_bc"] = np.ascontiguousarray(np.broadcast_to(np.asarray(inp["hy_bias"][0], np.float32).reshape(1, 2 * D), (128, 2 * D)))
    for nm, L in (("l", SEQ), ("c", CTX)):
        if ("FT_" + nm) not in p.inputs:
            continue
        if "hyc_" + nm not in _CACHE:
            _CACHE["hyc_" + nm] = _hy_consts(L)
        for k, v in _CACHE["hyc_" + nm].items():
            shared[k + "_" + nm] = v
    in_maps = []
    for b in range(NCORES):
        m = dict(shared)
        m["h0"] = np.ascontiguousarray(np.concatenate([inp["ctx"][b].T, inp["x"][b].T], axis=1), np.float32)
        m["vecs"] = _pack_vecs(inp, b)
        in_maps.append({k: v for k, v in m.items() if k in p.inputs})
    if TEST_CORES is not None:
        res = run_bass_kernel_spmd(p.nc, in_maps[:TEST_CORES], core_ids=list(range(TEST_CORES)))
        DEBUG_OUT.update({k: np.asarray(v) for k, v in res.results[0].items()})
        return None
    res = run_bass_kernel_spmd(p.nc, in_maps, core_ids=list(range(NCORES)))
    if DEBUG_DUMP:
        DEBUG_OUT.update({k: np.asarray(v) for k, v in res.results[0].items()})
    out = np.stack([np.ascontiguousarray(res.results[b]["outT"].T) for b in range(NCORES)])
    return out.astype(np.float32)
```

```python
import contextlib
import math
import numpy as np
import ml_dtypes
import concourse.bass as bass
import concourse.mybir as mybir
from concourse.bass_utils import run_bass_kernel_spmd

F32 = mybir.dt.float32
BF16 = mybir.dt.bfloat16
AF = mybir.ActivationFunctionType
ALU = mybir.AluOpType
AX = mybir.AxisListType

D = 1024
SEQ = 4096
CTX = 256
NT = SEQ + CTX
DEPTH = 4
FFN = 2816
NFC = FFN // 128
EPS = 1e-6
NCORES = 4
TEST_CORES = None
DEBUG_DUMP = False
DEBUG_OUT = {}
ENGS = ("tensor", "vector", "scalar", "gpsimd", "sync")

CTX_TILE = (0, CTX, 1)
LAT_TILES = [(CTX + 512 * i, 512, 0) for i in range(8)]


class _Op:
    __slots__ = ("eng", "fn", "waits", "tick", "needs_inc", "is_dma", "sem", "seq")


class Emitter:
    def __init__(self, nc):
        self.nc = nc
        self.ops = {e: [] for e in ENGS}
        self.last_w = {}
        self.readers = {}
        self.dma_sems = {}
        self.eng_sem = {}
        self.pending_dma = []
        self.cur_map = {}
        arena = nc.alloc_sbuf_tensor("arena", [128, 52000], F32)
        base = nc.lookup_mloc(arena).addr
        self.sb_base = base
        self.sb_top = base
        self.sb_cnt = 0
        self.sb_limit = base + 52000 * 4

    def _alloc(self, name, shape, dtype, off):
        self.sb_cnt += 1
        return self.nc.alloc_sbuf_tensor_at("%s_%d" % (name, self.sb_cnt), list(shape), dtype, offset=off)

    @staticmethod
    def _bytes(shape, dtype):
        n = 1
        for s in shape[1:]:
            n *= s
        return n * (2 if dtype == BF16 else 4)

    def persist(self, name, shape, dtype):
        assert self.sb_top == self.sb_base
        off = self.sb_base
        self.sb_base += (self._bytes(shape, dtype) + 63) // 64 * 64
        self.sb_top = self.sb_base
        assert self.sb_base <= self.sb_limit
        return self._alloc(name, shape, dtype, off)

    def tmp(self, name, shape, dtype):
        off = self.sb_top
        self.sb_top += (self._bytes(shape, dtype) + 63) // 64 * 64
        assert self.sb_top <= self.sb_limit, (name, self.sb_top)
        return self._alloc(name, shape, dtype, off)

    def phase(self):
        self.barrier()
        self.sb_top = self.sb_base

    def _deps(self, op, reads, writes):
        deps = []
        for k in reads:
            w = self.last_w.get(k)
            if w is not None:
                deps.append(w)
        for k in writes:
            w = self.last_w.get(k)
            if w is not None:
                deps.append(w)
            deps.extend(self.readers.get(k, ()))
        self._add_waits(op, deps)
        for k in reads:
            self.readers.setdefault(k, []).append(op)
        for k in writes:
            self.last_w[k] = op
            self.readers[k] = []

    def _add_waits(self, op, deps):
        seen = set(id(d) for d in op.waits)
        latest = {}
        rest = []
        for d in deps:
            if d.is_dma:
                rest.append(d)
            elif d.eng not in latest or d.seq > latest[d.eng].seq:
                latest[d.eng] = d
        deps = rest + list(latest.values())
        for d in deps:
            if d is op or id(d) in seen:
                continue
            seen.add(id(d))
            if d.eng == op.eng and d.eng == "tensor" and not d.is_dma and not op.is_dma:
                continue
            if not d.is_dma:
                d.needs_inc = True
            op.waits.append(d)

    def op(self, eng, fn, reads=(), writes=()):
        o = _Op()
        o.eng = eng; o.fn = fn; o.waits = []; o.tick = None; o.needs_inc = False
        o.is_dma = False; o.sem = None
        o.seq = len(self.ops[eng])
        self._deps(o, reads, writes)
        self.ops[eng].append(o)
        return o

    def dma(self, eng, fns, semkey, reads=(), writes=()):
        o = _Op()
        o.eng = eng; o.fn = fns; o.waits = []; o.needs_inc = False
        o.is_dma = True
        slot = self.cur_map.setdefault(semkey, len(self.cur_map))
        ent = self.dma_sems.setdefault(slot, [None, 0])
        ent[1] += 16 * len(fns)
        o.sem = slot; o.tick = ent[1]
        o.seq = len(self.ops[eng])
        self._deps(o, reads, writes)
        self.ops[eng].append(o)
        self.pending_dma.append(o)
        return o

    def barrier(self):
        lasts = []
        for e in ENGS:
            for o in reversed(self.ops[e]):
                if not o.is_dma:
                    if o.fn is not None:
                        lasts.append(o)
                    break
        deps = lasts + self.pending_dma
        self.pending_dma = []
        self.cur_map = {}
        for e in ENGS:
            o = _Op()
            o.eng = e; o.fn = None; o.waits = []; o.tick = None; o.needs_inc = False
            o.is_dma = False; o.sem = None
            o.seq = len(self.ops[e])
            self._add_waits(o, deps)
            self.ops[e].append(o)
        self.last_w = {}
        self.readers = {}

    def emit(self, stack):
        nc = self.nc
        for e in ENGS:
            self.eng_sem[e] = stack.enter_context(nc.semaphore("s_" + e))
        for i, (k, ent) in enumerate(self.dma_sems.items()):
            ent[0] = stack.enter_context(nc.semaphore("d_%d" % i))
        for e in ENGS:
            t = 0
            for o in self.ops[e]:
                if not o.is_dma and o.needs_inc:
                    assert o.fn is not None
                    t += 1
                    o.tick = t
        block = stack.enter_context(nc.Block())
        em = self

        def run(engname):
            def body(eng):
                waited = {}
                for o in em.ops[engname]:
                    for d in o.waits:
                        if d.is_dma:
                            sem = em.dma_sems[d.sem][0]; key = ("d", d.sem)
                        else:
                            sem = em.eng_sem[d.eng]; key = ("e", d.eng)
                        if waited.get(key, 0) >= d.tick:
                            continue
                        waited[key] = d.tick
                        eng.wait_ge(sem, d.tick)
                    if o.is_dma:
                        sem = em.dma_sems[o.sem][0]
                        for f in o.fn:
                            f(eng).then_inc(sem, 16)
                    elif o.fn is not None:
                        ins = o.fn(eng)
                        if o.needs_inc:
                            ins.then_inc(em.eng_sem[engname], 1)
            return body

        block.tensor(run("tensor"))
        block.vector(run("vector"))
        block.scalar(run("scalar"))
        block.gpsimd(run("gpsimd"))
        block.sync(run("sync"))


def _fm(vec):
    v = np.asarray(vec, np.float32)
    lead = v.shape[:-1]
    n = v.shape[-1] // 128
    v = v.reshape(*lead, n, 128)
    return np.ascontiguousarray(np.moveaxis(v, -1, 0))


def _rot_tables():
    half = 128
    inv = 1.0 / (10000.0 ** np.linspace(0.0, 1.0, half, dtype=np.float32).astype(np.float64))
    pos = np.arange(NT, dtype=np.float64)
    ang = (pos[None, :].astype(np.float32) * inv[:, None].astype(np.float32)).astype(np.float64)
    return np.cos(ang).astype(np.float32), np.sin(ang).astype(np.float32)


def _ret_consts():
    n = np.arange(128)
    m = np.arange(128)[:, None]
    nn = n[None, :]
    c = {}
    c["rel"] = np.stack([np.maximum(nn - m, 0), np.maximum(m - nn, 0)]).astype(np.float32)
    c["mask"] = np.stack([(nn >= m), (m > nn)]).astype(np.float32) / 16.0
    c["qexp"] = np.stack([np.broadcast_to(n + 1.0, (128, 128)), np.broadcast_to(128.0 - n, (128, 128))]).astype(np.float32)
    c["kexp"] = np.stack([127.0 - n, n * 1.0]).astype(np.float32)
    return c


class Prog:
    def __init__(self, stop_after=None):
        self.stop_after = stop_after
        nc = self.nc = bass.Bass("TRN2", target_bir_lowering=False)
        self.E = Emitter(nc)
        self.inputs = {}
        self.stopped = False

    _SPECS = {
        "h0": ([D, NT], F32), "vecs": ([128, None], F32), "ada_w": ([DEPTH, D, 9 * D], F32),
        "ffn_w1": ([DEPTH, 2, D, FFN], F32), "ffn_w3": ([DEPTH, 2, D, FFN], F32), "ffn_w2": ([DEPTH, 2, FFN, D], F32),
        "ret_w_in": ([2, D, 6144], F32), "ret_w_out": ([2, 2048, D], F32), "pool_w": ([4, 256, 256], F32),
        "rot": ([2, 128, NT], F32), "retc": ([128, None], F32), "poolrc": ([128, 4, 768], F32), "identb": ([128, 128], BF16),
        "hy_w_in": ([D, 3 * D], F32), "hy_w_out": ([D, D], F32), "hy_w_pos": ([33, 64], F32), "hy_w_mid": ([64, 64], F32),
        "hy_w_filt": ([64, 4 * D], F32), "hyv64": ([64, 8], F32), "hy_delta": ([128, D], F32), "hyb_bc": ([128, 2 * D], F32),
    }

    def __getattr__(self, name):
        specs = type(self)._SPECS
        if name in specs:
            shape, dt = specs[name]
            shape = [NVEC if (x is None and name == "vecs") else (RETC_N if x is None else x) for x in shape]
            t = self.din(name, shape, dt)
            self.__dict__[name] = t
            return t
        raise AttributeError(name)

    def get_hyd(self, nm):
        if nm not in self.hyd:
            nc = self.nc
            L = SEQ if nm == "l" else CTX
            nsc = L // 128
            dk = "ExternalOutput" if DEBUG_DUMP else "Internal"
            self.hyd[nm] = dict(
                featT=self.din("featT_" + nm, [33, L]),
                tneg=self.din("tneg_" + nm, [128, nsc]),
                FT=self.din("FT_" + nm, [2 * nsc, 128, nsc, 128], BF16),
                GT=self.din("GT_" + nm, [nsc, 128, 2 * nsc, 128], BF16),
                HSD=nc.dram_tensor("HSD_" + nm, [nsc, 128, 2, 2 * D], BF16, kind=dk).ap(),
                KS=nc.dram_tensor("KS_" + nm, [2, 2, L, D], F32, kind=dk).ap(),
                VX=nc.dram_tensor("VX_" + nm, [3, 8, 128, nsc, 128], BF16, kind=dk).ap(),
                ZT=nc.dram_tensor("ZT_" + nm, [D, L], BF16, kind=dk).ap(),
            )
        return self.hyd[nm]

    def din(self, name, shape, dtype=F32):
        t = self.nc.dram_tensor(name, list(shape), dtype, kind="ExternalInput").ap()
        self.inputs[name] = t
        return t

    def build(self):
        nc, E = self.nc, self.E
        _ = (self.h0, self.vecs, self.identb)
        self.hyd = {}
        self.outT = nc.dram_tensor("outT", [D, SEQ], F32, kind="ExternalOutput").ap()
        dk = "ExternalOutput" if DEBUG_DUMP else "Internal"
        self.H = nc.dram_tensor("Hres", [D, NT], F32, kind=dk).ap()
        self.QT = nc.dram_tensor("QT", [D, NT], BF16).ap()
        self.KT = nc.dram_tensor("KT", [D, NT], BF16).ap()
        self.VT = nc.dram_tensor("VT", [NT, 2048], BF16).ap()
        self.GT = nc.dram_tensor("GT", [NT, 2048], BF16).ap()
        self.OF = nc.dram_tensor("OF", [NT, 2048], F32).ap()

        self.ps = [nc.alloc_psum_tensor("ps%d" % i, [128, 512], F32) for i in range(8)]

        self.V = E.persist("vecs", [128, NVEC], F32)
        self.ones = E.persist("ones", [128, 128], BF16)
        self.ident = E.persist("ident", [128, 128], BF16)
        self.AD = E.persist("AD", [128, DEPTH, 3, 3, 8, 2], F32)
        self.finA = E.persist("finA", [128, 8], F32)
        self.epsD = E.persist("epsD", [128, 1], F32)
        self.eps1 = E.persist("eps1", [128, 1], F32)

        E.dma("sync", [lambda e: e.dma_start(out=self.V[:], in_=self.vecs)], "vecs", writes=["vecs"])
        E.dma("sync", [lambda e: e.dma_start(out=self.ident[:], in_=self.identb)], "ident", writes=["ident"])
        E.op("vector", lambda e: e.memset(self.ones[:], 1.0), writes=["ones"])
        E.op("vector", lambda e: e.memset(self.epsD[:], D * EPS), writes=["epsD"])
        E.op("vector", lambda e: e.memset(self.eps1[:], EPS), writes=["eps1"])
        E.op("vector", lambda e: e.tensor_scalar(out=self.finA[:], in0=self.vsl("final_g"), scalar1=32.0, scalar2=None, op0=ALU.mult),
             reads=["vecs"], writes=["finA"])
        E.barrier()

        self.mods_all()
        if self.stop_after is not None and self.stop_after.startswith("HY"):
            stage = int(self.stop_after[2])
            big = len(self.stop_after) > 3
            L_, col_, cond_, nm_ = (SEQ, CTX, 0, "l") if big else (CTX, 0, 1, "c")
            E.phase()
            tb_ = E.tmp("cp", [128, 8, 512], F32)
            for t0 in range(0, L_, 512):
                n = min(512, L_ - t0)
                E.dma("sync", [lambda e, t0=t0, n=n: e.dma_start(out=tb_[:, :, 0:n], in_=self.h0.rearrange("(c p) t -> p c t", p=128)[:, :, col_ + t0:col_ + t0 + n])], "cp", writes=["cp"])
                E.dma("sync", [lambda e, t0=t0, n=n: e.dma_start(out=self.H.rearrange("(c p) t -> p c t", p=128)[:, :, col_ + t0:col_ + t0 + n], in_=tb_[:, :, 0:n])], "cp2", reads=["cp"], writes=["Hcp"])
            self._hy_filter(L_, nm_)
            if stage >= 2:
                self._hy_proj(2, L_, col_, cond_, nm_)
            if stage >= 3:
                self._hy_conv(L_, nm_)
            if stage >= 4:
                self._hy_out(2, L_, col_, cond_, nm_)
            self.finish()
            return
        src = self.h0
        for layer in range(DEPTH):
            kind = layer % 3
            slot = layer // 3
            last = layer == DEPTH - 1
            ctx_out = not last
            tiles_a = [CTX_TILE] + LAT_TILES
            tiles_b = ([CTX_TILE] if ctx_out else []) + LAT_TILES
            self.ffn_half(layer, 0, tiles_a, src)
            src = self.H
            if self.check_stop("L%da" % layer):
                return
            if kind == 0:
                self.retention(layer, slot, ctx_out)
            elif kind == 1:
                self.pool(layer, slot, tiles_b)
            else:
                self.hyena(layer, slot, ctx_out)
            if self.check_stop("L%db" % layer):
                return
            self.ffn_half(layer, 1, tiles_b, self.H)
            if self.check_stop("L%dc" % layer):
                return
        self.final_norm()
        self.finish()

    def check_stop(self, tag):
        if self.stop_after == tag:
            self.dump_h()
            self.finish()
            self.stopped = True
            return True
        return False

    def finish(self):
        E = self.E
        E.barrier()

    def dump_h(self):
        E = self.E
        E.phase()
        bufs = [E.tmp("dump%d" % i, [128, 8, 512], F32) for i in range(2)]
        for i, (c0, n, cond) in enumerate(LAT_TILES):
            t = bufs[i % 2]
            E.dma("sync", [lambda e, t=t, c0=c0, n=n: e.dma_start(out=t[:], in_=self.H.rearrange("(c p) t -> p c t", p=128)[:, :, c0:c0 + n])],
                  ("dl", i % 2), writes=[("dump", i % 2)])
            E.dma("sync", [lambda e, t=t, c0=c0, n=n: e.dma_start(out=self.outT.rearrange("(c p) t -> p c t", p=128)[:, :, c0 - CTX:c0 - CTX + n], in_=t[:])],
                  ("ds", i % 2), reads=[("dump", i % 2)], writes=[("dumpo", i)])

    def vsl(self, name, *idx):
        off, shape = VEC_LAYOUT[name]
        n = int(np.prod(shape))
        ap = self.V[:, off:off + n]
        return ap

    def vcol(self, name, flat_index):
        off, shape = VEC_LAYOUT[name]
        return self.V[:, off + flat_index:off + flat_index + 1]

    def mods_all(self):
        E = self.E
        E.phase()
        sc = E.tmp("sc", [128, 8, 2], BF16)
        mods = E.tmp("mods", [128, 72, 2], F32)
        wb = [E.tmp("adaw%d" % i, [128, 8, 1152], BF16) for i in range(2)]
        E.op("scalar", lambda e: e.activation(out=sc[:].rearrange("p c k -> p (c k)"), in_=self.vsl("cond"), func=AF.Silu),
             reads=["vecs"], writes=["sc"])
        gi = 0
        for layer in range(DEPTH):
            if self.stop_after is not None and self.stop_after.startswith("HY") and layer != 2:
                continue
            for g in range(8):
                b = gi % 2
                gi += 1
                src = self.ada_w[layer].rearrange("(kc p) f -> p kc f", p=128)[:, :, g * 1152:(g + 1) * 1152]
                E.dma("gpsimd", [lambda e, b=b, src=src: e.dma_start(out=wb[b][:], in_=src)], ("adaw", b), writes=[("adaw", b)])
                for j in range(9):
                    oc = g * 9 + j
                    for kc in range(8):
                        E.op("tensor", lambda e, b=b, j=j, kc=kc, oc=oc: e.matmul(self.ps[0][:, oc * 2:oc * 2 + 2], lhsT=wb[b][:, kc, j * 128:(j + 1) * 128],
                                                                                 rhs=sc[:, kc, :], start=(kc == 0), stop=(kc == 7)),
                             reads=[("adaw", b), "sc"], writes=["psmods"])
            off, _ = VEC_LAYOUT["ada_b"]
            E.op("vector", lambda e, layer=layer, off=off: e.tensor_tensor(out=mods[:].rearrange("p c k -> p (c k)"), in0=self.ps[0][:, 0:144],
                                                                           in1=self.V[:, off + layer * 144: off + (layer + 1) * 144], op=ALU.add),
                 reads=["psmods", "vecs"], writes=["mods"])
            ngo, _ = VEC_LAYOUT["norm_g"]
            for n in range(3):
                shift = mods[:, (3 * n) * 8:(3 * n + 1) * 8, :]
                scale = mods[:, (3 * n + 1) * 8:(3 * n + 2) * 8, :]
                gate = mods[:, (3 * n + 2) * 8:(3 * n + 3) * 8, :]
                for cond in range(2):
                    gsl = self.V[:, ngo + (layer * 3 + n) * 8: ngo + (layer * 3 + n + 1) * 8]
                    E.op("vector", lambda e, layer=layer, n=n, cond=cond, scale=scale, gsl=gsl: e.scalar_tensor_tensor(
                        out=self.AD[:, layer, n, 0, :, cond], in0=scale[:, :, cond], scalar=1.0, in1=gsl, op0=ALU.add, op1=ALU.mult),
                        reads=["mods", "vecs"], writes=["AD"])
                E.op("vector", lambda e, layer=layer, n=n: e.tensor_scalar(out=self.AD[:, layer, n, 0, :, :], in0=self.AD[:, layer, n, 0, :, :],
                                                                           scalar1=32.0, scalar2=None, op0=ALU.mult), reads=["AD"], writes=["AD"])
                E.op("vector", lambda e, layer=layer, n=n, shift=shift: e.tensor_copy(out=self.AD[:, layer, n, 1, :, :], in_=shift), reads=["mods"], writes=["AD"])
                gmul = 1.0 if n == 1 else 0.5
                E.op("vector", lambda e, layer=layer, n=n, gate=gate, gmul=gmul: e.tensor_scalar(out=self.AD[:, layer, n, 2, :, :], in0=gate,
                                                                                                 scalar1=gmul, scalar2=None, op0=ALU.mult), reads=["mods"], writes=["AD"])
        E.barrier()

    def adaln(self, h, n, layer, nrm, cond, out, key_in, key_out, sq, rstd, tmp, psb, A=None, shift=None, idx=0, out3=None):
        E = self.E
        sqk = ("sq", idx % 2)
        for c in range(8):
            E.op("scalar", lambda e, c=c: e.activation(out=sq[:, c, 0:n], in_=h[:, c, 0:n], func=AF.Square), reads=[key_in], writes=[sqk])
        for c in range(8):
            E.op("tensor", lambda e, c=c: e.matmul(self.ps[psb][:, 0:n], lhsT=self.ones[:], rhs=sq[:, c, 0:n], start=(c == 0), stop=(c == 7)),
                 reads=[sqk], writes=[("ps", psb)])
        rk = ("rstd", idx % 2)
        E.op("scalar", lambda e: e.activation(out=rstd[:, 0:n], in_=self.ps[psb][:, 0:n], func=AF.Ln, bias=self.epsD[:, 0:1], scale=1.0),
             reads=[("ps", psb)], writes=[rk])
        E.op("scalar", lambda e: e.activation(out=rstd[:, 0:n], in_=rstd[:, 0:n], func=AF.Exp, scale=-0.5),
             reads=[rk], writes=[rk])
        for c in range(8):
            tk = ("tmpn", c % 2)
            tt = tmp[c % 2]
            Ac = A(c) if A is not None else self.AD[:, layer, nrm, 0, c, cond:cond + 1]
            E.op("vector", lambda e, c=c, tt=tt: e.tensor_tensor(out=tt[:, 0:n], in0=h[:, c, 0:n], in1=rstd[:, 0:n], op=ALU.mult),
                 reads=[key_in, rk], writes=[tk])
            if shift is None:
                sh = self.AD[:, layer, nrm, 1, c, cond:cond + 1]
            else:
                sh = shift(c)
            src_ap = tt[:, 0:n] if out3 is None else tt[:, 0:n].rearrange("p (r w) -> p r w", w=out3[1])
            E.op("scalar", lambda e, c=c, src_ap=src_ap, Ac=Ac, sh=sh: e.activation(out=out(c), in_=src_ap, func=AF.Identity, bias=sh, scale=Ac),
                 reads=[tk, "AD"], writes=[key_out])

    def ffn_half(self, layer, which, tiles, src):
        E = self.E
        nrm = 0 if which == 0 else 2
        sts = []
        cur, cnt = [], 0
        for t in tiles:
            if cnt + t[1] > 2304:
                sts.append(cur)
                cur, cnt = [], 0
            cur.append(t)
            cnt += t[1]
        sts.append(cur)
        srcv = src.rearrange("(c p) t -> p c t", p=128)
        dstv = self.H.rearrange("(c p) t -> p c t", p=128)
        w1v = self.ffn_w1[layer, which].rearrange("(kc p) f -> p kc f", p=128)
        w3v = self.ffn_w3[layer, which].rearrange("(kc p) f -> p kc f", p=128)
        w2v = self.ffn_w2[layer, which].rearrange("(fc p) d -> p fc d", p=128)
        for st in sts:
            self._ffn_st(layer, nrm, st, srcv, dstv, w1v, w3v, w2v)

    def _ffn_st(self, layer, nrm, st, srcv, dstv, w1v, w3v, w2v):
        E = self.E
        if True:
            E.phase()
            ntok = sum(t[1] for t in st)
            hbuf = E.tmp("h", [128, 8, ntok], F32)
            ubuf = E.tmp("u", [128, 8, ntok], BF16)
            sq = E.tmp("sq", [128, 8, 512], BF16)
            rstd = E.tmp("rstd", [128, 512], F32)
            tmpn = [E.tmp("tmpn%d" % i, [128, 512], F32) for i in range(2)]
            GS = 4
            groups = [(f0, min(GS, NFC - f0)) for f0 in range(0, NFC, GS)]
            w13 = [E.tmp("w13_%d" % i, [128, 2, 8, GS * 128], BF16) for i in range(2)]
            w2 = [E.tmp("w2_%d" % i, [128, GS, D], BF16) for i in range(2)]
            gb = [E.tmp("g%d" % i, [128, GS, 512], BF16) for i in range(2)]
            sl = [E.tmp("sl%d" % i, [128, 512], F32) for i in range(2)]
            offs = []
            o = 0
            for t in st:
                offs.append(o)
                o += t[1]
            for ti, (c0, n, cond) in enumerate(st):
                o = offs[ti]
                E.dma("sync", [lambda e, o=o, c0=c0, n=n: e.dma_start(out=hbuf[:, :, o:o + n], in_=srcv[:, :, c0:c0 + n])],
                      ("hld", ti), writes=[("h", ti)])

            def do_adaln(ti):
                c0, n, cond = st[ti]
                o = offs[ti]
                self.adaln(hbuf[:, :, o:o + n], n, layer, nrm, cond, lambda c, o=o, n=n: ubuf[:, c, o:o + n],
                           ("h", ti), ("u", ti), sq, rstd, tmpn, 6, idx=ti)

            do_adaln(0)
            items = [(gi_, ti) for gi_ in range(len(groups)) for ti in range(len(st))]

            def load_w(gi_):
                b = gi_ % 2
                f0, gsz = groups[gi_]
                E.dma("gpsimd", [lambda e: e.dma_start(out=w13[b][:, 0, :, 0:gsz * 128], in_=w1v[:, :, f0 * 128:(f0 + gsz) * 128]),
                                 lambda e: e.dma_start(out=w13[b][:, 1, :, 0:gsz * 128], in_=w3v[:, :, f0 * 128:(f0 + gsz) * 128]),
                                 lambda e: e.dma_start(out=w2[b][:, 0:gsz, :], in_=w2v[:, f0:f0 + gsz, :])],
                      ("ffw", b), writes=[("ffw", b)])

            def stage_a(it, k):
                gi_, ti = it
                b = gi_ % 2
                f0, gsz = groups[gi_]
                c0, n, cond = st[ti]
                o = offs[ti]
                gk = k % 2
                for j in range(gsz):
                    pa, pb = (2 * j) % 4, (2 * j + 1) % 4
                    for kc in range(8):
                        E.op("tensor", lambda e, kc=kc, j=j, pa=pa: e.matmul(self.ps[pa][:, 0:n], lhsT=w13[b][:, 0, kc, j * 128:(j + 1) * 128],
                                                                             rhs=ubuf[:, kc, o:o + n], start=(kc == 0), stop=(kc == 7)),
                             reads=[("ffw", b), ("u", ti)], writes=[("ps", pa)])
                    for kc in range(8):
                        E.op("tensor", lambda e, kc=kc, j=j, pb=pb: e.matmul(self.ps[pb][:, 0:n], lhsT=w13[b][:, 1, kc, j * 128:(j + 1) * 128],
                                                                             rhs=ubuf[:, kc, o:o + n], start=(kc == 0), stop=(kc == 7)),
                             reads=[("ffw", b), ("u", ti)], writes=[("ps", pb)])
                    E.op("scalar", lambda e, j=j, pa=pa: e.activation(out=sl[j % 2][:, 0:n], in_=self.ps[pa][:, 0:n], func=AF.Silu),
                         reads=[("ps", pa)], writes=[("sl", j % 2)])
                    E.op("vector", lambda e, j=j, pb=pb: e.tensor_tensor(out=gb[gk][:, j, 0:n], in0=sl[j % 2][:, 0:n], in1=self.ps[pb][:, 0:n], op=ALU.mult),
                         reads=[("sl", j % 2), ("ps", pb)], writes=[("g", gk, j)])

            def stage_b(it, k):
                gi_, ti = it
                b = gi_ % 2
                f0, gsz = groups[gi_]
                c0, n, cond = st[ti]
                o = offs[ti]
                gk = k % 2
                for dc in range(8):
                    py = 4 + dc % 4
                    for j in range(gsz):
                        E.op("tensor", lambda e, dc=dc, j=j, py=py: e.matmul(self.ps[py][:, 0:n], lhsT=w2[b][:, j, dc * 128:(dc + 1) * 128],
                                                                             rhs=gb[gk][:, j, 0:n], start=(j == 0), stop=(j == gsz - 1)),
                             reads=[("ffw", b), ("g", gk, j)], writes=[("ps", py)])
                    gate = self.AD[:, layer, nrm, 2, dc, cond:cond + 1]
                    E.op("vector", lambda e, dc=dc, py=py, gate=gate: e.scalar_tensor_tensor(out=hbuf[:, dc, o:o + n], in0=self.ps[py][:, 0:n], scalar=gate,
                                                                                             in1=hbuf[:, dc, o:o + n], op0=ALU.mult, op1=ALU.add),
                         reads=[("ps", py), "AD"], writes=[("hd", ti, dc)])

            load_w(0)
            prev = None
            for k, it in enumerate(items):
                stage_a(it, k)
                if it[0] == 0 and it[1] + 1 < len(st):
                    do_adaln(it[1] + 1)
                if prev is not None:
                    stage_b(prev[0], prev[1])
                if it[1] == 0 and it[0] + 1 < len(groups):
                    load_w(it[0] + 1)
                prev = (it, k)
            stage_b(prev[0], prev[1])
            for ti, (c0, n, cond) in enumerate(st):
                o = offs[ti]
                E.dma("sync", [lambda e, o=o, c0=c0, n=n: e.dma_start(out=dstv[:, :, c0:c0 + n], in_=hbuf[:, :, o:o + n])],
                      ("hst", ti % 2), reads=[("hd", ti, dc) for dc in range(8)] + [("h", ti)], writes=[("Hd", ti)])

    def final_norm(self):
        E = self.E
        E.phase()
        hv = self.H.rearrange("(c p) t -> p c t", p=128)
        ov = self.outT.rearrange("(c p) t -> p c t", p=128)
        hb = [E.tmp("fh%d" % i, [128, 8, 512], F32) for i in range(2)]
        ob = [E.tmp("fo%d" % i, [128, 8, 512], F32) for i in range(2)]
        sq = E.tmp("sq", [128, 8, 512], BF16)
        rstd = E.tmp("rstd", [128, 512], F32)
        tmpn = [E.tmp("tmpn%d" % i, [128, 512], F32) for i in range(2)]
        zero = E.tmp("zero", [128, 1], F32)
        E.op("vector", lambda e: e.memset(zero[:], 0.0), writes=["zero"])
        for ti, (c0, n, cond) in enumerate(LAT_TILES):
            b = ti % 2
            E.dma("sync", [lambda e, b=b, c0=c0, n=n: e.dma_start(out=hb[b][:], in_=hv[:, :, c0:c0 + n])], ("fh", b), writes=[("fh", b)])
            self.adaln(hb[b], n, 0, 0, 0, lambda c, b=b: ob[b][:, c, :], ("fh", b), ("fo", b), sq, rstd, tmpn, 6,
                       A=lambda c: self.finA[:, c:c + 1], shift=lambda c: zero[:, 0:1], idx=ti)
            E.dma("scalar", [lambda e, b=b, c0=c0, n=n: e.dma_start(out=ov[:, :, c0 - CTX:c0 - CTX + n], in_=ob[b][:])], ("fo", b),
                  reads=[("fo", b)], writes=[("out", ti)])

    def load_norm_tiles(self, tiles, layer, nrm, out_fn, hb, sq, rstd, tmpn, keyp):
        E = self.E
        hv = self.H.rearrange("(c p) t -> p c t", p=128)
        for ti, (c0, n, cond) in enumerate(tiles):
            b = ti % 2
            E.dma("sync", [lambda e, b=b, c0=c0, n=n: e.dma_start(out=hb[b][:, :, 0:n], in_=hv[:, :, c0:c0 + n])], (keyp, b), writes=[(keyp, b)])
            self.adaln(hb[b], n, layer, nrm, cond, (lambda c, ti=ti: out_fn(ti, c)), (keyp, b), (keyp + "o", ti), sq, rstd, tmpn, 6, idx=ti)

    def retention(self, layer, slot, ctx_out):
        E = self.E
        _ = (self.rot, self.retc, self.ret_w_in, self.ret_w_out)
        tiles = [CTX_TILE] + LAT_TILES
        E.phase()
        u_all = E.tmp("u_all", [128, 8, NT], BF16)
        cos = E.tmp("cos", [128, NT], F32)
        sin = E.tmp("sin", [128, NT], F32)
        hb = [E.tmp("hb%d" % i, [128, 8, 512], F32) for i in range(2)]
        sq = E.tmp("sq", [128, 8, 512], BF16)
        rstd = E.tmp("rstd", [128, 512], F32)
        tmpn = [E.tmp("tmpn%d" % i, [128, 512], F32) for i in range(2)]
        wg = [E.tmp("wg%d" % i, [128, 8, 512], BF16) for i in range(2)]
        rt = [E.tmp("rt%d" % i, [128, 512], F32) for i in range(4)]
        ob = [E.tmp("ob%d" % i, [128, 512], BF16) for i in range(4)]
        E.dma("sync", [lambda e: e.dma_start(out=cos[:], in_=self.rot[0]), lambda e: e.dma_start(out=sin[:], in_=self.rot[1])], "rot", writes=["rot"])
        self.load_norm_tiles(tiles, layer, 1, lambda ti, c: u_all[:, c, tiles[ti][0]:tiles[ti][0] + tiles[ti][1]], hb, sq, rstd, tmpn, "rh")
        ukeys = [("rho", ti) for ti in range(len(tiles))]
        wv = self.ret_w_in[slot].rearrange("(kc p) f -> p kc f", p=128)
        cnt = {"ob": 0, "t": 0}

        def loadw(g):
            b = g % 2
            E.dma("gpsimd", [lambda e, b=b, g=g: e.dma_start(out=wg[b][:], in_=wv[:, :, g * 512:(g + 1) * 512])], ("rwg", b), writes=[("rwg", b)])

        loadw(0)
        for g in range(12):
            b = g % 2
            if g + 1 < 12:
                loadw(g + 1)
            if g < 4:
                dst = (self.QT if g < 2 else self.KT).rearrange("(c p) t -> p c t", p=128)
                for hh in range(2):
                    for ti, (c0, n, cond) in enumerate(tiles):
                        pa, pb = (0, 1) if cnt["t"] % 2 == 0 else (2, 3)
                        cnt["t"] += 1
                        for kc in range(8):
                            E.op("tensor", lambda e, kc=kc, hh=hh, c0=c0, n=n, pa=pa, b=b: e.matmul(
                                self.ps[pa][:, 0:n], lhsT=wg[b][:, kc, hh * 256:hh * 256 + 128], rhs=u_all[:, kc, c0:c0 + n], start=(kc == 0), stop=(kc == 7)),
                                reads=[("rwg", b), ukeys[ti]], writes=[("ps", pa)])
                        for kc in range(8):
                            E.op("tensor", lambda e, kc=kc, hh=hh, c0=c0, n=n, pb=pb, b=b: e.matmul(
                                self.ps[pb][:, 0:n], lhsT=wg[b][:, kc, hh * 256 + 128:hh * 256 + 256], rhs=u_all[:, kc, c0:c0 + n], start=(kc == 0), stop=(kc == 7)),
                                reads=[("rwg", b), ukeys[ti]], writes=[("ps", pb)])
                        o1 = cnt["ob"] % 4
                        o2 = (cnt["ob"] + 1) % 4
                        cnt["ob"] += 2
                        E.op("vector", lambda e, c0=c0, n=n, pa=pa: e.tensor_tensor(out=rt[0][:, 0:n], in0=self.ps[pa][:, 0:n], in1=cos[:, c0:c0 + n], op=ALU.mult),
                             reads=[("ps", pa), "rot"], writes=[("rt", 0)])
                        E.op("vector", lambda e, c0=c0, n=n, pb=pb: e.tensor_tensor(out=rt[1][:, 0:n], in0=self.ps[pb][:, 0:n], in1=sin[:, c0:c0 + n], op=ALU.mult),
                             reads=[("ps", pb), "rot"], writes=[("rt", 1)])
                        E.op("gpsimd", lambda e, n=n, o1=o1: e.tensor_tensor(out=ob[o1][:, 0:n], in0=rt[0][:, 0:n], in1=rt[1][:, 0:n], op=ALU.subtract),
                             reads=[("rt", 0), ("rt", 1)], writes=[("ob", o1)])
                        E.op("vector", lambda e, c0=c0, n=n, pb=pb: e.tensor_tensor(out=rt[2][:, 0:n], in0=self.ps[pb][:, 0:n], in1=cos[:, c0:c0 + n], op=ALU.mult),
                             reads=[("ps", pb), "rot"], writes=[("rt", 2)])
                        E.op("vector", lambda e, c0=c0, n=n, pa=pa: e.tensor_tensor(out=rt[3][:, 0:n], in0=self.ps[pa][:, 0:n], in1=sin[:, c0:c0 + n], op=ALU.mult),
                             reads=[("ps", pa), "rot"], writes=[("rt", 3)])
                        E.op("gpsimd", lambda e, n=n, o2=o2: e.tensor_tensor(out=ob[o2][:, 0:n], in0=rt[2][:, 0:n], in1=rt[3][:, 0:n], op=ALU.add),
                             reads=[("rt", 2), ("rt", 3)], writes=[("ob", o2)])
                        fc = (g % 2) * 4 + hh * 2
                        E.dma("gpsimd", [lambda e, o1=o1, fc=fc, c0=c0, n=n, dst=dst: e.dma_start(out=dst[:, fc, c0:c0 + n], in_=ob[o1][:, 0:n])],
                              ("obs", o1), reads=[("ob", o1)], writes=[("qk", g, hh, ti, 0)])
                        E.dma("gpsimd", [lambda e, o2=o2, fc=fc, c0=c0, n=n, dst=dst: e.dma_start(out=dst[:, fc + 1, c0:c0 + n], in_=ob[o2][:, 0:n])],
                              ("obs", o2), reads=[("ob", o2)], writes=[("qk", g, hh, ti, 1)])
            else:
                dstT = self.VT if g < 8 else self.GT
                col0 = ((g - 4) % 4) * 512
                for tc in range(NT // 128):
                    tok0 = tc * 128
                    pa = cnt["t"] % 4
                    cnt["t"] += 1
                    ti = 0 if tok0 < CTX else 1 + (tok0 - CTX) // 512
                    for kc in range(8):
                        E.op("tensor", lambda e, kc=kc, tok0=tok0, pa=pa, b=b: e.matmul(
                            self.ps[pa][:, 0:512], lhsT=u_all[:, kc, tok0:tok0 + 128], rhs=wg[b][:, kc, :], start=(kc == 0), stop=(kc == 7)),
                            reads=[("rwg", b), ukeys[ti]], writes=[("ps", pa)])
                    o1 = cnt["ob"] % 4
                    cnt["ob"] += 1
                    fn = AF.Copy if g < 8 else AF.Silu
                    E.op("scalar", lambda e, pa=pa, o1=o1, fn=fn: e.activation(out=ob[o1][:], in_=self.ps[pa][:, 0:512], func=fn),
                         reads=[("ps", pa)], writes=[("ob", o1)])
                    E.dma("scalar", [lambda e, o1=o1, tok0=tok0, col0=col0, dstT=dstT: e.dma_start(out=dstT[tok0:tok0 + 128, col0:col0 + 512], in_=ob[o1][:])],
                          ("obs", o1), reads=[("ob", o1)], writes=[("vg", g, tc)])

        E.phase()
        RC = E.tmp("retc", [128, RETC_N], F32)
        E.dma("sync", [lambda e: e.dma_start(out=RC[:], in_=self.retc)], "retc", writes=["retc"])
        rel = [RC[:, 0:128], RC[:, 128:256]]
        mask = [RC[:, 256:384], RC[:, 384:512]]
        qexp = [RC[:, 512:640], RC[:, 640:768]]
        kexp = [RC[:, 768:769], RC[:, 769:770]]
        lg = E.tmp("lg", [128, 8], F32)
        onec = E.tmp("onec", [128, 1], F32)
        dm = E.tmp("dm", [128, 8, 128], F32)
        qd = E.tmp("qd", [128, 8, 128], F32)
        kd = E.tmp("kd", [128, 8], F32)
        cd = E.tmp("cd", [128, 8], F32)
        E.op("vector", lambda e: e.memset(onec[:], 1.0), writes=["onec"])
        do, _ = VEC_LAYOUT["ret_decay"]
        dsl = self.V[:, do + slot * 8: do + slot * 8 + 8]
        E.op("scalar", lambda e: e.activation(out=lg[:], in_=dsl, func=AF.Exp, scale=-1.0), reads=["retc"], writes=["lg"])
        E.op("scalar", lambda e: e.activation(out=lg[:], in_=lg[:], func=AF.Ln, bias=onec[:, 0:1], scale=1.0), reads=["lg", "onec"], writes=["lg"])
        E.op("vector", lambda e: e.tensor_scalar(out=lg[:], in0=lg[:], scalar1=-1.0, scalar2=None, op0=ALU.mult), reads=["lg"], writes=["lg"])
        for d_ in range(2):
            for h in range(4):
                i = d_ * 4 + h
                E.op("scalar", lambda e, i=i, d_=d_: e.activation(out=dm[:, i, :], in_=rel[d_], func=AF.Exp, scale=lg[:, i:i + 1]), reads=["lg", "retc"], writes=["dm"])
                E.op("vector", lambda e, i=i, d_=d_: e.tensor_tensor(out=dm[:, i, :], in0=dm[:, i, :], in1=mask[d_], op=ALU.mult), reads=["dm", "retc"], writes=["dm"])
                E.op("scalar", lambda e, i=i, d_=d_: e.activation(out=qd[:, i, :], in_=qexp[d_], func=AF.Exp, scale=lg[:, i:i + 1]), reads=["lg", "retc"], writes=["qd"])
                E.op("scalar", lambda e, i=i, d_=d_: e.activation(out=kd[:, i:i + 1], in_=kexp[d_], func=AF.Exp, scale=lg[:, i:i + 1]), reads=["lg", "retc"], writes=["kd"])
        E.op("vector", lambda e: e.tensor_scalar(out=kd[:], in0=kd[:], scalar1=1.0 / 16.0, scalar2=None, op0=ALU.mult), reads=["kd"], writes=["kd"])
        E.op("scalar", lambda e: e.activation(out=cd[:], in_=lg[:], func=AF.Exp, scale=128.0), reads=["lg"], writes=["cd"])

        S = E.tmp("S", [128, 4, 2, 512], F32)
        Sb = E.tmp("Sb", [128, 4, 2, 512], BF16)
        qt = [E.tmp("qt%d" % i, [128, 8, 128], BF16) for i in range(3)]
        kt = [E.tmp("kt%d" % i, [128, 8, 128], BF16) for i in range(3)]
        vt = [E.tmp("vt%d" % i, [128, 2048], BF16) for i in range(3)]
        gt = [E.tmp("gt%d" % i, [128, 2048], BF16) for i in range(2)]
        ofs = [E.tmp("of%d" % i, [128, 2048], F32) for i in range(2)]
        sT = [E.tmp("sT%d" % i, [128, 128], BF16) for i in range(2)]
        kp = [E.tmp("kp%d" % i, [128, 256], BF16) for i in range(2)]
        qp = [E.tmp("qp%d" % i, [128, 2, 128], BF16) for i in range(2)]
        ot = E.tmp("ot", [128, 2048], F32)
        ysq = E.tmp("ysq", [128, 512], F32)
        st4 = E.tmp("st4", [128, 8, 4], F32)
        yb = E.tmp("yb", [128, 2048], BF16)
        yT = E.tmp("yT", [128, 16, 128], BF16)
        wo = E.tmp("wo", [128, 16, D], BF16)
        hch = [E.tmp("hch%d" % i, [128, 8, 128], F32) for i in range(2)]
        epsc = E.tmp("epsc", [128, 1], F32)
        E.op("vector", lambda e: e.memset(epsc[:], EPS), writes=["epsc"])
        zeroc = E.tmp("zeroc", [128, 1], F32)
        E.op("vector", lambda e: e.memset(zeroc[:], 0.0), writes=["zeroc"])
        E.dma("gpsimd", [lambda e: e.dma_start(out=wo[:], in_=self.ret_w_out[slot].rearrange("(fc p) d -> p fc d", p=128))], "wo", writes=["wo"])
        qv = self.QT.rearrange("(c p) t -> p c t", p=128)
        kv = self.KT.rearrange("(c p) t -> p c t", p=128)
        hv = self.H.rearrange("(c p) t -> p c t", p=128)
        nchunk = NT // 128
        for d_ in range(2):
            order = list(range(nchunk)) if d_ == 0 else [1, 0] + list(range(nchunk - 1, 1, -1))
            for ci, tc in enumerate(order):
                self._ret_chunk(layer, d_, ci, tc, ci == 0, ci == len(order) - 1, (tc >= 2) or ctx_out,
                                dict(qt=qt, kt=kt, vt=vt, gt=gt, ofs=ofs, sT=sT, kp=kp, qp=qp, ot=ot, ysq=ysq, st4=st4, yb=yb, yT=yT, wo=wo,
                                     hch=hch, S=S, Sb=Sb, zeroc=zeroc, dm=dm, qd=qd, kd=kd, cd=cd, qv=qv, kv=kv, hv=hv, epsc=epsc))

    def _ret_chunk(self, layer, d_, ci, tc, first, lastc, need_out, B):
        E = self.E
        tok0 = tc * 128
        cond = 1 if tc < 2 else 0
        gi = d_ * 64 + ci
        b3 = gi % 3
        b2 = gi % 2
        qt, kt, vt = B["qt"][b3], B["kt"][b3], B["vt"][b3]
        E.dma("sync", [lambda e: e.dma_start(out=qt[:], in_=B["qv"][:, :, tok0:tok0 + 128])], ("qt", b3), writes=[("qt", b3)])
        E.dma("sync", [lambda e: e.dma_start(out=kt[:], in_=B["kv"][:, :, tok0:tok0 + 128])], ("kt", b3), writes=[("kt", b3)])
        E.dma("sync", [lambda e: e.dma_start(out=vt[:], in_=self.VT[tok0:tok0 + 128, :])], ("vt", b3), writes=[("vt", b3)])
        readout = need_out and d_ == 1
        if readout:
            gt, ofl, hch = B["gt"][b2], B["ofs"][b2], B["hch"][b2]
            E.dma("sync", [lambda e: e.dma_start(out=gt[:], in_=self.GT[tok0:tok0 + 128, :])], ("gt", b2), writes=[("gt", b2)])
            E.dma("sync", [lambda e: e.dma_start(out=ofl[:], in_=self.OF[tok0:tok0 + 128, :])], ("ofl", b2), writes=[("of", b2)])
            E.dma("sync", [lambda e: e.dma_start(out=hch[:], in_=B["hv"][:, :, tok0:tok0 + 128])], ("hch", b2), writes=[("hch", b2)])
        elif need_out:
            ofl = B["ofs"][b2]
        S, Sb, ot = B["S"], B["Sb"], B["ot"]

        def do_head(h):
            i = d_ * 4 + h
            hb_ = h % 2
            bs, bo = hb_, 2 + hb_
            sT, kp, qp = B["sT"][hb_], B["kp"][hb_], B["qp"][hb_]
            vh = vt[:, h * 512:(h + 1) * 512]
            if need_out:
                for c in range(2):
                    E.op("tensor", lambda e, c=c: e.matmul(self.ps[bs][:, 0:128], lhsT=kt[:, 2 * h + c, :], rhs=qt[:, 2 * h + c, :], start=(c == 0), stop=(c == 1)),
                         reads=[("qt", b3), ("kt", b3)], writes=[("pss", bs)])
                E.op("vector", lambda e: e.tensor_tensor(out=sT[:], in0=self.ps[bs][:, 0:128], in1=B["dm"][:, i, :], op=ALU.mult),
                     reads=[("pss", bs), "dm"], writes=[("sT", hb_)])
            if not lastc:
                for c in range(2):
                    E.op("tensor", lambda e, c=c: e.matmul(self.ps[bs][:, 128 + c * 128:256 + c * 128], lhsT=kt[:, 2 * h + c, :], rhs=self.ident[:], start=True, stop=True),
                         reads=[("kt", b3)], writes=[("pst", bs)])
                E.op("vector", lambda e: e.tensor_scalar(out=kp[:], in0=self.ps[bs][:, 128:384], scalar1=B["kd"][:, i:i + 1], scalar2=None, op0=ALU.mult),
                     reads=[("pst", bs), "kd"], writes=[("kp", hb_)])
            if need_out:
                if not first:
                    for c in range(2):
                        E.op("gpsimd", lambda e, c=c: e.tensor_tensor(out=qp[:, c, :], in0=qt[:, 2 * h + c, :], in1=B["qd"][:, i, :], op=ALU.mult),
                             reads=[("qt", b3), "qd"], writes=[("qp", hb_)])
                E.op("tensor", lambda e: e.matmul(self.ps[bo][:, 0:512], lhsT=sT[:], rhs=vh, start=True, stop=first),
                     reads=[("sT", hb_), ("vt", b3)], writes=[("ps", bo)])
                if not first:
                    for c in range(2):
                        E.op("tensor", lambda e, c=c: e.matmul(self.ps[bo][:, 0:512], lhsT=qp[:, c, :], rhs=Sb[:, h, c, :], start=False, stop=(c == 1)),
                             reads=[("qp", hb_), ("Sb", h, c)], writes=[("ps", bo)])
                if d_ == 0:
                    E.op("scalar", lambda e: e.activation(out=ofl[:, h * 512:(h + 1) * 512], in_=self.ps[bo][:, 0:512], func=AF.Copy),
                         reads=[("ps", bo)], writes=[("of", b2)])
                else:
                    E.op("vector", lambda e: e.tensor_tensor(out=ot[:, h * 512:(h + 1) * 512], in0=self.ps[bo][:, 0:512], in1=ofl[:, h * 512:(h + 1) * 512], op=ALU.add),
                         reads=[("ps", bo), ("of", b2)], writes=[("ot", h)])
            if not lastc:
                for c in range(2):
                    E.op("tensor", lambda e, c=c: e.matmul(self.ps[4 + c][:, 0:512], lhsT=kp[:, c * 128:(c + 1) * 128], rhs=vh, start=True, stop=True),
                         reads=[("kp", hb_), ("vt", b3)], writes=[("ps", 4 + c)])
                    if first:
                        E.op("vector", lambda e, c=c: e.tensor_copy(out=S[:, h, c, :], in_=self.ps[4 + c][:, 0:512]), reads=[("ps", 4 + c)], writes=[("S", h, c)])
                    else:
                        E.op("vector", lambda e, c=c: e.scalar_tensor_tensor(out=S[:, h, c, :], in0=S[:, h, c, :], scalar=B["cd"][:, i:i + 1], in1=self.ps[4 + c][:, 0:512],
                                                                             op0=ALU.mult, op1=ALU.add), reads=[("ps", 4 + c), ("S", h, c), "cd"], writes=[("S", h, c)])
                    E.op("scalar", lambda e, c=c: e.activation(out=Sb[:, h, c, :], in_=S[:, h, c, :], func=AF.Copy), reads=[("S", h, c)], writes=[("Sb", h, c)])
        for h_ in range(4):
            do_head(h_)
        if need_out and d_ == 0:
            E.dma("scalar", [lambda e: e.dma_start(out=self.OF[tok0:tok0 + 128, :], in_=ofl[:])], ("ofs", b2), reads=[("of", b2)], writes=[("OFd", tc)])
        if not readout:
            return
        st4, ysq, yb, yT, wo = B["st4"], B["ysq"], B["yb"], B["yT"], B["wo"]
        for h in range(4):
            oh = ot[:, h * 512:(h + 1) * 512]
            E.op("vector", lambda e, h=h, oh=oh: e.tensor_reduce(out=st4[:, 0, h:h + 1], in_=oh, axis=AX.X, op=ALU.add), reads=[("ot", h)], writes=[("st", 0, h)])
            E.op("scalar", lambda e, oh=oh: e.activation(out=ysq[:], in_=oh, func=AF.Square), reads=[("ot", h)], writes=["ysq"])
            E.op("vector", lambda e, h=h: e.tensor_reduce(out=st4[:, 1, h:h + 1], in_=ysq[:], axis=AX.X, op=ALU.add), reads=["ysq"], writes=[("st", 1, h)])
        allst = [("st", a, h) for a in range(2) for h in range(4)]
        E.op("vector", lambda e: e.tensor_scalar(out=st4[:, 0:2, :], in0=st4[:, 0:2, :], scalar1=1.0 / 512.0, scalar2=None, op0=ALU.mult), reads=allst, writes=["stA"])
        E.op("vector", lambda e: e.tensor_tensor(out=st4[:, 2, :], in0=st4[:, 0, :], in1=st4[:, 0, :], op=ALU.mult), reads=["stA"], writes=["stB"])
        E.op("vector", lambda e: e.tensor_tensor(out=st4[:, 3, :], in0=st4[:, 1, :], in1=st4[:, 2, :], op=ALU.subtract), reads=["stA", "stB"], writes=["stC"])
        E.op("scalar", lambda e: e.activation(out=st4[:, 4, :], in_=st4[:, 3, :], func=AF.Ln, bias=B["epsc"][:, 0:1], scale=1.0), reads=["stC", "epsc"], writes=["stD"])
        E.op("scalar", lambda e: e.activation(out=st4[:, 5, :], in_=st4[:, 4, :], func=AF.Exp, scale=-0.5), reads=["stD"], writes=["stE"])
        E.op("vector", lambda e: e.scalar_tensor_tensor(out=st4[:, 6, :], in0=st4[:, 0, :], scalar=-1.0, in1=st4[:, 5, :], op0=ALU.mult, op1=ALU.mult),
             reads=["stA", "stE"], writes=["stF"])
        for h in range(4):
            oh = ot[:, h * 512:(h + 1) * 512]
            E.op("scalar", lambda e, h=h, oh=oh: e.activation(out=oh, in_=oh, func=AF.Identity, bias=st4[:, 6, h:h + 1], scale=st4[:, 5, h:h + 1]),
                 reads=[("ot", h), "stE", "stF", "ysq"], writes=[("ot", h)])
            E.op("vector", lambda e, h=h, oh=oh: e.tensor_tensor(out=yb[:, h * 512:(h + 1) * 512], in0=oh, in1=gt[:, h * 512:(h + 1) * 512], op=ALU.mult),
                 reads=[("ot", h), ("gt", b2)], writes=[("yb", h)])
        for fq in range(4):
            for i2 in range(4):
                fc = fq * 4 + i2
                E.op("tensor", lambda e, fc=fc, i2=i2: e.matmul(self.ps[6][:, i2 * 128:(i2 + 1) * 128], lhsT=yb[:, fc * 128:(fc + 1) * 128], rhs=self.ident[:], start=True, stop=True),
                     reads=[("yb", fq)], writes=[("ps", 6)])
            E.op("scalar", lambda e, fq=fq: e.activation(out=yT[:, fq * 4:(fq + 1) * 4, :].rearrange("p a b -> p (a b)"), in_=self.ps[6][:, 0:512], func=AF.Copy),
                 reads=[("ps", 6)], writes=[("yT", fq)])
        for dc in range(8):
            for fc in range(16):
                E.op("tensor", lambda e, dc=dc, fc=fc: e.matmul(self.ps[7][:, 0:128], lhsT=wo[:, fc, dc * 128:(dc + 1) * 128], rhs=yT[:, fc, :], start=(fc == 0), stop=(fc == 15)),
                     reads=["wo"] + [("yT", q) for q in range(4)], writes=[("ps", 7)])
            gate = self.AD[:, layer, 1, 2, dc, cond:cond + 1]
            E.op("vector", lambda e, dc=dc, gate=gate: e.scalar_tensor_tensor(out=hch[:, dc, :], in0=self.ps[7][:, 0:128], scalar=gate, in1=hch[:, dc, :], op0=ALU.mult, op1=ALU.add),
                 reads=[("ps", 7), ("hch", b2), "AD"], writes=[("hcho", b2, dc)])
        E.dma("gpsimd", [lambda e: e.dma_start(out=B["hv"][:, :, tok0:tok0 + 128], in_=hch[:])], ("hchs", b2),
              reads=[("hcho", b2, dc) for dc in range(8)] + [("hch", b2)], writes=[("Hc", tc)])

    def pool(self, layer, slot, tiles):
        ct = [t for t in tiles if t[2] == 1]
        lt = [t for t in tiles if t[2] == 0]
        if ct:
            self._pool_tiles(layer, ct, "c", 1, 256)
        self._pool_tiles(layer, lt, "l", 8, 64)

    def _pool_tiles(self, layer, tiles, nm0, rows0, w0):
        E = self.E
        _ = (self.poolrc, self.pool_w)
        E.phase()
        hv = self.H.rearrange("(c p) t -> p c t", p=128)
        hb = [E.tmp("hb%d" % i, [128, 8, 512], F32) for i in range(2)]
        sq = E.tmp("sq", [128, 8, 512], BF16)
        rstd = E.tmp("rstd", [128, 512], F32)
        tmpn = [E.tmp("tmpn%d" % i, [128, 512], F32) for i in range(2)]
        rc = E.tmp("rc", [128, 4, 768], F32)
        pw = E.tmp("pw", [128, 4, 2, 256], BF16)
        E.dma("sync", [lambda e: e.dma_start(out=rc[:], in_=self.poolrc)], "rc", writes=["rc"])
        E.dma("gpsimd", [lambda e: e.dma_start(out=pw[:], in_=self.pool_w.rearrange("g (cc p) e -> p g cc e", p=128))], "pw", writes=["pw"])
        bufs = {}
        for nm, rows, w in ((nm0, rows0, w0),):
            wp = w + 16
            arr = [E.tmp("up" + nm, [128, 8 * rows, wp], F32)] + [E.tmp("A%d%s" % (k, nm), [128, 8 * rows, wp], F32) for k in range(4)]
            for a_i, a in enumerate(arr):
                E.op("gpsimd", lambda e, a=a: e.memset(a[:], 0.0), writes=[("pad", nm, a_i)])
            bufs[nm] = (arr, rows, w, wp)
        dl = E.tmp("dl", [128, 8, 512], BF16)
        tm = E.tmp("tm", [128, 512], F32)
        yv = E.tmp("yv", [128, 512], F32)
        pbo, _ = VEC_LAYOUT["pool_b"]
        pso, _ = VEC_LAYOUT["pool_scale"]
        for ti, (c0, n, cond) in enumerate(tiles):
            nm = "c" if cond == 1 else "l"
            arr, rows, w, wp = bufs[nm]
            up = arr[0]
            b = ti % 2
            E.dma("sync", [lambda e, b=b, c0=c0, n=n: e.dma_start(out=hb[b][:, :, 0:n], in_=hv[:, :, c0:c0 + n])], ("ph", b), writes=[("ph", b)])
            self.adaln(hb[b], n, layer, 1, cond,
                       (lambda c, up=up, rows=rows, w=w: up[:, c * rows:(c + 1) * rows, 8:8 + w]),
                       ("ph", b), ("up", nm), sq, rstd, tmpn, 6, idx=ti, out3=(rows, w))
            A2, A4, A8, A16 = arr[1], arr[2], arr[3], arr[4]
            R = 8 * rows
            E.op("vector", lambda e, up=up, A2=A2, wp=wp: e.tensor_tensor(out=A2[:, :, 1:wp], in0=up[:, :, 0:wp - 1], in1=up[:, :, 1:wp], op=ALU.add),
                 reads=[("up", nm), ("pad", nm, 0)], writes=[("A", nm, 2)])
            E.op("gpsimd", lambda e, A2=A2, A4=A4, wp=wp, rows=rows: e.tensor_tensor(out=A4[:, 2 * rows:, 1:wp - 1], in0=A2[:, 2 * rows:, 0:wp - 2], in1=A2[:, 2 * rows:, 2:wp], op=ALU.add),
                 reads=[("A", nm, 2)], writes=[("A", nm, 4)])
            E.op("vector", lambda e, A4=A4, A8=A8, wp=wp, rows=rows: e.tensor_tensor(out=A8[:, 4 * rows:, 2:wp - 2], in0=A4[:, 4 * rows:, 0:wp - 4], in1=A4[:, 4 * rows:, 4:wp], op=ALU.add),
                 reads=[("A", nm, 4)], writes=[("A", nm, 8)])
            E.op("gpsimd", lambda e, A8=A8, A16=A16, wp=wp, rows=rows: e.tensor_tensor(out=A16[:, 6 * rows:, 4:wp - 4], in0=A8[:, 6 * rows:, 0:wp - 8], in1=A8[:, 6 * rows:, 8:wp], op=ALU.add),
                 reads=[("A", nm, 8)], writes=[("A", nm, 16)])
            rco = 0 if nm == "l" else 512
            for c in range(8):
                gi = c // 2
                Aw = arr[1 + gi]
                wk = (2, 4, 8, 16)[gi]
                E.op("vector", lambda e, c=c, Aw=Aw, gi=gi, rows=rows, w=w, rco=rco, n=n: e.tensor_tensor(
                    out=tm[:, 0:n].rearrange("p (r w) -> p r w", w=w), in0=Aw[:, c * rows:(c + 1) * rows, 8:8 + w],
                    in1=rc[:, gi, rco:rco + n].rearrange("p (r w) -> p r w", w=w), op=ALU.mult),
                    reads=[("A", nm, wk), "rc"], writes=["tm"])
                E.op("vector", lambda e, c=c, up=up, rows=rows, w=w, n=n: e.tensor_tensor(
                    out=dl[:, c, 0:n].rearrange("p (r w) -> p r w", w=w), in0=tm[:, 0:n].rearrange("p (r w) -> p r w", w=w),
                    in1=up[:, c * rows:(c + 1) * rows, 8:8 + w], op=ALU.subtract),
                    reads=["tm", ("up", nm)], writes=[("dl", c)])
            for gi in range(4):
                for ec in range(2):
                    pbk = ec
                    for cc in range(2):
                        E.op("tensor", lambda e, gi=gi, ec=ec, cc=cc, n=n, pbk=pbk: e.matmul(self.ps[pbk][:, 0:n], lhsT=pw[:, gi, cc, ec * 128:(ec + 1) * 128],
                                                                                             rhs=dl[:, 2 * gi + cc, 0:n], start=(cc == 0), stop=(cc == 1)),
                             reads=["pw", ("dl", 2 * gi + cc)], writes=[("ps", pbk)])
                    dc = 2 * gi + ec
                    E.op("vector", lambda e, dc=dc, n=n, pbk=pbk: e.tensor_scalar(out=yv[:, 0:n], in0=self.ps[pbk][:, 0:n], scalar1=self.V[:, pbo + dc:pbo + dc + 1],
                                                                                  scalar2=self.V[:, pso + dc:pso + dc + 1], op0=ALU.add, op1=ALU.mult),
                         reads=[("ps", pbk), "vecs"], writes=["yv"])
                    gate = self.AD[:, layer, 1, 2, dc, cond:cond + 1]
                    E.op("vector", lambda e, dc=dc, n=n, b=b, gate=gate: e.scalar_tensor_tensor(out=hb[b][:, dc, 0:n], in0=yv[:, 0:n], scalar=gate, in1=hb[b][:, dc, 0:n],
                                                                                                op0=ALU.mult, op1=ALU.add),
                         reads=["yv", ("ph", b), ("up", nm), "AD"], writes=[("pho", b, dc)])
            E.dma("gpsimd", [lambda e, b=b, c0=c0, n=n: e.dma_start(out=hv[:, :, c0:c0 + n], in_=hb[b][:, :, 0:n])], ("phs", b),
                  reads=[("pho", b, dc) for dc in range(8)] + [("ph", b)], writes=[("Hp", ti)])

    def hyena(self, layer, slot, ctx_out):
        insts = [(SEQ, CTX, 0, "l")] + ([(CTX, 0, 1, "c")] if ctx_out else [])
        for (L, col0, cond, nm) in insts:
            self._hy_filter(L, nm)
            self._hy_proj(layer, L, col0, cond, nm)
            self._hy_conv(L, nm)
            self._hy_out(layer, L, col0, cond, nm)

    def _hy_filter(self, L, nm):
        E = self.E
        _ = (self.hy_w_pos, self.hy_w_mid, self.hy_w_filt, self.hyv64, self.hy_delta, self.hyb_bc)
        D_ = self.get_hyd(nm)
        nsc = L // 128
        E.phase()
        nrm = E.tmp("nrm", [128, 2 * D], F32)
        saved_base = E.sb_base
        E.sb_base = E.sb_top
        featT = E.tmp("featT", [33, L], F32)
        wpos = E.tmp("wpos", [33, 64], F32)
        wmid = E.tmp("wmid", [64, 64], F32)
        wfilt = E.tmp("wfilt", [64, 4096], F32)
        hv64 = E.tmp("hv64", [64, 8], F32)
        tneg = E.tmp("tneg", [128, nsc], F32)
        delta = E.tmp("delta", [128, D], F32)
        hdn = [E.tmp("hdn%d" % i, [64, L], F32) for i in range(2)]
        rr_f = E.tmp("rr_f", [64, L], F32)
        rr_i = E.tmp("rr_i", [64, L], mybir.dt.int32)
        ones32 = E.tmp("ones32", [128, 128], F32)
        win = E.tmp("win", [128, D], F32)
        hw = [E.tmp("hw%d" % i, [128, 2, D], F32) for i in range(2)]
        sqt = E.tmp("sqt", [128, 2, D], F32)
        hsd = [E.tmp("hsd%d" % i, [128, 2, 2 * D], BF16) for i in range(2)]
        E.dma("sync", [lambda e: e.dma_start(out=featT[:], in_=D_["featT"]), lambda e: e.dma_start(out=wpos[:], in_=self.hy_w_pos),
                       lambda e: e.dma_start(out=wmid[:], in_=self.hy_w_mid), lambda e: e.dma_start(out=wfilt[:], in_=self.hy_w_filt),
                       lambda e: e.dma_start(out=hv64[:], in_=self.hyv64), lambda e: e.dma_start(out=tneg[:], in_=D_["tneg"]),
                       lambda e: e.dma_start(out=delta[:], in_=self.hy_delta)], "hyf", writes=["hyf"])
        E.op("vector", lambda e: e.memset(ones32[:], 1.0), writes=["ones32"])
        E.op("vector", lambda e: e.tensor_tensor(out=hv64[:, 3:4], in0=hv64[:, 0:1], in1=hv64[:, 2:3], op=ALU.mult), reads=["hyf"], writes=["hv"])
        E.op("vector", lambda e: e.tensor_tensor(out=hv64[:, 4:5], in0=hv64[:, 1:2], in1=hv64[:, 2:3], op=ALU.mult), reads=["hyf", "hv"], writes=["hv"])
        E.op("vector", lambda e: e.memset(hv64[:, 5:6], -math.pi), reads=["hv"], writes=["hv"])
        srcs = [(wpos, featT, 3), (wmid, hdn[0], 4)]
        for li, (wl, src, bcol) in enumerate(srcs):
            for t0 in range(0, L, 512):
                n = min(512, L - t0)
                E.op("tensor", lambda e, wl=wl, src=src, t0=t0, n=n: e.matmul(self.ps[0][0:64, 0:n], lhsT=wl[:], rhs=src[:, t0:t0 + n], start=True, stop=True),
                     reads=["hyf", ("hdn", li - 1)], writes=[("ps", 0)])
                E.op("vector", lambda e, li=li, t0=t0, n=n, bcol=bcol: e.tensor_scalar(out=hdn[li][:, t0:t0 + n], in0=self.ps[0][0:64, 0:n], scalar1=hv64[:, 2:3],
                                                                                      scalar2=hv64[:, bcol:bcol + 1], op0=ALU.mult, op1=ALU.add),
                     reads=[("ps", 0), "hv"], writes=[("hdnA", li)])
            hl = hdn[li]
            E.op("vector", lambda e, hl=hl: e.tensor_scalar(out=hl[:], in0=hl[:], scalar1=8.0 * math.pi, scalar2=None, op0=ALU.add),
                 reads=[("hdnA", li)], writes=[("hdnB", li)])
            E.op("vector", lambda e, hl=hl: e.tensor_scalar(out=rr_f[:], in0=hl[:], scalar1=1.0 / (2.0 * math.pi), scalar2=None, op0=ALU.mult),
                 reads=[("hdnB", li)], writes=["rr_f"])
            E.op("vector", lambda e: e.tensor_copy(out=rr_i[:], in_=rr_f[:]), reads=["rr_f"], writes=["rr_i"])
            E.op("vector", lambda e: e.tensor_copy(out=rr_f[:], in_=rr_i[:]), reads=["rr_i"], writes=["rr_f"])
            E.op("vector", lambda e, hl=hl: e.scalar_tensor_tensor(out=hl[:], in0=rr_f[:], scalar=-2.0 * math.pi, in1=hl[:], op0=ALU.mult, op1=ALU.add),
                 reads=["rr_f", ("hdnB", li)], writes=[("hdnC", li)])
            E.op("vector", lambda e, hl=hl: e.tensor_scalar(out=rr_f[:], in0=hl[:], scalar1=math.pi, scalar2=2.0 * math.pi, op0=ALU.is_gt, op1=ALU.mult),
                 reads=[("hdnC", li)], writes=["rr_f"])
            E.op("vector", lambda e, hl=hl: e.tensor_tensor(out=hl[:], in0=hl[:], in1=rr_f[:], op=ALU.subtract),
                 reads=["rr_f", ("hdnC", li)], writes=[("hdnD", li)])
            E.op("scalar", lambda e, hl=hl: e.activation(out=hl[:], in_=hl[:], func=AF.Sin),
                 reads=[("hdnD", li)], writes=[("hdn", li)])
        for sc in range(nsc):
            E.op("scalar", lambda e, sc=sc: e.activation(out=win[:], in_=delta[:], func=AF.Exp, scale=tneg[:, sc:sc + 1]), reads=["hyf"], writes=["win"])
            b = sc % 2
            for o in range(2):
                for dr in range(2):
                    for hf in range(2):
                        cb = o * 2048 + dr * 1024 + hf * 512
                        pb = (dr * 2 + hf) % 4
                        E.op("tensor", lambda e, sc=sc, cb=cb, pb=pb: e.matmul(self.ps[pb][:, 0:512], lhsT=hdn[1][:, sc * 128:(sc + 1) * 128], rhs=wfilt[:, cb:cb + 512], start=True, stop=True),
                             reads=[("hdn", 1), "hyf"], writes=[("ps", pb)])
                        E.op("vector", lambda e, o=o, dr=dr, hf=hf, pb=pb: e.tensor_tensor(out=hw[o][:, dr, hf * 512:(hf + 1) * 512], in0=self.ps[pb][:, 0:512],
                                                                                           in1=win[:, hf * 512:(hf + 1) * 512], op=ALU.mult),
                             reads=[("ps", pb), "win"], writes=[("hw", o, dr)])
                if sc == 0:
                    E.op("vector", lambda e, o=o: e.memset(hw[o][0:1, 1, :], 0.0), reads=[("hw", o, 1)], writes=[("hw", o, 1)])
                E.op("scalar", lambda e, o=o: e.activation(out=sqt[:].rearrange("p a d -> p (a d)"), in_=hw[o][:].rearrange("p a d -> p (a d)"), func=AF.Square),
                     reads=[("hw", o, 0), ("hw", o, 1)], writes=["sqt"])
                for hf in range(2):
                    for dr in range(2):
                        E.op("tensor", lambda e, o=o, hf=hf, dr=dr, sc=sc: e.matmul(self.ps[4 + o * 2 + hf][:, 0:512], lhsT=ones32[:], rhs=sqt[:, dr, hf * 512:(hf + 1) * 512],
                                                                                   start=(sc == 0 and dr == 0), stop=(sc == nsc - 1 and dr == 1)),
                             reads=["sqt", "ones32"], writes=[("psn", o, hf)])
                E.op("gpsimd", lambda e, o=o, b=b: e.tensor_tensor(out=hsd[b][:, 0, o * D:(o + 1) * D], in0=hw[o][:, 0, :], in1=hw[o][:, 1, :], op=ALU.add),
                     reads=[("hw", o, 0), ("hw", o, 1)], writes=[("hsd", b, o)])
                E.op("gpsimd", lambda e, o=o, b=b: e.tensor_tensor(out=hsd[b][:, 1, o * D:(o + 1) * D], in0=hw[o][:, 0, :], in1=hw[o][:, 1, :], op=ALU.subtract),
                     reads=[("hw", o, 0), ("hw", o, 1)], writes=[("hsd", b, o)])
            E.dma("gpsimd", [lambda e, sc=sc, b=b: e.dma_start(out=D_["HSD"][sc], in_=hsd[b][:])], ("hsds", b), reads=[("hsd", b, 0), ("hsd", b, 1)], writes=[("HSDd", sc)])
        for o in range(2):
            for hf in range(2):
                sl_ = nrm[:, o * D + hf * 512: o * D + (hf + 1) * 512]
                E.op("scalar", lambda e, o=o, hf=hf, sl_=sl_: e.activation(out=sl_, in_=self.ps[4 + o * 2 + hf][:, 0:512], func=AF.Ln, bias=self.eps1[:, 0:1], scale=1.0),
                     reads=[("psn", o, hf)], writes=[("nrm", o, hf)])
                E.op("scalar", lambda e, sl_=sl_: e.activation(out=sl_, in_=sl_, func=AF.Exp, scale=-0.5), reads=[("nrm", o, hf)], writes=[("nrm", o, hf)])
        E.phase()
        nb2 = L // 128
        hs = E.tmp("hs", [128, nsc, 512], BF16)
        hd = E.tmp("hd", [128, nsc, 512], BF16)
        ftb = [E.tmp("ftb%d" % i, [128, 2, nsc, 128], BF16) for i in range(2)]
        fbb = E.tmp("fbb", [128, 2 * D], F32)
        kro = [E.tmp("kro%d" % i, [128, 2, 512], F32) for i in range(2)]
        E.dma("sync", [lambda e: e.dma_start(out=fbb[:], in_=self.hyb_bc)], "fbb", writes=["fbb"])
        for cb in range(4):
            o, hf = cb // 2, cb % 2
            E.dma("sync", [lambda e, cb=cb, q0=q0, w_=w_, dst_=dst_: e.dma_start(out=dst_[:, q0:q0 + 8, :], in_=D_["HSD"][q0:q0 + 8, :, w_, cb * 512:(cb + 1) * 512].rearrange("c p d -> p c d"))
                           for q0 in range(0, nsc, 8) for (w_, dst_) in ((0, hs), (1, hd))][:None] if nsc >= 8 else
                  [lambda e, cb=cb: e.dma_start(out=hs[:], in_=D_["HSD"][:, :, 0, cb * 512:(cb + 1) * 512].rearrange("c p d -> p c d")),
                   lambda e, cb=cb: e.dma_start(out=hd[:], in_=D_["HSD"][:, :, 1, cb * 512:(cb + 1) * 512].rearrange("c p d -> p c d"))],
                  "hshd", writes=["hshd"])
            nsl = nrm[:, cb * 512:(cb + 1) * 512]
            fsl = fbb[:, cb * 512:(cb + 1) * 512]
            for j in range(nb2):
                b = j % 2
                E.dma("scalar", [lambda e, j=j, b=b: e.dma_start(out=ftb[b][:, 0], in_=D_["FT"][j])], ("ftbA", b), writes=[("ftb", b, 0)])
                E.dma("gpsimd", [lambda e, j=j, b=b: e.dma_start(out=ftb[b][:, 1], in_=D_["FT"][nb2 + j])], ("ftbB", b), writes=[("ftb", b, 1)])
                pr, pi = (0, 1) if b == 0 else (2, 3)
                for sc in range(nsc):
                    E.op("tensor", lambda e, sc=sc, b=b, pr=pr: e.matmul(self.ps[pr][:, 0:512], lhsT=ftb[b][:, 0, sc, :], rhs=hs[:, sc, :], start=(sc == 0), stop=(sc == nsc - 1)),
                         reads=[("ftb", b, 0), "hshd"], writes=[("ps", pr)])
                for sc in range(nsc):
                    E.op("tensor", lambda e, sc=sc, b=b, pi=pi: e.matmul(self.ps[pi][:, 0:512], lhsT=ftb[b][:, 1, sc, :], rhs=hd[:, sc, :], start=(sc == 0), stop=(sc == nsc - 1)),
                         reads=[("ftb", b, 1), "hshd"], writes=[("ps", pi)])
                if j == 0:
                    for sc in range(nsc):
                        E.op("tensor", lambda e, sc=sc, b=b: e.matmul(self.ps[6][0:1, 0:512], lhsT=ftb[b][:, 1, sc, 0:1], rhs=hs[:, sc, :], start=(sc == 0), stop=(sc == nsc - 1)),
                             reads=[("ftb", b, 1), "hshd"], writes=[("ps", 6)])
                E.op("vector", lambda e, b=b, pr=pr, nsl=nsl: e.tensor_tensor(out=kro[b][:, 0, :], in0=self.ps[pr][:, 0:512], in1=nsl, op=ALU.mult),
                     reads=[("ps", pr), ("nrm", o, hf)], writes=[("kro", b, 0)])
                E.op("gpsimd", lambda e, b=b, fsl=fsl: e.tensor_tensor(out=kro[b][:, 0, :], in0=kro[b][:, 0, :], in1=fsl, op=ALU.add),
                     reads=[("kro", b, 0), "fbb"], writes=[("kro", b, 0)])
                E.op("vector", lambda e, b=b, pi=pi, nsl=nsl: e.tensor_tensor(out=kro[b][:, 1, :], in0=self.ps[pi][:, 0:512], in1=nsl, op=ALU.mult),
                     reads=[("ps", pi), ("nrm", o, hf)], writes=[("kro", b, 1)])
                if j == 0:
                    E.op("vector", lambda e, b=b, nsl=nsl: e.tensor_tensor(out=kro[b][0:1, 1, :], in0=self.ps[6][0:1, 0:512], in1=nsl[0:1, :], op=ALU.mult),
                         reads=[("ps", 6), ("kro", b, 1)], writes=[("kro", b, 1)])
                    E.op("vector", lambda e, b=b, fsl=fsl: e.tensor_tensor(out=kro[b][0:1, 1, :], in0=kro[b][0:1, 1, :], in1=fsl[0:1, :], op=ALU.add),
                         reads=[("kro", b, 1), "fbb"], writes=[("kro", b, 1)])
                E.dma("gpsimd", [lambda e, b=b, j=j, o=o, hf=hf: e.dma_start(out=D_["KS"][o, :, j * 128:(j + 1) * 128, hf * 512:(hf + 1) * 512].rearrange("a p d -> p a d"), in_=kro[b][:])],
                      ("kros", b), reads=[("kro", b, 0), ("kro", b, 1)], writes=[("KSd", cb, j)])
        E.barrier()
        E.sb_base = saved_base

    def _hy_proj(self, layer, L, col0, cond, nm):
        E = self.E
        _ = (self.hy_w_in,)
        D_ = self.get_hyd(nm)
        nsc = L // 128
        E.phase()
        u_all = E.tmp("u_all", [128, 8, L], BF16)
        hb = [E.tmp("hb%d" % i, [128, 8, 512], F32) for i in range(2)]
        sq = E.tmp("sq", [128, 8, 512], BF16)
        rstd = E.tmp("rstd", [128, 512], F32)
        tmpn = [E.tmp("tmpn%d" % i, [128, 512], F32) for i in range(2)]
        wg = [E.tmp("wg%d" % i, [128, 8, 512], BF16) for i in range(2)]
        ppad = [E.tmp("ppad%d" % i, [128, L + 2], F32) for i in range(1)] * 2
        acc = E.tmp("acc", [128, L], F32)
        pcb = [E.tmp("pcb%d" % i, [128, L], BF16) for i in range(1)] * 2
        tokm = [E.tmp("tokm%d" % i, [128, nsc, 128], BF16) for i in range(2)]
        tiles = [(col0 + t0, min(512, L - t0), cond) for t0 in range(0, L, 512)]
        self.load_norm_tiles(tiles, layer, 1, lambda ti, c: u_all[:, c, ti * 512:ti * 512 + tiles[ti][1]], hb, sq, rstd, tmpn, "yh")
        ukeys = [("yho", ti) for ti in range(len(tiles))]
        for i in range(2):
            E.op("gpsimd", lambda e, i=i: e.memset(ppad[i][:, 0:1], 0.0), writes=[("ppz", i)])
            E.op("gpsimd", lambda e, i=i: e.memset(ppad[i][:, L + 1:L + 2], 0.0), writes=[("ppz", i)])
        wv = self.hy_w_in.rearrange("(kc p) f -> p kc f", p=128)
        bio, _ = VEC_LAYOUT["hy_b_in"]
        wso, _ = VEC_LAYOUT["hy_w_short"]
        bso, _ = VEC_LAYOUT["hy_b_short"]

        def loadw(g):
            b = g % 2
            E.dma("gpsimd", [lambda e, b=b, g=g: e.dma_start(out=wg[b][:], in_=wv[:, :, g * 512:(g + 1) * 512])], ("ywg", b), writes=[("ywg", b)])

        loadw(0)
        cnt = 0
        for g in range(6):
            b = g % 2
            if g + 1 < 6:
                loadw(g + 1)
            for f4 in range(4):
                fc = g * 4 + f4
                pb_ = 0
                pp = ppad[pb_]
                for ti, (c0, n, _) in enumerate(tiles):
                    pa = cnt % 4
                    cnt += 1
                    for kc in range(8):
                        E.op("tensor", lambda e, kc=kc, f4=f4, ti=ti, n=n, pa=pa, b=b: e.matmul(self.ps[pa][:, 0:n], lhsT=wg[b][:, kc, f4 * 128:(f4 + 1) * 128],
                                                                                               rhs=u_all[:, kc, ti * 512:ti * 512 + n], start=(kc == 0), stop=(kc == 7)),
                             reads=[("ywg", b), ukeys[ti]], writes=[("ps", pa)])
                    E.op("vector", lambda e, ti=ti, n=n, pa=pa, pp=pp, fc=fc: e.tensor_scalar(out=pp[:, 1 + ti * 512:1 + ti * 512 + n], in0=self.ps[pa][:, 0:n],
                                                                                            scalar1=self.V[:, bio + fc:bio + fc + 1], scalar2=None, op0=ALU.add),
                         reads=[("ps", pa), "vecs", ("ppz", pb_)], writes=[("pp", pb_)])
                w0 = self.V[:, wso + fc:wso + fc + 1]
                w1 = self.V[:, wso + 24 + fc:wso + 24 + fc + 1]
                w2 = self.V[:, wso + 48 + fc:wso + 48 + fc + 1]
                bs_ = self.V[:, bso + fc:bso + fc + 1]
                E.op("vector", lambda e, pp=pp, w1=w1, bs_=bs_: e.tensor_scalar(out=acc[:], in0=pp[:, 1:L + 1], scalar1=w1, scalar2=bs_, op0=ALU.mult, op1=ALU.add),
                     reads=[("pp", pb_), "vecs"], writes=["acc"])
                E.op("vector", lambda e, pp=pp, w0=w0: e.scalar_tensor_tensor(out=acc[:], in0=pp[:, 0:L], scalar=w0, in1=acc[:], op0=ALU.mult, op1=ALU.add),
                     reads=[("pp", pb_), "acc"], writes=["acc"])
                E.op("vector", lambda e, pp=pp, w2=w2, pb_=pb_: e.scalar_tensor_tensor(out=pcb[pb_][:], in0=pp[:, 2:L + 2], scalar=w2, in1=acc[:], op0=ALU.mult, op1=ALU.add),
                     reads=[("pp", pb_), "acc"], writes=[("pcb", pb_)])
                tk = fc % 2
                for q in range(0, nsc, 4):
                    nq = min(4, nsc - q)
                    pa = 4 + (q // 4) % 2
                    for i in range(nq):
                        E.op("tensor", lambda e, q=q, i=i, pa=pa, pb_=pb_: e.matmul(self.ps[pa][:, i * 128:(i + 1) * 128], lhsT=pcb[pb_][:, (q + i) * 128:(q + i + 1) * 128], rhs=self.ident[:], start=True, stop=True),
                             reads=[("pcb", pb_)], writes=[("ps", pa)])
                    E.op("scalar", lambda e, q=q, nq=nq, pa=pa, tk=tk: e.activation(out=tokm[tk][:, q:q + nq, :].rearrange("p a b -> p (a b)"), in_=self.ps[pa][:, 0:nq * 128], func=AF.Copy),
                         reads=[("ps", pa)], writes=[("tokm", tk)])
                E.dma("scalar", [lambda e, fc=fc, tk=tk: e.dma_start(out=D_["VX"][fc // 8, fc % 8], in_=tokm[tk][:])], ("tokms", tk), reads=[("tokm", tk)], writes=[("VXd", fc)])

    def _hy_conv(self, L, nm):
        E = self.E
        D_ = self.get_hyd(nm)
        nsc = L // 128
        nb2 = L // 128
        nrc = 2 * L // 128
        E.phase()
        zin = E.tmp("zin", [128, nsc, 512], BF16)
        Y = E.tmp("Y", [128, nrc, 512], BF16)
        ftb = [E.tmp("ftb%d" % i, [128, 2, nsc, 128], BF16) for i in range(2)]
        gtb = [E.tmp("gtb%d" % i, [128, nrc, 128], BF16) for i in range(2)]
        ks = [E.tmp("ks%d" % i, [128, 2, 512], F32) for i in range(2)]
        tt = [E.tmp("tt%d" % i, [128, 512], F32) for i in range(4)]
        xt = [E.tmp("xt%d" % i, [128, 512], BF16) for i in range(2)]
        z2 = [E.tmp("z2%d" % i, [128, 512], BF16) for i in range(2)]
        zT = [E.tmp("zT%d" % i, [128, 4, 128], BF16) for i in range(2)]
        ZTv = D_["ZT"].rearrange("(c p) t -> p c t", p=128)
        gcnt = {"ft": 0, "gt": 0, "x": 0}
        for dblk in range(2):
            qs = 8 if nsc >= 8 else nsc
            E.dma("sync", [lambda e, i=i, dblk=dblk, q0=q0: e.dma_start(out=zin[:, q0:q0 + qs, i * 128:(i + 1) * 128], in_=D_["VX"][0, dblk * 4 + i, :, q0:q0 + qs, :])
                           for i in range(4) for q0 in range(0, nsc, qs)],
                  "zin", writes=["zin"] + [("zin1", sc) for sc in range(nsc)])
            for o in range(2):
                zkey = (lambda sc: "zin") if o == 0 else (lambda sc: ("zin1", sc))
                for j in range(nb2):
                    b = gcnt["ft"] % 2
                    gcnt["ft"] += 1
                    E.dma("scalar", [lambda e, j=j, b=b: e.dma_start(out=ftb[b][:, 0], in_=D_["FT"][j])], ("ftbA", b), writes=[("ftb", b, 0)])
                    E.dma("gpsimd", [lambda e, j=j, b=b: e.dma_start(out=ftb[b][:, 1], in_=D_["FT"][nb2 + j])], ("ftbB", b), writes=[("ftb", b, 1)])
                    E.dma("sync", [lambda e, j=j, b=b, o=o, dblk=dblk: e.dma_start(out=ks[b][:], in_=D_["KS"][o, :, j * 128:(j + 1) * 128, dblk * 512:(dblk + 1) * 512].rearrange("a p d -> p a d"))],
                          ("ks", b), writes=[("ks", b)])
                    pr, pi = (0, 1) if b == 0 else (2, 3)
                    for sc in range(nsc):
                        E.op("tensor", lambda e, sc=sc, b=b, pr=pr: e.matmul(self.ps[pr][:, 0:512], lhsT=ftb[b][:, 0, sc, :], rhs=zin[:, sc, :], start=(sc == 0), stop=(sc == nsc - 1)),
                             reads=[("ftb", b, 0), zkey(sc)], writes=[("ps", pr)])
                    for sc in range(nsc):
                        E.op("tensor", lambda e, sc=sc, b=b, pi=pi: e.matmul(self.ps[pi][:, 0:512], lhsT=ftb[b][:, 1, sc, :], rhs=zin[:, sc, :], start=(sc == 0), stop=(sc == nsc - 1)),
                             reads=[("ftb", b, 1), zkey(sc)], writes=[("ps", pi)])
                    KR, KI = ks[b][:, 0, :], ks[b][:, 1, :]
                    E.op("vector", lambda e, pr=pr, KR=KR: e.tensor_tensor(out=tt[0][:], in0=self.ps[pr][:, 0:512], in1=KR, op=ALU.mult), reads=[("ps", pr), ("ks", b)], writes=[("tt", 0)])
                    E.op("vector", lambda e, pi=pi, KI=KI: e.tensor_tensor(out=tt[1][:], in0=self.ps[pi][:, 0:512], in1=KI, op=ALU.mult), reads=[("ps", pi), ("ks", b)], writes=[("tt", 1)])
                    E.op("gpsimd", lambda e, j=j: e.tensor_tensor(out=Y[:, j, :], in0=tt[0][:], in1=tt[1][:], op=ALU.subtract), reads=[("tt", 0), ("tt", 1)], writes=[("Y", j)])
                    E.op("vector", lambda e, pr=pr, KI=KI: e.tensor_tensor(out=tt[2][:], in0=self.ps[pr][:, 0:512], in1=KI, op=ALU.mult), reads=[("ps", pr), ("ks", b)], writes=[("tt", 2)])
                    E.op("vector", lambda e, pi=pi, KR=KR: e.tensor_tensor(out=tt[3][:], in0=self.ps[pi][:, 0:512], in1=KR, op=ALU.mult), reads=[("ps", pi), ("ks", b)], writes=[("tt", 3)])
                    E.op("gpsimd", lambda e, j=j: e.tensor_tensor(out=Y[:, nb2 + j, :], in0=tt[2][:], in1=tt[3][:], op=ALU.add), reads=[("tt", 2), ("tt", 3)], writes=[("Y", nb2 + j)])
                    if j == 0:
                        E.op("gpsimd", lambda e: e.tensor_copy(out=Y[0:1, 0, :], in_=tt[0][0:1, :]), reads=[("tt", 0), ("Y", 0)], writes=[("Y", 0)])
                        E.op("gpsimd", lambda e: e.tensor_copy(out=Y[0:1, nb2, :], in_=tt[1][0:1, :]), reads=[("tt", 1), ("Y", nb2)], writes=[("Y", nb2)])
                for tb in range(nsc):
                    b = gcnt["gt"] % 2
                    gcnt["gt"] += 1
                    xb = gcnt["x"] % 2
                    gcnt["x"] += 1
                    hr = nrc // 2
                    E.dma("scalar", [lambda e, tb=tb, b=b: e.dma_start(out=gtb[b][:, 0:hr, :], in_=D_["GT"][tb, :, 0:hr, :])], ("gtbA", b), writes=[("gtb", b, 0)])
                    E.dma("gpsimd", [lambda e, tb=tb, b=b: e.dma_start(out=gtb[b][:, hr:nrc, :], in_=D_["GT"][tb, :, hr:nrc, :])], ("gtbB", b), writes=[("gtb", b, 1)])
                    E.dma("sync", [lambda e, i=i, tb=tb, xb=xb, o=o, dblk=dblk: e.dma_start(out=xt[xb][:, i * 128:(i + 1) * 128], in_=D_["VX"][1 + o, dblk * 4 + i, :, tb, :]) for i in range(4)],
                          ("xt", xb), writes=[("xt", xb)])
                    po = 4 + b
                    for rc in range(nrc):
                        E.op("tensor", lambda e, rc=rc, b=b, po=po: e.matmul(self.ps[po][:, 0:512], lhsT=gtb[b][:, rc, :], rhs=Y[:, rc, :], start=(rc == 0), stop=(rc == nrc - 1)),
                             reads=[("gtb", b, 0 if rc < nrc // 2 else 1), ("Y", rc)], writes=[("ps", po)])
                    if o == 0:
                        E.op("vector", lambda e, tb=tb, po=po, xb=xb: e.tensor_tensor(out=zin[:, tb, :], in0=self.ps[po][:, 0:512], in1=xt[xb][:], op=ALU.mult),
                             reads=[("ps", po), ("xt", xb)], writes=[("zin1", tb)])
                    else:
                        E.op("vector", lambda e, po=po, xb=xb: e.tensor_tensor(out=z2[xb][:], in0=self.ps[po][:, 0:512], in1=xt[xb][:], op=ALU.mult),
                             reads=[("ps", po), ("xt", xb)], writes=[("z2", xb)])
                        for i in range(4):
                            E.op("tensor", lambda e, i=i, xb=xb: e.matmul(self.ps[6 + xb][:, i * 128:(i + 1) * 128], lhsT=z2[xb][:, i * 128:(i + 1) * 128], rhs=self.ident[:], start=True, stop=True),
                                 reads=[("z2", xb)], writes=[("ps", 6 + xb)])
                        E.op("scalar", lambda e, xb=xb: e.activation(out=zT[xb][:].rearrange("p a b -> p (a b)"), in_=self.ps[6 + xb][:, 0:512], func=AF.Copy),
                             reads=[("ps", 6 + xb)], writes=[("zT", xb)])
                        E.dma("scalar", [lambda e, xb=xb, tb=tb, dblk=dblk: e.dma_start(out=ZTv[:, dblk * 4:(dblk + 1) * 4, tb * 128:(tb + 1) * 128], in_=zT[xb][:])],
                              ("zTs", xb), reads=[("zT", xb)], writes=[("ZTd", dblk, tb)])

    def _hy_out(self, layer, L, col0, cond, nm):
        E = self.E
        _ = (self.hy_w_out,)
        D_ = self.get_hyd(nm)
        E.phase()
        hv = self.H.rearrange("(c p) t -> p c t", p=128)
        ZTv = D_["ZT"].rearrange("(c p) t -> p c t", p=128)
        wo = E.tmp("hwo", [128, 8, D], BF16)
        hb = [E.tmp("hb%d" % i, [128, 8, 512], F32) for i in range(2)]
        zb = [E.tmp("zb%d" % i, [128, 8, 512], BF16) for i in range(2)]
        yv = E.tmp("yv", [128, 512], F32)
        E.dma("gpsimd", [lambda e: e.dma_start(out=wo[:], in_=self.hy_w_out.rearrange("(kc p) d -> p kc d", p=128))], "hwo", writes=["hwo"])
        boo, _ = VEC_LAYOUT["hy_b_out"]
        for ti, t0 in enumerate(range(0, L, 512)):
            n = min(512, L - t0)
            b = ti % 2
            E.dma("sync", [lambda e, b=b, t0=t0, n=n: e.dma_start(out=hb[b][:, :, 0:n], in_=hv[:, :, col0 + t0:col0 + t0 + n])], ("oh", b), writes=[("oh", b)])
            E.dma("sync", [lambda e, b=b, t0=t0, n=n: e.dma_start(out=zb[b][:, :, 0:n], in_=ZTv[:, :, t0:t0 + n])], ("oz", b), writes=[("oz", b)])
            for dc in range(8):
                pa = dc % 2
                for kc in range(8):
                    E.op("tensor", lambda e, dc=dc, kc=kc, b=b, n=n, pa=pa: e.matmul(self.ps[pa][:, 0:n], lhsT=wo[:, kc, dc * 128:(dc + 1) * 128], rhs=zb[b][:, kc, 0:n], start=(kc == 0), stop=(kc == 7)),
                         reads=["hwo", ("oz", b)], writes=[("ps", pa)])
                E.op("vector", lambda e, dc=dc, n=n, pa=pa: e.tensor_scalar(out=yv[:, 0:n], in0=self.ps[pa][:, 0:n], scalar1=self.V[:, boo + dc:boo + dc + 1], scalar2=None, op0=ALU.add),
                     reads=[("ps", pa), "vecs"], writes=["yv"])
                gate = self.AD[:, layer, 1, 2, dc, cond:cond + 1]
                E.op("vector", lambda e, dc=dc, n=n, b=b, gate=gate: e.scalar_tensor_tensor(out=hb[b][:, dc, 0:n], in0=yv[:, 0:n], scalar=gate, in1=hb[b][:, dc, 0:n], op0=ALU.mult, op1=ALU.add),
                     reads=["yv", ("oh", b), "AD"], writes=[("oho", b, dc)])
            E.dma("gpsimd", [lambda e, b=b, t0=t0, n=n: e.dma_start(out=hv[:, :, col0 + t0:col0 + t0 + n], in_=hb[b][:, :, 0:n])], ("ohs", b),
                  reads=[("oho", b, dc) for dc in range(8)] + [("oh", b)], writes=[("Ho", ti)])


VEC_LAYOUT = {}
NVEC = 0


def _vreg(name, shape):
    global NVEC
    VEC_LAYOUT[name] = (NVEC, tuple(shape))
    NVEC += int(np.prod(shape))


_vreg("cond", (8, 2))
_vreg("ada_b", (DEPTH, 72, 2))
_vreg("norm_g", (DEPTH, 3, 8))
_vreg("final_g", (8,))
_vreg("pool_b", (8,))
_vreg("pool_scale", (8,))
_vreg("ret_decay", (2, 8))
_vreg("hy_b_in", (24,))
_vreg("hy_w_short", (3, 24))
_vreg("hy_b_short", (24,))
_vreg("hy_b_out", (8,))
RETC_N = 2 * 128 + 2 * 128 + 2 * 128 + 2


def _pack_vecs(inp, b):
    V = np.zeros((128, NVEC), np.float32)

    def put(name, arr):
        off, shape = VEC_LAYOUT[name]
        n = int(np.prod(shape))
        V[:, off:off + n] = np.asarray(arr, np.float32).reshape(128, n)

    cond = np.stack([_fm(inp["c"][b]), _fm(inp["c_ctx"])], axis=-1)
    put("cond", cond)
    ab = _fm(inp["ada_b"])
    put("ada_b", np.repeat(ab[..., None], 2, axis=-1))
    put("norm_g", _fm(inp["norm_g"]))
    put("final_g", _fm(inp["final_g"]))
    put("pool_b", _fm(inp["pool_b"][0]))
    put("pool_scale", _fm(inp["pool_scale"][0]))
    put("hy_b_in", _fm(inp["hy_b_in"][0]))
    put("hy_w_short", _fm(inp["hy_w_short"][0]))
    put("hy_b_short", _fm(inp["hy_b_short"][0]))
    put("hy_b_out", _fm(inp["hy_b_out"][0]))
    put("ret_decay", np.broadcast_to(np.asarray(inp["ret_decay"], np.float32).reshape(1, 2, 8), (128, 2, 8)))
    return V


def _pool_rc():
    rc = np.zeros((4, 512 + 256), np.float32)
    for gi, win in enumerate((2, 4, 8, 16)):
        for w, off, reps in ((64, 0, 8), (256, 512, 1)):
            pos = np.arange(w)
            lo = np.clip(pos - win // 2, 0, w)
            hi = np.clip(pos - win // 2 + win, 0, w)
            r = (1.0 / (hi - lo)).astype(np.float32)
            rc[gi, off:off + w * reps] = np.tile(r, reps)
    return np.ascontiguousarray(np.broadcast_to(rc[None], (128, 4, 768)))


def _hy_consts(L):
    N = 2 * L
    nsc = L // 128
    s_ = np.arange(L, dtype=np.int64)
    r = np.arange(N, dtype=np.int64)
    f = np.where(r < L, r, r - L)
    ang = (2.0 * np.pi / N) * ((s_[:, None] * f[None, :]) % N)
    FT = np.where((r <= L)[None, :], np.cos(ang), -np.sin(ang))
    FT[:, L] = np.where(s_ % 2 == 0, 1.0, -1.0)
    cf = np.where((f == 0) | (r == L), 1.0, 2.0) / N
    GT = (FT * cf[None, :]).T
    FTt = FT.reshape(nsc, 128, 2 * nsc, 128).transpose(2, 1, 0, 3)
    GTt = GT.reshape(2 * nsc, 128, nsc, 128).transpose(2, 1, 0, 3)
    t = np.linspace(0.0, 1.0, L, dtype=np.float32)
    bands = np.linspace(1e-4, 15.0, 16, dtype=np.float32)
    a2 = (np.float32(2.0 * math.pi / L) * np.arange(L, dtype=np.float32)[:, None] * bands[None, :]).astype(np.float32)
    feat = np.concatenate([t[:, None], np.cos(a2), -np.sin(a2)], axis=-1).astype(np.float32)
    tneg = (-t).reshape(nsc, 128).T
    return dict(FT=np.ascontiguousarray(FTt).astype(ml_dtypes.bfloat16), GT=np.ascontiguousarray(GTt).astype(ml_dtypes.bfloat16),
                featT=np.ascontiguousarray(feat.T), tneg=np.ascontiguousarray(tneg, np.float32))


_CACHE = {}


def _get_prog(stop_after=None):
    key = stop_after
    if key not in _CACHE:
        p = Prog(stop_after)
        with contextlib.ExitStack() as st:
            p.build()
            p.E.emit(st)
        _CACHE[key] = p
    return _CACHE[key]


def kernel(stop_after=None, **inp):
    inp = {k: np.asarray(v) for k, v in inp.items()}
    p = _get_prog(stop_after)
    cosT, sinT = _rot_tables()
    rc = _ret_consts()
    retc = np.concatenate([rc["rel"][0], rc["rel"][1], rc["mask"][0], rc["mask"][1], rc["qexp"][0], rc["qexp"][1],
                           rc["kexp"].T], axis=1).astype(np.float32)
    shared = {
        "ada_w": np.ascontiguousarray(inp["ada_w"], np.float32),
        "ffn_w1": np.ascontiguousarray(inp["ffn_w1"], np.float32),
        "ffn_w3": np.ascontiguousarray(inp["ffn_w3"], np.float32),
        "ffn_w2": np.ascontiguousarray(inp["ffn_w2"], np.float32),
        "ret_w_in": np.ascontiguousarray(inp["ret_w_in"], np.float32),
        "ret_w_out": np.ascontiguousarray(inp["ret_w_out"], np.float32),
        "pool_w": np.ascontiguousarray(inp["pool_w"][0], np.float32),
        "rot": np.stack([cosT, sinT]),
        "retc": np.ascontiguousarray(retc),
        "poolrc": _pool_rc(),
        "identb": np.eye(128, dtype=np.float32).astype(ml_dtypes.bfloat16),
        "hy_w_in": np.ascontiguousarray(inp["hy_w_in"][0], np.float32),
        "hy_w_out": np.ascontiguousarray(inp["hy_w_out"][0], np.float32),
        "hy_w_pos": np.ascontiguousarray(inp["hy_w_pos"][0], np.float32),
        "hy_w_mid": np.ascontiguousarray(inp["hy_w_mid"][0], np.float32),
        "hy_w_filt": np.ascontiguousarray(inp["hy_w_filt"][0], np.float32),
    }
    hv = np.zeros((64, 8), np.float32)
    hv[:, 0] = inp["hy_b_pos"][0]
    hv[:, 1] = inp["hy_b_mid"][0]
    hv[:, 2] = inp["hy_freq"][0]
    shared["hyv64"] = hv
    min_decay = math.log(1e-2) / 1.5
    max_decay = math.log(1e-2) / 0.3
    shared["hy_delta"] = np.ascontiguousarray(np.broadcast_to(np.abs(np.linspace(min_decay, max_decay, D, dtype=np.float32))[None], (128, D)))
    shared["hyb_bc"] = np.ascontiguousarray(np.broadcast_to(np.asarray(inp["hy_bias"][0], np.float32).reshape(1, 2 * D), (128, 2 * D)))
    for nm, L in (("l", SEQ), ("c", CTX)):
        if ("FT_" + nm) not in p.inputs:
            continue
        if "hyc_" + nm not in _CACHE:
            _CACHE["hyc_" + nm] = _hy_consts(L)
        for k, v in _CACHE["hyc_" + nm].items():
            shared[k + "_" + nm] = v
    in_maps = []
    for b in range(NCORES):
        m = dict(shared)
        m["h0"] = np.ascontiguousarray(np.concatenate([inp["ctx"][b].T, inp["x"][b].T], axis=1), np.float32)
        m["vecs"] = _pack_vecs(inp, b)
        in_maps.append({k: v for k, v in m.items() if k in p.inputs})
    if TEST_CORES is not None:
        res = run_bass_kernel_spmd(p.nc, in_maps[:TEST_CORES], core_ids=list(range(TEST_CORES)))
        DEBUG_OUT.update({k: np.asarray(v) for k, v in res.results[0].items()})
        return None
    res = run_bass_kernel_spmd(p.nc, in_maps, core_ids=list(range(NCORES)))
    if DEBUG_DUMP:
        DEBUG_OUT.update({k: np.asarray(v) for k, v in res.results[0].items()})
    out = np.stack([np.ascontiguousarray(res.results[b]["outT"].T) for b in range(NCORES)])
    return out.astype(np.float32)
```

```python
import contextlib
import math
import numpy as np
import ml_dtypes
import concourse.bass as bass
import concourse.mybir as mybir
from concourse.bass_utils import run_bass_kernel_spmd

F32 = mybir.dt.float32
BF16 = mybir.dt.bfloat16
AF = mybir.ActivationFunctionType
ALU = mybir.AluOpType
AX = mybir.AxisListType

D = 1024
SEQ = 4096
CTX = 256
NT = SEQ + CTX
DEPTH = 4
FFN = 2816
NFC = FFN // 128
EPS = 1e-6
NCORES = 4
SPREAD = True
ACTIVE_CORES = (0, 1, 4, 5)
TEST_CORES = None
DEBUG_DUMP = False
DEBUG_OUT = {}
ENGS = ("tensor", "vector", "scalar", "gpsimd", "sync")

CTX_TILE = (0, CTX, 1)
LAT_TILES = [(CTX + 512 * i, 512, 0) for i in range(8)]


class _Op:
    __slots__ = ("eng", "fn", "waits", "tick", "needs_inc", "is_dma", "sem", "seq")


class Emitter:
    def __init__(self, nc):
        self.nc = nc
        self.ops = {e: [] for e in ENGS}
        self.last_w = {}
        self.readers = {}
        self.dma_sems = {}
        self.eng_sem = {}
        self.pending_dma = []
        self.cur_map = {}
        arena = nc.alloc_sbuf_tensor("arena", [128, 52000], F32)
        base = nc.lookup_mloc(arena).addr
        self.sb_base = base
        self.sb_top = base
        self.sb_cnt = 0
        self.sb_limit = base + 52000 * 4

    def _alloc(self, name, shape, dtype, off):
        self.sb_cnt += 1
        return self.nc.alloc_sbuf_tensor_at("%s_%d" % (name, self.sb_cnt), list(shape), dtype, offset=off)

    @staticmethod
    def _bytes(shape, dtype):
        n = 1
        for s in shape[1:]:
            n *= s
        return n * (2 if dtype == BF16 else 4)

    def persist(self, name, shape, dtype):
        assert self.sb_top == self.sb_base
        off = self.sb_base
        self.sb_base += (self._bytes(shape, dtype) + 63) // 64 * 64
        self.sb_top = self.sb_base
        assert self.sb_base <= self.sb_limit
        return self._alloc(name, shape, dtype, off)

    def tmp(self, name, shape, dtype):
        off = self.sb_top
        self.sb_top += (self._bytes(shape, dtype) + 63) // 64 * 64
        assert self.sb_top <= self.sb_limit, (name, self.sb_top)
        return self._alloc(name, shape, dtype, off)

    def phase(self):
        self.barrier()
        self.sb_top = self.sb_base

    def _deps(self, op, reads, writes):
        deps = []
        for k in reads:
            w = self.last_w.get(k)
            if w is not None:
                deps.append(w)
        for k in writes:
            w = self.last_w.get(k)
            if w is not None:
                deps.append(w)
            deps.extend(self.readers.get(k, ()))
        self._add_waits(op, deps)
        for k in reads:
            self.readers.setdefault(k, []).append(op)
        for k in writes:
            self.last_w[k] = op
            self.readers[k] = []

    def _add_waits(self, op, deps):
        seen = set(id(d) for d in op.waits)
        latest = {}
        rest = []
        for d in deps:
            if d.is_dma:
                rest.append(d)
            elif d.eng not in latest or d.seq > latest[d.eng].seq:
                latest[d.eng] = d
        deps = rest + list(latest.values())
        for d in deps:
            if d is op or id(d) in seen:
                continue
            seen.add(id(d))
            if d.eng == op.eng and d.eng == "tensor" and not d.is_dma and not op.is_dma:
                continue
            if not d.is_dma:
                d.needs_inc = True
            op.waits.append(d)

    def op(self, eng, fn, reads=(), writes=()):
        o = _Op()
        o.eng = eng; o.fn = fn; o.waits = []; o.tick = None; o.needs_inc = False
        o.is_dma = False; o.sem = None
        o.seq = len(self.ops[eng])
        self._deps(o, reads, writes)
        self.ops[eng].append(o)
        return o

    def dma(self, eng, fns, semkey, reads=(), writes=()):
        o = _Op()
        o.eng = eng; o.fn = fns; o.waits = []; o.needs_inc = False
        o.is_dma = True
        pool = "g" if eng == "gpsimd" else "h"
        pm = self.cur_map.setdefault(pool, {})
        slot = (pool, pm.setdefault(semkey, len(pm)))
        ent = self.dma_sems.setdefault(slot, [None, 0])
        ent[1] += 16 * len(fns)
        o.sem = slot; o.tick = ent[1]
        o.seq = len(self.ops[eng])
        self._deps(o, reads, writes)
        self.ops[eng].append(o)
        self.pending_dma.append(o)
        return o

    def barrier(self):
        lasts = []
        for e in ENGS:
            for o in reversed(self.ops[e]):
                if not o.is_dma:
                    if o.fn is not None:
                        lasts.append(o)
                    break
        deps = lasts + self.pending_dma
        self.pending_dma = []
        self.cur_map = {}
        for e in ENGS:
            o = _Op()
            o.eng = e; o.fn = None; o.waits = []; o.tick = None; o.needs_inc = False
            o.is_dma = False; o.sem = None
            o.seq = len(self.ops[e])
            self._add_waits(o, deps)
            self.ops[e].append(o)
        self.last_w = {}
        self.readers = {}

    def emit(self, stack):
        nc = self.nc
        for e in ENGS:
            self.eng_sem[e] = stack.enter_context(nc.semaphore("s_" + e))
        for i, (k, ent) in enumerate(self.dma_sems.items()):
            ent[0] = stack.enter_context(nc.semaphore("d_%d" % i))
        for e in ENGS:
            t = 0
            for o in self.ops[e]:
                if not o.is_dma and o.needs_inc:
                    assert o.fn is not None
                    t += 1
                    o.tick = t
        block = stack.enter_context(nc.Block())
        em = self

        def run(engname):
            def body(eng):
                waited = {}
                for o in em.ops[engname]:
                    for d in o.waits:
                        if d.is_dma:
                            sem = em.dma_sems[d.sem][0]; key = ("d", d.sem)
                        else:
                            sem = em.eng_sem[d.eng]; key = ("e", d.eng)
                        if waited.get(key, 0) >= d.tick:
                            continue
                        waited[key] = d.tick
                        eng.wait_ge(sem, d.tick)
                    if o.is_dma:
                        sem = em.dma_sems[o.sem][0]
                        for f in o.fn:
                            f(eng).then_inc(sem, 16)
                    elif o.fn is not None:
                        ins = o.fn(eng)
                        if o.needs_inc:
                            ins.then_inc(em.eng_sem[engname], 1)
            return body

        block.tensor(run("tensor"))
        block.vector(run("vector"))
        block.scalar(run("scalar"))
        block.gpsimd(run("gpsimd"))
        block.sync(run("sync"))


def _fm(vec):
    v = np.asarray(vec, np.float32)
    lead = v.shape[:-1]
    n = v.shape[-1] // 128
    v = v.reshape(*lead, n, 128)
    return np.ascontiguousarray(np.moveaxis(v, -1, 0))


def _rot_tables():
    half = 128
    inv = 1.0 / (10000.0 ** np.linspace(0.0, 1.0, half, dtype=np.float32).astype(np.float64))
    pos = np.arange(NT, dtype=np.float64)
    ang = (pos[None, :].astype(np.float32) * inv[:, None].astype(np.float32)).astype(np.float64)
    return np.cos(ang).astype(np.float32), np.sin(ang).astype(np.float32)


def _ret_consts():
    n = np.arange(128)
    m = np.arange(128)[:, None]
    nn = n[None, :]
    c = {}
    c["rel"] = np.stack([np.maximum(nn - m, 0), np.maximum(m - nn, 0)]).astype(np.float32)
    c["mask"] = np.stack([(nn >= m), (m > nn)]).astype(np.float32) / 16.0
    c["qexp"] = np.stack([np.broadcast_to(n + 1.0, (128, 128)), np.broadcast_to(128.0 - n, (128, 128))]).astype(np.float32)
    c["kexp"] = np.stack([127.0 - n, n * 1.0]).astype(np.float32)
    return c


class Prog:
    def __init__(self, stop_after=None):
        self.stop_after = stop_after
        nc = self.nc = bass.Bass("TRN2", target_bir_lowering=False)
        self.E = Emitter(nc)
        self.inputs = {}
        self.stopped = False

    _SPECS = {
        "h0": ([D, NT], F32), "vecs": ([128, None], F32), "ada_w": ([DEPTH, D, 9 * D], F32),
        "ffn_w1": ([DEPTH, 2, D, FFN], F32), "ffn_w3": ([DEPTH, 2, D, FFN], F32), "ffn_w2": ([DEPTH, 2, FFN, D], F32),
        "ret_w_in": ([2, D, 6144], F32), "ret_w_out": ([2, 2048, D], F32), "pool_w": ([4, 256, 256], F32),
        "rot": ([2, 128, NT], F32), "retc": ([128, None], F32), "poolrc": ([128, 4, 768], F32), "identb": ([128, 128], BF16),
        "hy_w_in": ([D, 3 * D], F32), "hy_w_out": ([D, D], F32), "hy_w_pos": ([33, 64], F32), "hy_w_mid": ([64, 64], F32),
        "hy_w_filt": ([64, 4 * D], F32), "hyv64": ([64, 8], F32), "hy_delta": ([128, D], F32), "hyb_bc": ([128, 2 * D], F32),
    }

    def __getattr__(self, name):
        specs = type(self)._SPECS
        if name in specs:
            shape, dt = specs[name]
            shape = [NVEC if (x is None and name == "vecs") else (RETC_N if x is None else x) for x in shape]
            t = self.din(name, shape, dt)
            self.__dict__[name] = t
            return t
        raise AttributeError(name)

    def get_hyd(self, nm):
        if nm not in self.hyd:
            nc = self.nc
            L = SEQ if nm == "l" else CTX
            nsc = L // 128
            dk = "ExternalOutput" if DEBUG_DUMP else "Internal"
            self.hyd[nm] = dict(
                featT=self.din("featT_" + nm, [33, L]),
                tneg=self.din("tneg_" + nm, [128, nsc]),
                FT=self.din("FT_" + nm, [2 * nsc, 128, nsc, 128], BF16),
                GT=self.din("GT_" + nm, [nsc, 128, 2 * nsc, 128], BF16),
                HSD=nc.dram_tensor("HSD_" + nm, [nsc, 128, 2, 2 * D], BF16, kind=dk).ap(),
                KS=nc.dram_tensor("KS_" + nm, [2, 2, L, D], F32, kind=dk).ap(),
                VX=nc.dram_tensor("VX_" + nm, [3, 8, 128, nsc, 128], BF16, kind=dk).ap(),
                ZT=nc.dram_tensor("ZT_" + nm, [D, L], BF16, kind=dk).ap(),
            )
        return self.hyd[nm]

    def din(self, name, shape, dtype=F32):
        t = self.nc.dram_tensor(name, list(shape), dtype, kind="ExternalInput").ap()
        self.inputs[name] = t
        return t

    def build(self):
        nc, E = self.nc, self.E
        _ = (self.h0, self.vecs, self.identb)
        self.hyd = {}
        self.outT = nc.dram_tensor("outT", [D, SEQ], F32, kind="ExternalOutput").ap()
        dk = "ExternalOutput" if DEBUG_DUMP else "Internal"
        self.H = nc.dram_tensor("Hres", [D, NT], F32, kind=dk).ap()
        self.QT = nc.dram_tensor("QT", [D, NT], BF16).ap()
        self.KT = nc.dram_tensor("KT", [D, NT], BF16).ap()
        self.VT = nc.dram_tensor("VT", [NT, 2048], BF16).ap()
        self.GT = nc.dram_tensor("GT", [NT, 2048], BF16).ap()
        self.OF = nc.dram_tensor("OF", [NT, 2048], F32).ap()

        self.ps = [nc.alloc_psum_tensor("ps%d" % i, [128, 512], F32) for i in range(8)]

        self.V = E.persist("vecs", [128, NVEC], F32)
        self.ones = E.persist("ones", [128, 128], BF16)
        self.ident = E.persist("ident", [128, 128], BF16)
        self.AD = E.persist("AD", [128, DEPTH, 3, 3, 8, 2], F32)
        self.finA = E.persist("finA", [128, 8], F32)
        self.epsD = E.persist("epsD", [128, 1], F32)
        self.eps1 = E.persist("eps1", [128, 1], F32)

        E.dma("sync", [lambda e: e.dma_start(out=self.V[:], in_=self.vecs)], "vecs", writes=["vecs"])
        E.dma("sync", [lambda e: e.dma_start(out=self.ident[:], in_=self.identb)], "ident", writes=["ident"])
        E.op("vector", lambda e: e.memset(self.ones[:], 1.0), writes=["ones"])
        E.op("vector", lambda e: e.memset(self.epsD[:], D * EPS), writes=["epsD"])
        E.op("vector", lambda e: e.memset(self.eps1[:], EPS), writes=["eps1"])
        E.op("vector", lambda e: e.tensor_scalar(out=self.finA[:], in0=self.vsl("final_g"), scalar1=32.0, scalar2=None, op0=ALU.mult),
             reads=["vecs"], writes=["finA"])
        E.barrier()

        self.mods_all()
        if self.stop_after is not None and self.stop_after.startswith("HY"):
            stage = int(self.stop_after[2])
            big = len(self.stop_after) > 3
            L_, col_, cond_, nm_ = (SEQ, CTX, 0, "l") if big else (CTX, 0, 1, "c")
            E.phase()
            tb_ = E.tmp("cp", [128, 8, 512], F32)
            for t0 in range(0, L_, 512):
                n = min(512, L_ - t0)
                E.dma("sync", [lambda e, t0=t0, n=n: e.dma_start(out=tb_[:, :, 0:n], in_=self.h0.rearrange("(c p) t -> p c t", p=128)[:, :, col_ + t0:col_ + t0 + n])], "cp", writes=["cp"])
                E.dma("sync", [lambda e, t0=t0, n=n: e.dma_start(out=self.H.rearrange("(c p) t -> p c t", p=128)[:, :, col_ + t0:col_ + t0 + n], in_=tb_[:, :, 0:n])], "cp2", reads=["cp"], writes=["Hcp"])
            self._hy_filter(L_, nm_)
            if stage >= 2:
                self._hy_proj(2, L_, col_, cond_, nm_)
            if stage >= 3:
                self._hy_conv(L_, nm_)
            if stage >= 4:
                self._hy_out(2, L_, col_, cond_, nm_)
            self.finish()
            return
        src = self.h0
        for layer in range(DEPTH):
            kind = layer % 3
            slot = layer // 3
            last = layer == DEPTH - 1
            ctx_out = not last
            tiles_a = [CTX_TILE] + LAT_TILES
            tiles_b = ([CTX_TILE] if ctx_out else []) + LAT_TILES
            self.ffn_half(layer, 0, tiles_a, src)
            src = self.H
            if self.check_stop("L%da" % layer):
                return
            if kind == 0:
                self.retention(layer, slot, ctx_out)
            elif kind == 1:
                self.pool(layer, slot, tiles_b)
            else:
                self.hyena(layer, slot, ctx_out)
            if self.check_stop("L%db" % layer):
                return
            self.ffn_half(layer, 1, tiles_b, self.H)
            if self.check_stop("L%dc" % layer):
                return
        self.final_norm()
        self.finish()

    def check_stop(self, tag):
        if self.stop_after == tag:
            self.dump_h()
            self.finish()
            self.stopped = True
            return True
        return False

    def finish(self):
        E = self.E
        E.barrier()

    def dump_h(self):
        E = self.E
        E.phase()
        bufs = [E.tmp("dump%d" % i, [128, 8, 512], F32) for i in range(2)]
        for i, (c0, n, cond) in enumerate(LAT_TILES):
            t = bufs[i % 2]
            E.dma("sync", [lambda e, t=t, c0=c0, n=n: e.dma_start(out=t[:], in_=self.H.rearrange("(c p) t -> p c t", p=128)[:, :, c0:c0 + n])],
                  ("dl", i % 2), writes=[("dump", i % 2)])
            E.dma("sync", [lambda e, t=t, c0=c0, n=n: e.dma_start(out=self.outT.rearrange("(c p) t -> p c t", p=128)[:, :, c0 - CTX:c0 - CTX + n], in_=t[:])],
                  ("ds", i % 2), reads=[("dump", i % 2)], writes=[("dumpo", i)])

    def vsl(self, name, *idx):
        off, shape = VEC_LAYOUT[name]
        n = int(np.prod(shape))
        ap = self.V[:, off:off + n]
        return ap

    def vcol(self, name, flat_index):
        off, shape = VEC_LAYOUT[name]
        return self.V[:, off + flat_index:off + flat_index + 1]

    def mods_all(self):
        E = self.E
        E.phase()
        sc = E.tmp("sc", [128, 8, 2], BF16)
        mods = E.tmp("mods", [128, 72, 2], F32)
        wb = [E.tmp("adaw%d" % i, [128, 8, 1152], BF16) for i in range(2)]
        E.op("scalar", lambda e: e.activation(out=sc[:].rearrange("p c k -> p (c k)"), in_=self.vsl("cond"), func=AF.Silu),
             reads=["vecs"], writes=["sc"])
        gi = 0
        for layer in range(DEPTH):
            if self.stop_after is not None and self.stop_after.startswith("HY") and layer != 2:
                continue
            for g in range(8):
                b = gi % 2
                gi += 1
                src = self.ada_w[layer].rearrange("(kc p) f -> p kc f", p=128)[:, :, g * 1152:(g + 1) * 1152]
                E.dma("gpsimd", [lambda e, b=b, src=src: e.dma_start(out=wb[b][:], in_=src)], ("adaw", b), writes=[("adaw", b)])
                for j in range(9):
                    oc = g * 9 + j
                    for kc in range(8):
                        E.op("tensor", lambda e, b=b, j=j, kc=kc, oc=oc: e.matmul(self.ps[0][:, oc * 2:oc * 2 + 2], lhsT=wb[b][:, kc, j * 128:(j + 1) * 128],
                                                                                 rhs=sc[:, kc, :], start=(kc == 0), stop=(kc == 7)),
                             reads=[("adaw", b), "sc"], writes=["psmods"])
            off, _ = VEC_LAYOUT["ada_b"]
            E.op("vector", lambda e, layer=layer, off=off: e.tensor_tensor(out=mods[:].rearrange("p c k -> p (c k)"), in0=self.ps[0][:, 0:144],
                                                                           in1=self.V[:, off + layer * 144: off + (layer + 1) * 144], op=ALU.add),
                 reads=["psmods", "vecs"], writes=["mods"])
            ngo, _ = VEC_LAYOUT["norm_g"]
            for n in range(3):
                shift = mods[:, (3 * n) * 8:(3 * n + 1) * 8, :]
                scale = mods[:, (3 * n + 1) * 8:(3 * n + 2) * 8, :]
                gate = mods[:, (3 * n + 2) * 8:(3 * n + 3) * 8, :]
                for cond in range(2):
                    gsl = self.V[:, ngo + (layer * 3 + n) * 8: ngo + (layer * 3 + n + 1) * 8]
                    E.op("vector", lambda e, layer=layer, n=n, cond=cond, scale=scale, gsl=gsl: e.scalar_tensor_tensor(
                        out=self.AD[:, layer, n, 0, :, cond], in0=scale[:, :, cond], scalar=1.0, in1=gsl, op0=ALU.add, op1=ALU.mult),
                        reads=["mods", "vecs"], writes=["AD"])
                E.op("vector", lambda e, layer=layer, n=n: e.tensor_scalar(out=self.AD[:, layer, n, 0, :, :], in0=self.AD[:, layer, n, 0, :, :],
                                                                           scalar1=32.0, scalar2=None, op0=ALU.mult), reads=["AD"], writes=["AD"])
                E.op("vector", lambda e, layer=layer, n=n, shift=shift: e.tensor_copy(out=self.AD[:, layer, n, 1, :, :], in_=shift), reads=["mods"], writes=["AD"])
                gmul = 1.0 if n == 1 else 0.5
                E.op("vector", lambda e, layer=layer, n=n, gate=gate, gmul=gmul: e.tensor_scalar(out=self.AD[:, layer, n, 2, :, :], in0=gate,
                                                                                                 scalar1=gmul, scalar2=None, op0=ALU.mult), reads=["mods"], writes=["AD"])
        E.barrier()

    def adaln(self, h, n, layer, nrm, cond, out, key_in, key_out, sq, rstd, tmp, psb, A=None, shift=None, idx=0, out3=None):
        E = self.E
        sqk = ("sq", idx % 2)
        for c in range(8):
            E.op("scalar", lambda e, c=c: e.activation(out=sq[:, c, 0:n], in_=h[:, c, 0:n], func=AF.Square), reads=[key_in], writes=[sqk])
        for c in range(8):
            E.op("tensor", lambda e, c=c: e.matmul(self.ps[psb][:, 0:n], lhsT=self.ones[:], rhs=sq[:, c, 0:n], start=(c == 0), stop=(c == 7)),
                 reads=[sqk], writes=[("ps", psb)])
        rk = ("rstd", idx % 2)
        E.op("scalar", lambda e: e.activation(out=rstd[:, 0:n], in_=self.ps[psb][:, 0:n], func=AF.Ln, bias=self.epsD[:, 0:1], scale=1.0),
             reads=[("ps", psb)], writes=[rk])
        E.op("scalar", lambda e: e.activation(out=rstd[:, 0:n], in_=rstd[:, 0:n], func=AF.Exp, scale=-0.5),
             reads=[rk], writes=[rk])
        for c in range(8):
            tk = ("tmpn", c % 2)
            tt = tmp[c % 2]
            Ac = A(c) if A is not None else self.AD[:, layer, nrm, 0, c, cond:cond + 1]
            E.op("vector", lambda e, c=c, tt=tt: e.tensor_tensor(out=tt[:, 0:n], in0=h[:, c, 0:n], in1=rstd[:, 0:n], op=ALU.mult),
                 reads=[key_in, rk], writes=[tk])
            if shift is None:
                sh = self.AD[:, layer, nrm, 1, c, cond:cond + 1]
            else:
                sh = shift(c)
            src_ap = tt[:, 0:n] if out3 is None else tt[:, 0:n].rearrange("p (r w) -> p r w", w=out3[1])
            E.op("scalar", lambda e, c=c, src_ap=src_ap, Ac=Ac, sh=sh: e.activation(out=out(c), in_=src_ap, func=AF.Identity, bias=sh, scale=Ac),
                 reads=[tk, "AD"], writes=[key_out])

    def ffn_half(self, layer, which, tiles, src):
        E = self.E
        nrm = 0 if which == 0 else 2
        sts = []
        cur, cnt = [], 0
        for t in tiles:
            if cnt + t[1] > 2304:
                sts.append(cur)
                cur, cnt = [], 0
            cur.append(t)
            cnt += t[1]
        sts.append(cur)
        srcv = src.rearrange("(c p) t -> p c t", p=128)
        dstv = self.H.rearrange("(c p) t -> p c t", p=128)
        w1v = self.ffn_w1[layer, which].rearrange("(kc p) f -> p kc f", p=128)
        w3v = self.ffn_w3[layer, which].rearrange("(kc p) f -> p kc f", p=128)
        w2v = self.ffn_w2[layer, which].rearrange("(fc p) d -> p fc d", p=128)
        for st in sts:
            self._ffn_st(layer, nrm, st, srcv, dstv, w1v, w3v, w2v)

    def _ffn_st(self, layer, nrm, st, srcv, dstv, w1v, w3v, w2v):
        E = self.E
        if True:
            E.phase()
            ntok = sum(t[1] for t in st)
            hbuf = E.tmp("h", [128, 8, ntok], F32)
            ubuf = E.tmp("u", [128, 8, ntok], BF16)
            sq = E.tmp("sq", [128, 8, 512], BF16)
            rstd = E.tmp("rstd", [128, 512], F32)
            tmpn = [E.tmp("tmpn%d" % i, [128, 512], F32) for i in range(2)]
            GS = 4
            groups = [(f0, min(GS, NFC - f0)) for f0 in range(0, NFC, GS)]
            w13 = [E.tmp("w13_%d" % i, [128, 2, 8, GS * 128], BF16) for i in range(2)]
            w2 = [E.tmp("w2_%d" % i, [128, GS, D], BF16) for i in range(2)]
            gb = [E.tmp("g%d" % i, [128, GS, 512], BF16) for i in range(2)]
            sl = [E.tmp("sl%d" % i, [128, 512], F32) for i in range(2)]
            offs = []
            o = 0
            for t in st:
                offs.append(o)
                o += t[1]
            for ti, (c0, n, cond) in enumerate(st):
                o = offs[ti]
                E.dma("sync", [lambda e, o=o, c0=c0, n=n: e.dma_start(out=hbuf[:, :, o:o + n], in_=srcv[:, :, c0:c0 + n])],
                      ("hld", ti), writes=[("h", ti)])

            def do_adaln(ti):
                c0, n, cond = st[ti]
                o = offs[ti]
                self.adaln(hbuf[:, :, o:o + n], n, layer, nrm, cond, lambda c, o=o, n=n: ubuf[:, c, o:o + n],
                           ("h", ti), ("u", ti), sq, rstd, tmpn, 6, idx=ti)

            do_adaln(0)
            items = [(gi_, ti) for gi_ in range(len(groups)) for ti in range(len(st))]

            def load_w(gi_):
                b = gi_ % 2
                f0, gsz = groups[gi_]
                E.dma("gpsimd", [lambda e: e.dma_start(out=w13[b][:, 0, :, 0:gsz * 128], in_=w1v[:, :, f0 * 128:(f0 + gsz) * 128]),
                                 lambda e: e.dma_start(out=w13[b][:, 1, :, 0:gsz * 128], in_=w3v[:, :, f0 * 128:(f0 + gsz) * 128]),
                                 lambda e: e.dma_start(out=w2[b][:, 0:gsz, :], in_=w2v[:, f0:f0 + gsz, :])],
                      ("ffw", b), writes=[("ffw", b)])

            def stage_a(it, k):
                gi_, ti = it
                b = gi_ % 2
                f0, gsz = groups[gi_]
                c0, n, cond = st[ti]
                o = offs[ti]
                gk = k % 2
                for j in range(gsz):
                    pa, pb = (2 * j) % 4, (2 * j + 1) % 4
                    for kc in range(8):
                        E.op("tensor", lambda e, kc=kc, j=j, pa=pa: e.matmul(self.ps[pa][:, 0:n], lhsT=w13[b][:, 0, kc, j * 128:(j + 1) * 128],
                                                                             rhs=ubuf[:, kc, o:o + n], start=(kc == 0), stop=(kc == 7)),
                             reads=[("ffw", b), ("u", ti)], writes=[("ps", pa)])
                    for kc in range(8):
                        E.op("tensor", lambda e, kc=kc, j=j, pb=pb: e.matmul(self.ps[pb][:, 0:n], lhsT=w13[b][:, 1, kc, j * 128:(j + 1) * 128],
                                                                             rhs=ubuf[:, kc, o:o + n], start=(kc == 0), stop=(kc == 7)),
                             reads=[("ffw", b), ("u", ti)], writes=[("ps", pb)])
                    E.op("scalar", lambda e, j=j, pa=pa: e.activation(out=sl[j % 2][:, 0:n], in_=self.ps[pa][:, 0:n], func=AF.Silu),
                         reads=[("ps", pa)], writes=[("sl", j % 2)])
                    E.op("vector", lambda e, j=j, pb=pb: e.tensor_tensor(out=gb[gk][:, j, 0:n], in0=sl[j % 2][:, 0:n], in1=self.ps[pb][:, 0:n], op=ALU.mult),
                         reads=[("sl", j % 2), ("ps", pb)], writes=[("g", gk, j)])

            def stage_b(it, k):
                gi_, ti = it
                b = gi_ % 2
                f0, gsz = groups[gi_]
                c0, n, cond = st[ti]
                o = offs[ti]
                gk = k % 2
                for dc in range(8):
                    py = 4 + dc % 4
                    for j in range(gsz):
                        E.op("tensor", lambda e, dc=dc, j=j, py=py: e.matmul(self.ps[py][:, 0:n], lhsT=w2[b][:, j, dc * 128:(dc + 1) * 128],
                                                                             rhs=gb[gk][:, j, 0:n], start=(j == 0), stop=(j == gsz - 1)),
                             reads=[("ffw", b), ("g", gk, j)], writes=[("ps", py)])
                    gate = self.AD[:, layer, nrm, 2, dc, cond:cond + 1]
                    E.op("vector", lambda e, dc=dc, py=py, gate=gate: e.scalar_tensor_tensor(out=hbuf[:, dc, o:o + n], in0=self.ps[py][:, 0:n], scalar=gate,
                                                                                             in1=hbuf[:, dc, o:o + n], op0=ALU.mult, op1=ALU.add),
                         reads=[("ps", py), "AD"], writes=[("hd", ti, dc)])

            load_w(0)
            prev = None
            for k, it in enumerate(items):
                stage_a(it, k)
                if it[0] == 0 and it[1] + 1 < len(st):
                    do_adaln(it[1] + 1)
                if prev is not None:
                    stage_b(prev[0], prev[1])
                if it[1] == 0 and it[0] + 1 < len(groups):
                    load_w(it[0] + 1)
                prev = (it, k)
            stage_b(prev[0], prev[1])
            for ti, (c0, n, cond) in enumerate(st):
                o = offs[ti]
                E.dma("sync", [lambda e, o=o, c0=c0, n=n: e.dma_start(out=dstv[:, :, c0:c0 + n], in_=hbuf[:, :, o:o + n])],
                      ("hst", ti % 2), reads=[("hd", ti, dc) for dc in range(8)] + [("h", ti)], writes=[("Hd", ti)])

    def final_norm(self):
        E = self.E
        E.phase()
        hv = self.H.rearrange("(c p) t -> p c t", p=128)
        ov = self.outT.rearrange("(c p) t -> p c t", p=128)
        hb = [E.tmp("fh%d" % i, [128, 8, 512], F32) for i in range(2)]
        ob = [E.tmp("fo%d" % i, [128, 8, 512], F32) for i in range(2)]
        sq = E.tmp("sq", [128, 8, 512], BF16)
        rstd = E.tmp("rstd", [128, 512], F32)
        tmpn = [E.tmp("tmpn%d" % i, [128, 512], F32) for i in range(2)]
        zero = E.tmp("zero", [128, 1], F32)
        E.op("vector", lambda e: e.memset(zero[:], 0.0), writes=["zero"])
        for ti, (c0, n, cond) in enumerate(LAT_TILES):
            b = ti % 2
            E.dma("sync", [lambda e, b=b, c0=c0, n=n: e.dma_start(out=hb[b][:], in_=hv[:, :, c0:c0 + n])], ("fh", b), writes=[("fh", b)])
            self.adaln(hb[b], n, 0, 0, 0, lambda c, b=b: ob[b][:, c, :], ("fh", b), ("fo", b), sq, rstd, tmpn, 6,
                       A=lambda c: self.finA[:, c:c + 1], shift=lambda c: zero[:, 0:1], idx=ti)
            E.dma("scalar", [lambda e, b=b, c0=c0, n=n: e.dma_start(out=ov[:, :, c0 - CTX:c0 - CTX + n], in_=ob[b][:])], ("fo", b),
                  reads=[("fo", b)], writes=[("out", ti)])

    def load_norm_tiles(self, tiles, layer, nrm, out_fn, hb, sq, rstd, tmpn, keyp):
        E = self.E
        hv = self.H.rearrange("(c p) t -> p c t", p=128)
        for ti, (c0, n, cond) in enumerate(tiles):
            b = ti % 2
            E.dma("sync", [lambda e, b=b, c0=c0, n=n: e.dma_start(out=hb[b][:, :, 0:n], in_=hv[:, :, c0:c0 + n])], (keyp, b), writes=[(keyp, b)])
            self.adaln(hb[b], n, layer, nrm, cond, (lambda c, ti=ti: out_fn(ti, c)), (keyp, b), (keyp + "o", ti), sq, rstd, tmpn, 6, idx=ti)

    def retention(self, layer, slot, ctx_out):
        E = self.E
        _ = (self.rot, self.retc, self.ret_w_in, self.ret_w_out)
        tiles = [CTX_TILE] + LAT_TILES
        E.phase()
        u_all = E.tmp("u_all", [128, 8, NT], BF16)
        cos = E.tmp("cos", [128, NT], F32)
        sin = E.tmp("sin", [128, NT], F32)
        hb = [E.tmp("hb%d" % i, [128, 8, 512], F32) for i in range(2)]
        sq = E.tmp("sq", [128, 8, 512], BF16)
        rstd = E.tmp("rstd", [128, 512], F32)
        tmpn = [E.tmp("tmpn%d" % i, [128, 512], F32) for i in range(2)]
        wg = [E.tmp("wg%d" % i, [128, 8, 512], BF16) for i in range(2)]
        rt = [E.tmp("rt%d" % i, [128, 512], F32) for i in range(4)]
        ob = [E.tmp("ob%d" % i, [128, 512], BF16) for i in range(4)]
        E.dma("sync", [lambda e: e.dma_start(out=cos[:], in_=self.rot[0]), lambda e: e.dma_start(out=sin[:], in_=self.rot[1])], "rot", writes=["rot"])
        self.load_norm_tiles(tiles, layer, 1, lambda ti, c: u_all[:, c, tiles[ti][0]:tiles[ti][0] + tiles[ti][1]], hb, sq, rstd, tmpn, "rh")
        ukeys = [("rho", ti) for ti in range(len(tiles))]
        wv = self.ret_w_in[slot].rearrange("(kc p) f -> p kc f", p=128)
        cnt = {"ob": 0, "t": 0}

        def loadw(g):
            b = g % 2
            E.dma("gpsimd", [lambda e, b=b, g=g: e.dma_start(out=wg[b][:], in_=wv[:, :, g * 512:(g + 1) * 512])], ("rwg", b), writes=[("rwg", b)])

        loadw(0)
        for g in range(12):
            b = g % 2
            if g + 1 < 12:
                loadw(g + 1)
            if g < 4:
                dst = (self.QT if g < 2 else self.KT).rearrange("(c p) t -> p c t", p=128)
                for hh in range(2):
                    for ti, (c0, n, cond) in enumerate(tiles):
                        pa, pb = (0, 1) if cnt["t"] % 2 == 0 else (2, 3)
                        cnt["t"] += 1
                        for kc in range(8):
                            E.op("tensor", lambda e, kc=kc, hh=hh, c0=c0, n=n, pa=pa, b=b: e.matmul(
                                self.ps[pa][:, 0:n], lhsT=wg[b][:, kc, hh * 256:hh * 256 + 128], rhs=u_all[:, kc, c0:c0 + n], start=(kc == 0), stop=(kc == 7)),
                                reads=[("rwg", b), ukeys[ti]], writes=[("ps", pa)])
                        for kc in range(8):
                            E.op("tensor", lambda e, kc=kc, hh=hh, c0=c0, n=n, pb=pb, b=b: e.matmul(
                                self.ps[pb][:, 0:n], lhsT=wg[b][:, kc, hh * 256 + 128:hh * 256 + 256], rhs=u_all[:, kc, c0:c0 + n], start=(kc == 0), stop=(kc == 7)),
                                reads=[("rwg", b), ukeys[ti]], writes=[("ps", pb)])
                        o1 = cnt["ob"] % 4
                        o2 = (cnt["ob"] + 1) % 4
                        cnt["ob"] += 2
                        E.op("vector", lambda e, c0=c0, n=n, pa=pa: e.tensor_tensor(out=rt[0][:, 0:n], in0=self.ps[pa][:, 0:n], in1=cos[:, c0:c0 + n], op=ALU.mult),
                             reads=[("ps", pa), "rot"], writes=[("rt", 0)])
                        E.op("vector", lambda e, c0=c0, n=n, pb=pb: e.tensor_tensor(out=rt[1][:, 0:n], in0=self.ps[pb][:, 0:n], in1=sin[:, c0:c0 + n], op=ALU.mult),
                             reads=[("ps", pb), "rot"], writes=[("rt", 1)])
                        E.op("gpsimd", lambda e, n=n, o1=o1: e.tensor_tensor(out=ob[o1][:, 0:n], in0=rt[0][:, 0:n], in1=rt[1][:, 0:n], op=ALU.subtract),
                             reads=[("rt", 0), ("rt", 1)], writes=[("ob", o1)])
                        E.op("vector", lambda e, c0=c0, n=n, pb=pb: e.tensor_tensor(out=rt[2][:, 0:n], in0=self.ps[pb][:, 0:n], in1=cos[:, c0:c0 + n], op=ALU.mult),
                             reads=[("ps", pb), "rot"], writes=[("rt", 2)])
                        E.op("vector", lambda e, c0=c0, n=n, pa=pa: e.tensor_tensor(out=rt[3][:, 0:n], in0=self.ps[pa][:, 0:n], in1=sin[:, c0:c0 + n], op=ALU.mult),
                             reads=[("ps", pa), "rot"], writes=[("rt", 3)])
                        E.op("gpsimd", lambda e, n=n, o2=o2: e.tensor_tensor(out=ob[o2][:, 0:n], in0=rt[2][:, 0:n], in1=rt[3][:, 0:n], op=ALU.add),
                             reads=[("rt", 2), ("rt", 3)], writes=[("ob", o2)])
                        fc = (g % 2) * 4 + hh * 2
                        E.dma("gpsimd", [lambda e, o1=o1, fc=fc, c0=c0, n=n, dst=dst: e.dma_start(out=dst[:, fc, c0:c0 + n], in_=ob[o1][:, 0:n])],
                              ("obs", o1), reads=[("ob", o1)], writes=[("qk", g, hh, ti, 0)])
                        E.dma("gpsimd", [lambda e, o2=o2, fc=fc, c0=c0, n=n, dst=dst: e.dma_start(out=dst[:, fc + 1, c0:c0 + n], in_=ob[o2][:, 0:n])],
                              ("obs", o2), reads=[("ob", o2)], writes=[("qk", g, hh, ti, 1)])
            else:
                dstT = self.VT if g < 8 else self.GT
                col0 = ((g - 4) % 4) * 512
                for tc in range(NT // 128):
                    tok0 = tc * 128
                    pa = cnt["t"] % 4
                    cnt["t"] += 1
                    ti = 0 if tok0 < CTX else 1 + (tok0 - CTX) // 512
                    for kc in range(8):
                        E.op("tensor", lambda e, kc=kc, tok0=tok0, pa=pa, b=b: e.matmul(
                            self.ps[pa][:, 0:512], lhsT=u_all[:, kc, tok0:tok0 + 128], rhs=wg[b][:, kc, :], start=(kc == 0), stop=(kc == 7)),
                            reads=[("rwg", b), ukeys[ti]], writes=[("ps", pa)])
                    o1 = cnt["ob"] % 4
                    cnt["ob"] += 1
                    fn = AF.Copy if g < 8 else AF.Silu
                    E.op("scalar", lambda e, pa=pa, o1=o1, fn=fn: e.activation(out=ob[o1][:], in_=self.ps[pa][:, 0:512], func=fn),
                         reads=[("ps", pa)], writes=[("ob", o1)])
                    E.dma("scalar", [lambda e, o1=o1, tok0=tok0, col0=col0, dstT=dstT: e.dma_start(out=dstT[tok0:tok0 + 128, col0:col0 + 512], in_=ob[o1][:])],
                          ("obs", o1), reads=[("ob", o1)], writes=[("vg", g, tc)])

        E.phase()
        RC = E.tmp("retc", [128, RETC_N], F32)
        E.dma("sync", [lambda e: e.dma_start(out=RC[:], in_=self.retc)], "retc", writes=["retc"])
        rel = [RC[:, 0:128], RC[:, 128:256]]
        mask = [RC[:, 256:384], RC[:, 384:512]]
        qexp = [RC[:, 512:640], RC[:, 640:768]]
        kexp = [RC[:, 768:769], RC[:, 769:770]]
        lg = E.tmp("lg", [128, 8], F32)
        onec = E.tmp("onec", [128, 1], F32)
        dm = E.tmp("dm", [128, 8, 128], F32)
        qd = E.tmp("qd", [128, 8, 128], F32)
        kd = E.tmp("kd", [128, 8], F32)
        cd = E.tmp("cd", [128, 8], F32)
        E.op("vector", lambda e: e.memset(onec[:], 1.0), writes=["onec"])
        do, _ = VEC_LAYOUT["ret_decay"]
        dsl = self.V[:, do + slot * 8: do + slot * 8 + 8]
        E.op("scalar", lambda e: e.activation(out=lg[:], in_=dsl, func=AF.Exp, scale=-1.0), reads=["retc"], writes=["lg"])
        E.op("scalar", lambda e: e.activation(out=lg[:], in_=lg[:], func=AF.Ln, bias=onec[:, 0:1], scale=1.0), reads=["lg", "onec"], writes=["lg"])
        E.op("vector", lambda e: e.tensor_scalar(out=lg[:], in0=lg[:], scalar1=-1.0, scalar2=None, op0=ALU.mult), reads=["lg"], writes=["lg"])
        for d_ in range(2):
            for h in range(4):
                i = d_ * 4 + h
                E.op("scalar", lambda e, i=i, d_=d_: e.activation(out=dm[:, i, :], in_=rel[d_], func=AF.Exp, scale=lg[:, i:i + 1]), reads=["lg", "retc"], writes=["dm"])
                E.op("vector", lambda e, i=i, d_=d_: e.tensor_tensor(out=dm[:, i, :], in0=dm[:, i, :], in1=mask[d_], op=ALU.mult), reads=["dm", "retc"], writes=["dm"])
                E.op("scalar", lambda e, i=i, d_=d_: e.activation(out=qd[:, i, :], in_=qexp[d_], func=AF.Exp, scale=lg[:, i:i + 1]), reads=["lg", "retc"], writes=["qd"])
                E.op("scalar", lambda e, i=i, d_=d_: e.activation(out=kd[:, i:i + 1], in_=kexp[d_], func=AF.Exp, scale=lg[:, i:i + 1]), reads=["lg", "retc"], writes=["kd"])
        E.op("vector", lambda e: e.tensor_scalar(out=kd[:], in0=kd[:], scalar1=1.0 / 16.0, scalar2=None, op0=ALU.mult), reads=["kd"], writes=["kd"])
        E.op("scalar", lambda e: e.activation(out=cd[:], in_=lg[:], func=AF.Exp, scale=128.0), reads=["lg"], writes=["cd"])

        S = E.tmp("S", [128, 4, 2, 512], F32)
        Sb = E.tmp("Sb", [128, 4, 2, 512], BF16)
        qt = [E.tmp("qt%d" % i, [128, 8, 128], BF16) for i in range(3)]
        kt = [E.tmp("kt%d" % i, [128, 8, 128], BF16) for i in range(3)]
        vt = [E.tmp("vt%d" % i, [128, 2048], BF16) for i in range(3)]
        gt = [E.tmp("gt%d" % i, [128, 2048], BF16) for i in range(2)]
        ofs = [E.tmp("of%d" % i, [128, 2048], F32) for i in range(2)]
        sT = [E.tmp("sT%d" % i, [128, 128], BF16) for i in range(2)]
        kp = [E.tmp("kp%d" % i, [128, 256], BF16) for i in range(2)]
        qp = [E.tmp("qp%d" % i, [128, 2, 128], BF16) for i in range(2)]
        ot = E.tmp("ot", [128, 2048], F32)
        ysq = E.tmp("ysq", [128, 512], F32)
        st4 = E.tmp("st4", [128, 8, 4], F32)
        yb = E.tmp("yb", [128, 2048], BF16)
        yT = E.tmp("yT", [128, 16, 128], BF16)
        wo = E.tmp("wo", [128, 16, D], BF16)
        hch = [E.tmp("hch%d" % i, [128, 8, 128], F32) for i in range(2)]
        epsc = E.tmp("epsc", [128, 1], F32)
        E.op("vector", lambda e: e.memset(epsc[:], EPS), writes=["epsc"])
        zeroc = E.tmp("zeroc", [128, 1], F32)
        E.op("vector", lambda e: e.memset(zeroc[:], 0.0), writes=["zeroc"])
        E.dma("gpsimd", [lambda e: e.dma_start(out=wo[:], in_=self.ret_w_out[slot].rearrange("(fc p) d -> p fc d", p=128))], "wo", writes=["wo"])
        qv = self.QT.rearrange("(c p) t -> p c t", p=128)
        kv = self.KT.rearrange("(c p) t -> p c t", p=128)
        hv = self.H.rearrange("(c p) t -> p c t", p=128)
        nchunk = NT // 128
        for d_ in range(2):
            order = list(range(nchunk)) if d_ == 0 else [1, 0] + list(range(nchunk - 1, 1, -1))
            for ci, tc in enumerate(order):
                self._ret_chunk(layer, d_, ci, tc, ci == 0, ci == len(order) - 1, (tc >= 2) or ctx_out,
                                dict(qt=qt, kt=kt, vt=vt, gt=gt, ofs=ofs, sT=sT, kp=kp, qp=qp, ot=ot, ysq=ysq, st4=st4, yb=yb, yT=yT, wo=wo,
                                     hch=hch, S=S, Sb=Sb, zeroc=zeroc, dm=dm, qd=qd, kd=kd, cd=cd, qv=qv, kv=kv, hv=hv, epsc=epsc))

    def _ret_chunk(self, layer, d_, ci, tc, first, lastc, need_out, B):
        E = self.E
        tok0 = tc * 128
        cond = 1 if tc < 2 else 0
        gi = d_ * 64 + ci
        b3 = gi % 3
        b2 = gi % 2
        qt, kt, vt = B["qt"][b3], B["kt"][b3], B["vt"][b3]
        E.dma("sync", [lambda e: e.dma_start(out=qt[:], in_=B["qv"][:, :, tok0:tok0 + 128])], ("qt", b3), writes=[("qt", b3)])
        E.dma("sync", [lambda e: e.dma_start(out=kt[:], in_=B["kv"][:, :, tok0:tok0 + 128])], ("kt", b3), writes=[("kt", b3)])
        E.dma("sync", [lambda e: e.dma_start(out=vt[:], in_=self.VT[tok0:tok0 + 128, :])], ("vt", b3), writes=[("vt", b3)])
        readout = need_out and d_ == 1
        if readout:
            gt, ofl, hch = B["gt"][b2], B["ofs"][b2], B["hch"][b2]
            E.dma("sync", [lambda e: e.dma_start(out=gt[:], in_=self.GT[tok0:tok0 + 128, :])], ("gt", b2), writes=[("gt", b2)])
            E.dma("sync", [lambda e: e.dma_start(out=ofl[:], in_=self.OF[tok0:tok0 + 128, :])], ("ofl", b2), writes=[("of", b2)])
            E.dma("sync", [lambda e: e.dma_start(out=hch[:], in_=B["hv"][:, :, tok0:tok0 + 128])], ("hch", b2), writes=[("hch", b2)])
        elif need_out:
            ofl = B["ofs"][b2]
        S, Sb, ot = B["S"], B["Sb"], B["ot"]

        def do_head(h):
            i = d_ * 4 + h
            hb_ = h % 2
            bs, bo = hb_, 2 + hb_
            sT, kp, qp = B["sT"][hb_], B["kp"][hb_], B["qp"][hb_]
            vh = vt[:, h * 512:(h + 1) * 512]
            if need_out:
                for c in range(2):
                    E.op("tensor", lambda e, c=c: e.matmul(self.ps[bs][:, 0:128], lhsT=kt[:, 2 * h + c, :], rhs=qt[:, 2 * h + c, :], start=(c == 0), stop=(c == 1)),
                         reads=[("qt", b3), ("kt", b3)], writes=[("pss", bs)])
                E.op("vector", lambda e: e.tensor_tensor(out=sT[:], in0=self.ps[bs][:, 0:128], in1=B["dm"][:, i, :], op=ALU.mult),
                     reads=[("pss", bs), "dm"], writes=[("sT", hb_)])
            if not lastc:
                for c in range(2):
                    E.op("tensor", lambda e, c=c: e.matmul(self.ps[bs][:, 128 + c * 128:256 + c * 128], lhsT=kt[:, 2 * h + c, :], rhs=self.ident[:], start=True, stop=True),
                         reads=[("kt", b3)], writes=[("pst", bs)])
                E.op("vector", lambda e: e.tensor_scalar(out=kp[:], in0=self.ps[bs][:, 128:384], scalar1=B["kd"][:, i:i + 1], scalar2=None, op0=ALU.mult),
                     reads=[("pst", bs), "kd"], writes=[("kp", hb_)])
            if need_out:
                if not first:
                    for c in range(2):
                        E.op("gpsimd", lambda e, c=c: e.tensor_tensor(out=qp[:, c, :], in0=qt[:, 2 * h + c, :], in1=B["qd"][:, i, :], op=ALU.mult),
                             reads=[("qt", b3), "qd"], writes=[("qp", hb_)])
                E.op("tensor", lambda e: e.matmul(self.ps[bo][:, 0:512], lhsT=sT[:], rhs=vh, start=True, stop=first),
                     reads=[("sT", hb_), ("vt", b3)], writes=[("ps", bo)])
                if not first:
                    for c in range(2):
                        E.op("tensor", lambda e, c=c: e.matmul(self.ps[bo][:, 0:512], lhsT=qp[:, c, :], rhs=Sb[:, h, c, :], start=False, stop=(c == 1)),
                             reads=[("qp", hb_), ("Sb", h, c)], writes=[("ps", bo)])
                if d_ == 0:
                    E.op("scalar", lambda e: e.activation(out=ofl[:, h * 512:(h + 1) * 512], in_=self.ps[bo][:, 0:512], func=AF.Copy),
                         reads=[("ps", bo)], writes=[("of", b2)])
                else:
                    E.op("vector", lambda e: e.tensor_tensor(out=ot[:, h * 512:(h + 1) * 512], in0=self.ps[bo][:, 0:512], in1=ofl[:, h * 512:(h + 1) * 512], op=ALU.add),
                         reads=[("ps", bo), ("of", b2)], writes=[("ot", h)])
            if not lastc:
                for c in range(2):
                    E.op("tensor", lambda e, c=c: e.matmul(self.ps[4 + c][:, 0:512], lhsT=kp[:, c * 128:(c + 1) * 128], rhs=vh, start=True, stop=True),
                         reads=[("kp", hb_), ("vt", b3)], writes=[("ps", 4 + c)])
                    if first:
                        E.op("vector", lambda e, c=c: e.tensor_copy(out=S[:, h, c, :], in_=self.ps[4 + c][:, 0:512]), reads=[("ps", 4 + c)], writes=[("S", h, c)])
                    else:
                        E.op("vector", lambda e, c=c: e.scalar_tensor_tensor(out=S[:, h, c, :], in0=S[:, h, c, :], scalar=B["cd"][:, i:i + 1], in1=self.ps[4 + c][:, 0:512],
                                                                             op0=ALU.mult, op1=ALU.add), reads=[("ps", 4 + c), ("S", h, c), "cd"], writes=[("S", h, c)])
                    E.op("scalar", lambda e, c=c: e.activation(out=Sb[:, h, c, :], in_=S[:, h, c, :], func=AF.Copy), reads=[("S", h, c)], writes=[("Sb", h, c)])
        for h_ in range(4):
            do_head(h_)
        if need_out and d_ == 0:
            E.dma("scalar", [lambda e: e.dma_start(out=self.OF[tok0:tok0 + 128, :], in_=ofl[:])], ("ofs", b2), reads=[("of", b2)], writes=[("OFd", tc)])
        if not readout:
            return
        st4, ysq, yb, yT, wo = B["st4"], B["ysq"], B["yb"], B["yT"], B["wo"]
        for h in range(4):
            oh = ot[:, h * 512:(h + 1) * 512]
            E.op("vector", lambda e, h=h, oh=oh: e.tensor_reduce(out=st4[:, 0, h:h + 1], in_=oh, axis=AX.X, op=ALU.add), reads=[("ot", h)], writes=[("st", 0, h)])
            E.op("scalar", lambda e, oh=oh: e.activation(out=ysq[:], in_=oh, func=AF.Square), reads=[("ot", h)], writes=["ysq"])
            E.op("vector", lambda e, h=h: e.tensor_reduce(out=st4[:, 1, h:h + 1], in_=ysq[:], axis=AX.X, op=ALU.add), reads=["ysq"], writes=[("st", 1, h)])
        allst = [("st", a, h) for a in range(2) for h in range(4)]
        E.op("vector", lambda e: e.tensor_scalar(out=st4[:, 0:2, :], in0=st4[:, 0:2, :], scalar1=1.0 / 512.0, scalar2=None, op0=ALU.mult), reads=allst, writes=["stA"])
        E.op("vector", lambda e: e.tensor_tensor(out=st4[:, 2, :], in0=st4[:, 0, :], in1=st4[:, 0, :], op=ALU.mult), reads=["stA"], writes=["stB"])
        E.op("vector", lambda e: e.tensor_tensor(out=st4[:, 3, :], in0=st4[:, 1, :], in1=st4[:, 2, :], op=ALU.subtract), reads=["stA", "stB"], writes=["stC"])
        E.op("scalar", lambda e: e.activation(out=st4[:, 4, :], in_=st4[:, 3, :], func=AF.Ln, bias=B["epsc"][:, 0:1], scale=1.0), reads=["stC", "epsc"], writes=["stD"])
        E.op("scalar", lambda e: e.activation(out=st4[:, 5, :], in_=st4[:, 4, :], func=AF.Exp, scale=-0.5), reads=["stD"], writes=["stE"])
        E.op("vector", lambda e: e.scalar_tensor_tensor(out=st4[:, 6, :], in0=st4[:, 0, :], scalar=-1.0, in1=st4[:, 5, :], op0=ALU.mult, op1=ALU.mult),
             reads=["stA", "stE"], writes=["stF"])
        for h in range(4):
            oh = ot[:, h * 512:(h + 1) * 512]
            E.op("scalar", lambda e, h=h, oh=oh: e.activation(out=oh, in_=oh, func=AF.Identity, bias=st4[:, 6, h:h + 1], scale=st4[:, 5, h:h + 1]),
                 reads=[("ot", h), "stE", "stF", "ysq"], writes=[("ot", h)])
            E.op("vector", lambda e, h=h, oh=oh: e.tensor_tensor(out=yb[:, h * 512:(h + 1) * 512], in0=oh, in1=gt[:, h * 512:(h + 1) * 512], op=ALU.mult),
                 reads=[("ot", h), ("gt", b2)], writes=[("yb", h)])
        for fq in range(4):
            for i2 in range(4):
                fc = fq * 4 + i2
                E.op("tensor", lambda e, fc=fc, i2=i2: e.matmul(self.ps[6][:, i2 * 128:(i2 + 1) * 128], lhsT=yb[:, fc * 128:(fc + 1) * 128], rhs=self.ident[:], start=True, stop=True),
                     reads=[("yb", fq)], writes=[("ps", 6)])
            E.op("scalar", lambda e, fq=fq: e.activation(out=yT[:, fq * 4:(fq + 1) * 4, :].rearrange("p a b -> p (a b)"), in_=self.ps[6][:, 0:512], func=AF.Copy),
                 reads=[("ps", 6)], writes=[("yT", fq)])
        for dc in range(8):
            for fc in range(16):
                E.op("tensor", lambda e, dc=dc, fc=fc: e.matmul(self.ps[7][:, 0:128], lhsT=wo[:, fc, dc * 128:(dc + 1) * 128], rhs=yT[:, fc, :], start=(fc == 0), stop=(fc == 15)),
                     reads=["wo"] + [("yT", q) for q in range(4)], writes=[("ps", 7)])
            gate = self.AD[:, layer, 1, 2, dc, cond:cond + 1]
            E.op("vector", lambda e, dc=dc, gate=gate: e.scalar_tensor_tensor(out=hch[:, dc, :], in0=self.ps[7][:, 0:128], scalar=gate, in1=hch[:, dc, :], op0=ALU.mult, op1=ALU.add),
                 reads=[("ps", 7), ("hch", b2), "AD"], writes=[("hcho", b2, dc)])
        E.dma("gpsimd", [lambda e: e.dma_start(out=B["hv"][:, :, tok0:tok0 + 128], in_=hch[:])], ("hchs", b2),
              reads=[("hcho", b2, dc) for dc in range(8)] + [("hch", b2)], writes=[("Hc", tc)])

    def pool(self, layer, slot, tiles):
        ct = [t for t in tiles if t[2] == 1]
        lt = [t for t in tiles if t[2] == 0]
        if ct:
            self._pool_tiles(layer, ct, "c", 1, 256)
        self._pool_tiles(layer, lt, "l", 8, 64)

    def _pool_tiles(self, layer, tiles, nm0, rows0, w0):
        E = self.E
        _ = (self.poolrc, self.pool_w)
        E.phase()
        hv = self.H.rearrange("(c p) t -> p c t", p=128)
        hb = [E.tmp("hb%d" % i, [128, 8, 512], F32) for i in range(2)]
        sq = E.tmp("sq", [128, 8, 512], BF16)
        rstd = E.tmp("rstd", [128, 512], F32)
        tmpn = [E.tmp("tmpn%d" % i, [128, 512], F32) for i in range(2)]
        rc = E.tmp("rc", [128, 4, 768], F32)
        pw = E.tmp("pw", [128, 4, 2, 256], BF16)
        E.dma("sync", [lambda e: e.dma_start(out=rc[:], in_=self.poolrc)], "rc", writes=["rc"])
        E.dma("gpsimd", [lambda e: e.dma_start(out=pw[:], in_=self.pool_w.rearrange("g (cc p) e -> p g cc e", p=128))], "pw", writes=["pw"])
        bufs = {}
        for nm, rows, w in ((nm0, rows0, w0),):
            wp = w + 16
            arr = [E.tmp("up" + nm, [128, 8 * rows, wp], F32)] + [E.tmp("A%d%s" % (k, nm), [128, 8 * rows, wp], F32) for k in range(4)]
            for a_i, a in enumerate(arr):
                E.op("gpsimd", lambda e, a=a: e.memset(a[:], 0.0), writes=[("pad", nm, a_i)])
            bufs[nm] = (arr, rows, w, wp)
        dl = E.tmp("dl", [128, 8, 512], BF16)
        tm = E.tmp("tm", [128, 512], F32)
        yv = E.tmp("yv", [128, 512], F32)
        pbo, _ = VEC_LAYOUT["pool_b"]
        pso, _ = VEC_LAYOUT["pool_scale"]
        for ti, (c0, n, cond) in enumerate(tiles):
            nm = "c" if cond == 1 else "l"
            arr, rows, w, wp = bufs[nm]
            up = arr[0]
            b = ti % 2
            E.dma("sync", [lambda e, b=b, c0=c0, n=n: e.dma_start(out=hb[b][:, :, 0:n], in_=hv[:, :, c0:c0 + n])], ("ph", b), writes=[("ph", b)])
            self.adaln(hb[b], n, layer, 1, cond,
                       (lambda c, up=up, rows=rows, w=w: up[:, c * rows:(c + 1) * rows, 8:8 + w]),
                       ("ph", b), ("up", nm), sq, rstd, tmpn, 6, idx=ti, out3=(rows, w))
            A2, A4, A8, A16 = arr[1], arr[2], arr[3], arr[4]
            R = 8 * rows
            E.op("vector", lambda e, up=up, A2=A2, wp=wp: e.tensor_tensor(out=A2[:, :, 1:wp], in0=up[:, :, 0:wp - 1], in1=up[:, :, 1:wp], op=ALU.add),
                 reads=[("up", nm), ("pad", nm, 0)], writes=[("A", nm, 2)])
            E.op("gpsimd", lambda e, A2=A2, A4=A4, wp=wp, rows=rows: e.tensor_tensor(out=A4[:, 2 * rows:, 1:wp - 1], in0=A2[:, 2 * rows:, 0:wp - 2], in1=A2[:, 2 * rows:, 2:wp], op=ALU.add),
                 reads=[("A", nm, 2)], writes=[("A", nm, 4)])
            E.op("vector", lambda e, A4=A4, A8=A8, wp=wp, rows=rows: e.tensor_tensor(out=A8[:, 4 * rows:, 2:wp - 2], in0=A4[:, 4 * rows:, 0:wp - 4], in1=A4[:, 4 * rows:, 4:wp], op=ALU.add),
                 reads=[("A", nm, 4)], writes=[("A", nm, 8)])
            E.op("gpsimd", lambda e, A8=A8, A16=A16, wp=wp, rows=rows: e.tensor_tensor(out=A16[:, 6 * rows:, 4:wp - 4], in0=A8[:, 6 * rows:, 0:wp - 8], in1=A8[:, 6 * rows:, 8:wp], op=ALU.add),
                 reads=[("A", nm, 8)], writes=[("A", nm, 16)])
            rco = 0 if nm == "l" else 512
            for c in range(8):
                gi = c // 2
                Aw = arr[1 + gi]
                wk = (2, 4, 8, 16)[gi]
                E.op("vector", lambda e, c=c, Aw=Aw, gi=gi, rows=rows, w=w, rco=rco, n=n: e.tensor_tensor(
                    out=tm[:, 0:n].rearrange("p (r w) -> p r w", w=w), in0=Aw[:, c * rows:(c + 1) * rows, 8:8 + w],
                    in1=rc[:, gi, rco:rco + n].rearrange("p (r w) -> p r w", w=w), op=ALU.mult),
                    reads=[("A", nm, wk), "rc"], writes=["tm"])
                E.op("vector", lambda e, c=c, up=up, rows=rows, w=w, n=n: e.tensor_tensor(
                    out=dl[:, c, 0:n].rearrange("p (r w) -> p r w", w=w), in0=tm[:, 0:n].rearrange("p (r w) -> p r w", w=w),
                    in1=up[:, c * rows:(c + 1) * rows, 8:8 + w], op=ALU.subtract),
                    reads=["tm", ("up", nm)], writes=[("dl", c)])
            for gi in range(4):
                for ec in range(2):
                    pbk = ec
                    for cc in range(2):
                        E.op("tensor", lambda e, gi=gi, ec=ec, cc=cc, n=n, pbk=pbk: e.matmul(self.ps[pbk][:, 0:n], lhsT=pw[:, gi, cc, ec * 128:(ec + 1) * 128],
                                                                                             rhs=dl[:, 2 * gi + cc, 0:n], start=(cc == 0), stop=(cc == 1)),
                             reads=["pw", ("dl", 2 * gi + cc)], writes=[("ps", pbk)])
                    dc = 2 * gi + ec
                    E.op("vector", lambda e, dc=dc, n=n, pbk=pbk: e.tensor_scalar(out=yv[:, 0:n], in0=self.ps[pbk][:, 0:n], scalar1=self.V[:, pbo + dc:pbo + dc + 1],
                                                                                  scalar2=self.V[:, pso + dc:pso + dc + 1], op0=ALU.add, op1=ALU.mult),
                         reads=[("ps", pbk), "vecs"], writes=["yv"])
                    gate = self.AD[:, layer, 1, 2, dc, cond:cond + 1]
                    E.op("vector", lambda e, dc=dc, n=n, b=b, gate=gate: e.scalar_tensor_tensor(out=hb[b][:, dc, 0:n], in0=yv[:, 0:n], scalar=gate, in1=hb[b][:, dc, 0:n],
                                                                                                op0=ALU.mult, op1=ALU.add),
                         reads=["yv", ("ph", b), ("up", nm), "AD"], writes=[("pho", b, dc)])
            E.dma("gpsimd", [lambda e, b=b, c0=c0, n=n: e.dma_start(out=hv[:, :, c0:c0 + n], in_=hb[b][:, :, 0:n])], ("phs", b),
                  reads=[("pho", b, dc) for dc in range(8)] + [("ph", b)], writes=[("Hp", ti)])

    def hyena(self, layer, slot, ctx_out):
        insts = [(SEQ, CTX, 0, "l")] + ([(CTX, 0, 1, "c")] if ctx_out else [])
        for (L, col0, cond, nm) in insts:
            self._hy_filter(L, nm)
            self._hy_proj(layer, L, col0, cond, nm)
            self._hy_conv(L, nm)
            self._hy_out(layer, L, col0, cond, nm)

    def _hy_filter(self, L, nm):
        E = self.E
        _ = (self.hy_w_pos, self.hy_w_mid, self.hy_w_filt, self.hyv64, self.hy_delta, self.hyb_bc)
        D_ = self.get_hyd(nm)
        nsc = L // 128
        E.phase()
        nrm = E.tmp("nrm", [128, 2 * D], F32)
        saved_base = E.sb_base
        E.sb_base = E.sb_top
        featT = E.tmp("featT", [33, L], F32)
        wpos = E.tmp("wpos", [33, 64], F32)
        wmid = E.tmp("wmid", [64, 64], F32)
        wfilt = E.tmp("wfilt", [64, 4096], F32)
        hv64 = E.tmp("hv64", [64, 8], F32)
        tneg = E.tmp("tneg", [128, nsc], F32)
        delta = E.tmp("delta", [128, D], F32)
        hdn = [E.tmp("hdn%d" % i, [64, L], F32) for i in range(2)]
        rr_f = E.tmp("rr_f", [64, L], F32)
        rr_i = E.tmp("rr_i", [64, L], mybir.dt.int32)
        ones32 = E.tmp("ones32", [128, 128], F32)
        win = E.tmp("win", [128, D], F32)
        hw = [E.tmp("hw%d" % i, [128, 2, D], F32) for i in range(2)]
        sqt = E.tmp("sqt", [128, 2, D], F32)
        hsd = [E.tmp("hsd%d" % i, [128, 2, 2 * D], BF16) for i in range(2)]
        E.dma("sync", [lambda e: e.dma_start(out=featT[:], in_=D_["featT"]), lambda e: e.dma_start(out=wpos[:], in_=self.hy_w_pos),
                       lambda e: e.dma_start(out=wmid[:], in_=self.hy_w_mid), lambda e: e.dma_start(out=wfilt[:], in_=self.hy_w_filt),
                       lambda e: e.dma_start(out=hv64[:], in_=self.hyv64), lambda e: e.dma_start(out=tneg[:], in_=D_["tneg"]),
                       lambda e: e.dma_start(out=delta[:], in_=self.hy_delta)], "hyf", writes=["hyf"])
        E.op("vector", lambda e: e.memset(ones32[:], 1.0), writes=["ones32"])
        E.op("vector", lambda e: e.tensor_tensor(out=hv64[:, 3:4], in0=hv64[:, 0:1], in1=hv64[:, 2:3], op=ALU.mult), reads=["hyf"], writes=["hv"])
        E.op("vector", lambda e: e.tensor_tensor(out=hv64[:, 4:5], in0=hv64[:, 1:2], in1=hv64[:, 2:3], op=ALU.mult), reads=["hyf", "hv"], writes=["hv"])
        E.op("vector", lambda e: e.memset(hv64[:, 5:6], -math.pi), reads=["hv"], writes=["hv"])
        srcs = [(wpos, featT, 3), (wmid, hdn[0], 4)]
        for li, (wl, src, bcol) in enumerate(srcs):
            for t0 in range(0, L, 512):
                n = min(512, L - t0)
                E.op("tensor", lambda e, wl=wl, src=src, t0=t0, n=n: e.matmul(self.ps[0][0:64, 0:n], lhsT=wl[:], rhs=src[:, t0:t0 + n], start=True, stop=True),
                     reads=["hyf", ("hdn", li - 1)], writes=[("ps", 0)])
                E.op("vector", lambda e, li=li, t0=t0, n=n, bcol=bcol: e.tensor_scalar(out=hdn[li][:, t0:t0 + n], in0=self.ps[0][0:64, 0:n], scalar1=hv64[:, 2:3],
                                                                                      scalar2=hv64[:, bcol:bcol + 1], op0=ALU.mult, op1=ALU.add),
                     reads=[("ps", 0), "hv"], writes=[("hdnA", li)])
            hl = hdn[li]
            E.op("vector", lambda e, hl=hl: e.tensor_scalar(out=hl[:], in0=hl[:], scalar1=8.0 * math.pi, scalar2=None, op0=ALU.add),
                 reads=[("hdnA", li)], writes=[("hdnB", li)])
            E.op("vector", lambda e, hl=hl: e.tensor_scalar(out=rr_f[:], in0=hl[:], scalar1=1.0 / (2.0 * math.pi), scalar2=None, op0=ALU.mult),
                 reads=[("hdnB", li)], writes=["rr_f"])
            E.op("vector", lambda e: e.tensor_copy(out=rr_i[:], in_=rr_f[:]), reads=["rr_f"], writes=["rr_i"])
            E.op("vector", lambda e: e.tensor_copy(out=rr_f[:], in_=rr_i[:]), reads=["rr_i"], writes=["rr_f"])
            E.op("vector", lambda e, hl=hl: e.scalar_tensor_tensor(out=hl[:], in0=rr_f[:], scalar=-2.0 * math.pi, in1=hl[:], op0=ALU.mult, op1=ALU.add),
                 reads=["rr_f", ("hdnB", li)], writes=[("hdnC", li)])
            E.op("vector", lambda e, hl=hl: e.tensor_scalar(out=rr_f[:], in0=hl[:], scalar1=math.pi, scalar2=2.0 * math.pi, op0=ALU.is_gt, op1=ALU.mult),
                 reads=[("hdnC", li)], writes=["rr_f"])
            E.op("vector", lambda e, hl=hl: e.tensor_tensor(out=hl[:], in0=hl[:], in1=rr_f[:], op=ALU.subtract),
                 reads=["rr_f", ("hdnC", li)], writes=[("hdnD", li)])
            E.op("scalar", lambda e, hl=hl: e.activation(out=hl[:], in_=hl[:], func=AF.Sin),
                 reads=[("hdnD", li)], writes=[("hdn", li)])
        for sc in range(nsc):
            E.op("scalar", lambda e, sc=sc: e.activation(out=win[:], in_=delta[:], func=AF.Exp, scale=tneg[:, sc:sc + 1]), reads=["hyf"], writes=["win"])
            b = sc % 2
            for o in range(2):
                for dr in range(2):
                    for hf in range(2):
                        cb = o * 2048 + dr * 1024 + hf * 512
                        pb = (dr * 2 + hf) % 4
                        E.op("tensor", lambda e, sc=sc, cb=cb, pb=pb: e.matmul(self.ps[pb][:, 0:512], lhsT=hdn[1][:, sc * 128:(sc + 1) * 128], rhs=wfilt[:, cb:cb + 512], start=True, stop=True),
                             reads=[("hdn", 1), "hyf"], writes=[("ps", pb)])
                        E.op("vector", lambda e, o=o, dr=dr, hf=hf, pb=pb: e.tensor_tensor(out=hw[o][:, dr, hf * 512:(hf + 1) * 512], in0=self.ps[pb][:, 0:512],
                                                                                           in1=win[:, hf * 512:(hf + 1) * 512], op=ALU.mult),
                             reads=[("ps", pb), "win"], writes=[("hw", o, dr)])
                if sc == 0:
                    E.op("vector", lambda e, o=o: e.memset(hw[o][0:1, 1, :], 0.0), reads=[("hw", o, 1)], writes=[("hw", o, 1)])
                E.op("scalar", lambda e, o=o: e.activation(out=sqt[:].rearrange("p a d -> p (a d)"), in_=hw[o][:].rearrange("p a d -> p (a d)"), func=AF.Square),
                     reads=[("hw", o, 0), ("hw", o, 1)], writes=["sqt"])
                for hf in range(2):
                    for dr in range(2):
                        E.op("tensor", lambda e, o=o, hf=hf, dr=dr, sc=sc: e.matmul(self.ps[4 + o * 2 + hf][:, 0:512], lhsT=ones32[:], rhs=sqt[:, dr, hf * 512:(hf + 1) * 512],
                                                                                   start=(sc == 0 and dr == 0), stop=(sc == nsc - 1 and dr == 1)),
                             reads=["sqt", "ones32"], writes=[("psn", o, hf)])
                E.op("gpsimd", lambda e, o=o, b=b: e.tensor_tensor(out=hsd[b][:, 0, o * D:(o + 1) * D], in0=hw[o][:, 0, :], in1=hw[o][:, 1, :], op=ALU.add),
                     reads=[("hw", o, 0), ("hw", o, 1)], writes=[("hsd", b, o)])
                E.op("gpsimd", lambda e, o=o, b=b: e.tensor_tensor(out=hsd[b][:, 1, o * D:(o + 1) * D], in0=hw[o][:, 0, :], in1=hw[o][:, 1, :], op=ALU.subtract),
                     reads=[("hw", o, 0), ("hw", o, 1)], writes=[("hsd", b, o)])
            E.dma("gpsimd", [lambda e, sc=sc, b=b: e.dma_start(out=D_["HSD"][sc], in_=hsd[b][:])], ("hsds", b), reads=[("hsd", b, 0), ("hsd", b, 1)], writes=[("HSDd", sc)])
        for o in range(2):
            for hf in range(2):
                sl_ = nrm[:, o * D + hf * 512: o * D + (hf + 1) * 512]
                E.op("scalar", lambda e, o=o, hf=hf, sl_=sl_: e.activation(out=sl_, in_=self.ps[4 + o * 2 + hf][:, 0:512], func=AF.Ln, bias=self.eps1[:, 0:1], scale=1.0),
                     reads=[("psn", o, hf)], writes=[("nrm", o, hf)])
                E.op("scalar", lambda e, sl_=sl_: e.activation(out=sl_, in_=sl_, func=AF.Exp, scale=-0.5), reads=[("nrm", o, hf)], writes=[("nrm", o, hf)])
        E.phase()
        nb2 = L // 128
        hs = E.tmp("hs", [128, nsc, 512], BF16)
        hd = E.tmp("hd", [128, nsc, 512], BF16)
        ftb = [E.tmp("ftb%d" % i, [128, 2, nsc, 128], BF16) for i in range(2)]
        fbb = E.tmp("fbb", [128, 2 * D], F32)
        kro = [E.tmp("kro%d" % i, [128, 2, 512], F32) for i in range(2)]
        E.dma("sync", [lambda e: e.dma_start(out=fbb[:], in_=self.hyb_bc)], "fbb", writes=["fbb"])
        for cb in range(4):
            o, hf = cb // 2, cb % 2
            E.dma("sync", [lambda e, cb=cb, q0=q0, w_=w_, dst_=dst_: e.dma_start(out=dst_[:, q0:q0 + 8, :], in_=D_["HSD"][q0:q0 + 8, :, w_, cb * 512:(cb + 1) * 512].rearrange("c p d -> p c d"))
                           for q0 in range(0, nsc, 8) for (w_, dst_) in ((0, hs), (1, hd))][:None] if nsc >= 8 else
                  [lambda e, cb=cb: e.dma_start(out=hs[:], in_=D_["HSD"][:, :, 0, cb * 512:(cb + 1) * 512].rearrange("c p d -> p c d")),
                   lambda e, cb=cb: e.dma_start(out=hd[:], in_=D_["HSD"][:, :, 1, cb * 512:(cb + 1) * 512].rearrange("c p d -> p c d"))],
                  "hshd", writes=["hshd"])
            nsl = nrm[:, cb * 512:(cb + 1) * 512]
            fsl = fbb[:, cb * 512:(cb + 1) * 512]
            for j in range(nb2):
                b = j % 2
                E.dma("scalar", [lambda e, j=j, b=b: e.dma_start(out=ftb[b][:, 0], in_=D_["FT"][j])], ("ftbA", b), writes=[("ftb", b, 0)])
                E.dma("gpsimd", [lambda e, j=j, b=b: e.dma_start(out=ftb[b][:, 1], in_=D_["FT"][nb2 + j])], ("ftbB", b), writes=[("ftb", b, 1)])
                pr, pi = (0, 1) if b == 0 else (2, 3)
                for sc in range(nsc):
                    E.op("tensor", lambda e, sc=sc, b=b, pr=pr: e.matmul(self.ps[pr][:, 0:512], lhsT=ftb[b][:, 0, sc, :], rhs=hs[:, sc, :], start=(sc == 0), stop=(sc == nsc - 1)),
                         reads=[("ftb", b, 0), "hshd"], writes=[("ps", pr)])
                for sc in range(nsc):
                    E.op("tensor", lambda e, sc=sc, b=b, pi=pi: e.matmul(self.ps[pi][:, 0:512], lhsT=ftb[b][:, 1, sc, :], rhs=hd[:, sc, :], start=(sc == 0), stop=(sc == nsc - 1)),
                         reads=[("ftb", b, 1), "hshd"], writes=[("ps", pi)])
                if j == 0:
                    for sc in range(nsc):
                        E.op("tensor", lambda e, sc=sc, b=b: e.matmul(self.ps[6][0:1, 0:512], lhsT=ftb[b][:, 1, sc, 0:1], rhs=hs[:, sc, :], start=(sc == 0), stop=(sc == nsc - 1)),
                             reads=[("ftb", b, 1), "hshd"], writes=[("ps", 6)])
                E.op("vector", lambda e, b=b, pr=pr, nsl=nsl: e.tensor_tensor(out=kro[b][:, 0, :], in0=self.ps[pr][:, 0:512], in1=nsl, op=ALU.mult),
                     reads=[("ps", pr), ("nrm", o, hf)], writes=[("kro", b, 0)])
                E.op("gpsimd", lambda e, b=b, fsl=fsl: e.tensor_tensor(out=kro[b][:, 0, :], in0=kro[b][:, 0, :], in1=fsl, op=ALU.add),
                     reads=[("kro", b, 0), "fbb"], writes=[("kro", b, 0)])
                E.op("vector", lambda e, b=b, pi=pi, nsl=nsl: e.tensor_tensor(out=kro[b][:, 1, :], in0=self.ps[pi][:, 0:512], in1=nsl, op=ALU.mult),
                     reads=[("ps", pi), ("nrm", o, hf)], writes=[("kro", b, 1)])
                if j == 0:
                    E.op("vector", lambda e, b=b, nsl=nsl: e.tensor_tensor(out=kro[b][0:1, 1, :], in0=self.ps[6][0:1, 0:512], in1=nsl[0:1, :], op=ALU.mult),
                         reads=[("ps", 6), ("kro", b, 1)], writes=[("kro", b, 1)])
                    E.op("vector", lambda e, b=b, fsl=fsl: e.tensor_tensor(out=kro[b][0:1, 1, :], in0=kro[b][0:1, 1, :], in1=fsl[0:1, :], op=ALU.add),
                         reads=[("kro", b, 1), "fbb"], writes=[("kro", b, 1)])
                E.dma("gpsimd", [lambda e, b=b, j=j, o=o, hf=hf: e.dma_start(out=D_["KS"][o, :, j * 128:(j + 1) * 128, hf * 512:(hf + 1) * 512].rearrange("a p d -> p a d"), in_=kro[b][:])],
                      ("kros", b), reads=[("kro", b, 0), ("kro", b, 1)], writes=[("KSd", cb, j)])
        E.barrier()
        E.sb_base = saved_base

    def _hy_proj(self, layer, L, col0, cond, nm):
        E = self.E
        _ = (self.hy_w_in,)
        D_ = self.get_hyd(nm)
        nsc = L // 128
        E.phase()
        u_all = E.tmp("u_all", [128, 8, L], BF16)
        hb = [E.tmp("hb%d" % i, [128, 8, 512], F32) for i in range(2)]
        sq = E.tmp("sq", [128, 8, 512], BF16)
        rstd = E.tmp("rstd", [128, 512], F32)
        tmpn = [E.tmp("tmpn%d" % i, [128, 512], F32) for i in range(2)]
        wg = [E.tmp("wg%d" % i, [128, 8, 512], BF16) for i in range(2)]
        ppad = [E.tmp("ppad%d" % i, [128, L + 2], F32) for i in range(1)] * 2
        acc = E.tmp("acc", [128, L], F32)
        pcb = [E.tmp("pcb%d" % i, [128, L], BF16) for i in range(1)] * 2
        tokm = [E.tmp("tokm%d" % i, [128, nsc, 128], BF16) for i in range(2)]
        tiles = [(col0 + t0, min(512, L - t0), cond) for t0 in range(0, L, 512)]
        self.load_norm_tiles(tiles, layer, 1, lambda ti, c: u_all[:, c, ti * 512:ti * 512 + tiles[ti][1]], hb, sq, rstd, tmpn, "yh")
        ukeys = [("yho", ti) for ti in range(len(tiles))]
        for i in range(2):
            E.op("gpsimd", lambda e, i=i: e.memset(ppad[i][:, 0:1], 0.0), writes=[("ppz", i)])
            E.op("gpsimd", lambda e, i=i: e.memset(ppad[i][:, L + 1:L + 2], 0.0), writes=[("ppz", i)])
        wv = self.hy_w_in.rearrange("(kc p) f -> p kc f", p=128)
        bio, _ = VEC_LAYOUT["hy_b_in"]
        wso, _ = VEC_LAYOUT["hy_w_short"]
        bso, _ = VEC_LAYOUT["hy_b_short"]

        def loadw(g):
            b = g % 2
            E.dma("gpsimd", [lambda e, b=b, g=g: e.dma_start(out=wg[b][:], in_=wv[:, :, g * 512:(g + 1) * 512])], ("ywg", b), writes=[("ywg", b)])

        loadw(0)
        cnt = 0
        for g in range(6):
            b = g % 2
            if g + 1 < 6:
                loadw(g + 1)
            for f4 in range(4):
                fc = g * 4 + f4
                pb_ = 0
                pp = ppad[pb_]
                for ti, (c0, n, _) in enumerate(tiles):
                    pa = cnt % 4
                    cnt += 1
                    for kc in range(8):
                        E.op("tensor", lambda e, kc=kc, f4=f4, ti=ti, n=n, pa=pa, b=b: e.matmul(self.ps[pa][:, 0:n], lhsT=wg[b][:, kc, f4 * 128:(f4 + 1) * 128],
                                                                                               rhs=u_all[:, kc, ti * 512:ti * 512 + n], start=(kc == 0), stop=(kc == 7)),
                             reads=[("ywg", b), ukeys[ti]], writes=[("ps", pa)])
                    E.op("vector", lambda e, ti=ti, n=n, pa=pa, pp=pp, fc=fc: e.tensor_scalar(out=pp[:, 1 + ti * 512:1 + ti * 512 + n], in0=self.ps[pa][:, 0:n],
                                                                                            scalar1=self.V[:, bio + fc:bio + fc + 1], scalar2=None, op0=ALU.add),
                         reads=[("ps", pa), "vecs", ("ppz", pb_)], writes=[("pp", pb_)])
                w0 = self.V[:, wso + fc:wso + fc + 1]
                w1 = self.V[:, wso + 24 + fc:wso + 24 + fc + 1]
                w2 = self.V[:, wso + 48 + fc:wso + 48 + fc + 1]
                bs_ = self.V[:, bso + fc:bso + fc + 1]
                E.op("vector", lambda e, pp=pp, w1=w1, bs_=bs_: e.tensor_scalar(out=acc[:], in0=pp[:, 1:L + 1], scalar1=w1, scalar2=bs_, op0=ALU.mult, op1=ALU.add),
                     reads=[("pp", pb_), "vecs"], writes=["acc"])
                E.op("vector", lambda e, pp=pp, w0=w0: e.scalar_tensor_tensor(out=acc[:], in0=pp[:, 0:L], scalar=w0, in1=acc[:], op0=ALU.mult, op1=ALU.add),
                     reads=[("pp", pb_), "acc"], writes=["acc"])
                E.op("vector", lambda e, pp=pp, w2=w2, pb_=pb_: e.scalar_tensor_tensor(out=pcb[pb_][:], in0=pp[:, 2:L + 2], scalar=w2, in1=acc[:], op0=ALU.mult, op1=ALU.add),
                     reads=[("pp", pb_), "acc"], writes=[("pcb", pb_)])
                tk = fc % 2
                for q in range(0, nsc, 4):
                    nq = min(4, nsc - q)
                    pa = 4 + (q // 4) % 2
                    for i in range(nq):
                        E.op("tensor", lambda e, q=q, i=i, pa=pa, pb_=pb_: e.matmul(self.ps[pa][:, i * 128:(i + 1) * 128], lhsT=pcb[pb_][:, (q + i) * 128:(q + i + 1) * 128], rhs=self.ident[:], start=True, stop=True),
                             reads=[("pcb", pb_)], writes=[("ps", pa)])
                    E.op("scalar", lambda e, q=q, nq=nq, pa=pa, tk=tk: e.activation(out=tokm[tk][:, q:q + nq, :].rearrange("p a b -> p (a b)"), in_=self.ps[pa][:, 0:nq * 128], func=AF.Copy),
                         reads=[("ps", pa)], writes=[("tokm", tk)])
                E.dma("scalar", [lambda e, fc=fc, tk=tk: e.dma_start(out=D_["VX"][fc // 8, fc % 8], in_=tokm[tk][:])], ("tokms", tk), reads=[("tokm", tk)], writes=[("VXd", fc)])

    def _hy_conv(self, L, nm):
        E = self.E
        D_ = self.get_hyd(nm)
        nsc = L // 128
        nb2 = L // 128
        nrc = 2 * L // 128
        E.phase()
        zin = E.tmp("zin", [128, nsc, 512], BF16)
        Y = E.tmp("Y", [128, nrc, 512], BF16)
        ftb = [E.tmp("ftb%d" % i, [128, 2, nsc, 128], BF16) for i in range(2)]
        gtb = [E.tmp("gtb%d" % i, [128, nrc, 128], BF16) for i in range(2)]
        ks = [E.tmp("ks%d" % i, [128, 2, 512], F32) for i in range(2)]
        tt = [E.tmp("tt%d" % i, [128, 512], F32) for i in range(4)]
        xt = [E.tmp("xt%d" % i, [128, 512], BF16) for i in range(2)]
        z2 = [E.tmp("z2%d" % i, [128, 512], BF16) for i in range(2)]
        zT = [E.tmp("zT%d" % i, [128, 4, 128], BF16) for i in range(2)]
        ZTv = D_["ZT"].rearrange("(c p) t -> p c t", p=128)
        gcnt = {"ft": 0, "gt": 0, "x": 0}
        for dblk in range(2):
            qs = 8 if nsc >= 8 else nsc
            E.dma("sync", [lambda e, i=i, dblk=dblk, q0=q0: e.dma_start(out=zin[:, q0:q0 + qs, i * 128:(i + 1) * 128], in_=D_["VX"][0, dblk * 4 + i, :, q0:q0 + qs, :])
                           for i in range(4) for q0 in range(0, nsc, qs)],
                  "zin", writes=["zin"] + [("zin1", sc) for sc in range(nsc)])
            for o in range(2):
                zkey = (lambda sc: "zin") if o == 0 else (lambda sc: ("zin1", sc))
                for j in range(nb2):
                    b = gcnt["ft"] % 2
                    gcnt["ft"] += 1
                    E.dma("scalar", [lambda e, j=j, b=b: e.dma_start(out=ftb[b][:, 0], in_=D_["FT"][j])], ("ftbA", b), writes=[("ftb", b, 0)])
                    E.dma("gpsimd", [lambda e, j=j, b=b: e.dma_start(out=ftb[b][:, 1], in_=D_["FT"][nb2 + j])], ("ftbB", b), writes=[("ftb", b, 1)])
                    E.dma("sync", [lambda e, j=j, b=b, o=o, dblk=dblk: e.dma_start(out=ks[b][:], in_=D_["KS"][o, :, j * 128:(j + 1) * 128, dblk * 512:(dblk + 1) * 512].rearrange("a p d -> p a d"))],
                          ("ks", b), writes=[("ks", b)])
                    pr, pi = (0, 1) if b == 0 else (2, 3)
                    for sc in range(nsc):
                        E.op("tensor", lambda e, sc=sc, b=b, pr=pr: e.matmul(self.ps[pr][:, 0:512], lhsT=ftb[b][:, 0, sc, :], rhs=zin[:, sc, :], start=(sc == 0), stop=(sc == nsc - 1)),
                             reads=[("ftb", b, 0), zkey(sc)], writes=[("ps", pr)])
                    for sc in range(nsc):
                        E.op("tensor", lambda e, sc=sc, b=b, pi=pi: e.matmul(self.ps[pi][:, 0:512], lhsT=ftb[b][:, 1, sc, :], rhs=zin[:, sc, :], start=(sc == 0), stop=(sc == nsc - 1)),
                             reads=[("ftb", b, 1), zkey(sc)], writes=[("ps", pi)])
                    KR, KI = ks[b][:, 0, :], ks[b][:, 1, :]
                    E.op("vector", lambda e, pr=pr, KR=KR: e.tensor_tensor(out=tt[0][:], in0=self.ps[pr][:, 0:512], in1=KR, op=ALU.mult), reads=[("ps", pr), ("ks", b)], writes=[("tt", 0)])
                    E.op("vector", lambda e, pi=pi, KI=KI: e.tensor_tensor(out=tt[1][:], in0=self.ps[pi][:, 0:512], in1=KI, op=ALU.mult), reads=[("ps", pi), ("ks", b)], writes=[("tt", 1)])
                    E.op("gpsimd", lambda e, j=j: e.tensor_tensor(out=Y[:, j, :], in0=tt[0][:], in1=tt[1][:], op=ALU.subtract), reads=[("tt", 0), ("tt", 1)], writes=[("Y", j)])
                    E.op("vector", lambda e, pr=pr, KI=KI: e.tensor_tensor(out=tt[2][:], in0=self.ps[pr][:, 0:512], in1=KI, op=ALU.mult), reads=[("ps", pr), ("ks", b)], writes=[("tt", 2)])
                    E.op("vector", lambda e, pi=pi, KR=KR: e.tensor_tensor(out=tt[3][:], in0=self.ps[pi][:, 0:512], in1=KR, op=ALU.mult), reads=[("ps", pi), ("ks", b)], writes=[("tt", 3)])
                    E.op("gpsimd", lambda e, j=j: e.tensor_tensor(out=Y[:, nb2 + j, :], in0=tt[2][:], in1=tt[3][:], op=ALU.add), reads=[("tt", 2), ("tt", 3)], writes=[("Y", nb2 + j)])
                    if j == 0:
                        E.op("gpsimd", lambda e: e.tensor_copy(out=Y[0:1, 0, :], in_=tt[0][0:1, :]), reads=[("tt", 0), ("Y", 0)], writes=[("Y", 0)])
                        E.op("gpsimd", lambda e: e.tensor_copy(out=Y[0:1, nb2, :], in_=tt[1][0:1, :]), reads=[("tt", 1), ("Y", nb2)], writes=[("Y", nb2)])
                for tb in range(nsc):
                    b = gcnt["gt"] % 2
                    gcnt["gt"] += 1
                    xb = gcnt["x"] % 2
                    gcnt["x"] += 1
                    hr = nrc // 2
                    E.dma("scalar", [lambda e, tb=tb, b=b: e.dma_start(out=gtb[b][:, 0:hr, :], in_=D_["GT"][tb, :, 0:hr, :])], ("gtbA", b), writes=[("gtb", b, 0)])
                    E.dma("gpsimd", [lambda e, tb=tb, b=b: e.dma_start(out=gtb[b][:, hr:nrc, :], in_=D_["GT"][tb, :, hr:nrc, :])], ("gtbB", b), writes=[("gtb", b, 1)])
                    E.dma("sync", [lambda e, i=i, tb=tb, xb=xb, o=o, dblk=dblk: e.dma_start(out=xt[xb][:, i * 128:(i + 1) * 128], in_=D_["VX"][1 + o, dblk * 4 + i, :, tb, :]) for i in range(4)],
                          ("xt", xb), writes=[("xt", xb)])
                    po = 4 + b
                    for rc in range(nrc):
                        E.op("tensor", lambda e, rc=rc, b=b, po=po: e.matmul(self.ps[po][:, 0:512], lhsT=gtb[b][:, rc, :], rhs=Y[:, rc, :], start=(rc == 0), stop=(rc == nrc - 1)),
                             reads=[("gtb", b, 0 if rc < nrc // 2 else 1), ("Y", rc)], writes=[("ps", po)])
                    if o == 0:
                        E.op("vector", lambda e, tb=tb, po=po, xb=xb: e.tensor_tensor(out=zin[:, tb, :], in0=self.ps[po][:, 0:512], in1=xt[xb][:], op=ALU.mult),
                             reads=[("ps", po), ("xt", xb)], writes=[("zin1", tb)])
                    else:
                        E.op("vector", lambda e, po=po, xb=xb: e.tensor_tensor(out=z2[xb][:], in0=self.ps[po][:, 0:512], in1=xt[xb][:], op=ALU.mult),
                             reads=[("ps", po), ("xt", xb)], writes=[("z2", xb)])
                        for i in range(4):
                            E.op("tensor", lambda e, i=i, xb=xb: e.matmul(self.ps[6 + xb][:, i * 128:(i + 1) * 128], lhsT=z2[xb][:, i * 128:(i + 1) * 128], rhs=self.ident[:], start=True, stop=True),
                                 reads=[("z2", xb)], writes=[("ps", 6 + xb)])
                        E.op("scalar", lambda e, xb=xb: e.activation(out=zT[xb][:].rearrange("p a b -> p (a b)"), in_=self.ps[6 + xb][:, 0:512], func=AF.Copy),
                             reads=[("ps", 6 + xb)], writes=[("zT", xb)])
                        E.dma("scalar", [lambda e, xb=xb, tb=tb, dblk=dblk: e.dma_start(out=ZTv[:, dblk * 4:(dblk + 1) * 4, tb * 128:(tb + 1) * 128], in_=zT[xb][:])],
                              ("zTs", xb), reads=[("zT", xb)], writes=[("ZTd", dblk, tb)])

    def _hy_out(self, layer, L, col0, cond, nm):
        E = self.E
        _ = (self.hy_w_out,)
        D_ = self.get_hyd(nm)
        E.phase()
        hv = self.H.rearrange("(c p) t -> p c t", p=128)
        ZTv = D_["ZT"].rearrange("(c p) t -> p c t", p=128)
        wo = E.tmp("hwo", [128, 8, D], BF16)
        hb = [E.tmp("hb%d" % i, [128, 8, 512], F32) for i in range(2)]
        zb = [E.tmp("zb%d" % i, [128, 8, 512], BF16) for i in range(2)]
        yv = E.tmp("yv", [128, 512], F32)
        E.dma("gpsimd", [lambda e: e.dma_start(out=wo[:], in_=self.hy_w_out.rearrange("(kc p) d -> p kc d", p=128))], "hwo", writes=["hwo"])
        boo, _ = VEC_LAYOUT["hy_b_out"]
        for ti, t0 in enumerate(range(0, L, 512)):
            n = min(512, L - t0)
            b = ti % 2
            E.dma("sync", [lambda e, b=b, t0=t0, n=n: e.dma_start(out=hb[b][:, :, 0:n], in_=hv[:, :, col0 + t0:col0 + t0 + n])], ("oh", b), writes=[("oh", b)])
            E.dma("sync", [lambda e, b=b, t0=t0, n=n: e.dma_start(out=zb[b][:, :, 0:n], in_=ZTv[:, :, t0:t0 + n])], ("oz", b), writes=[("oz", b)])
            for dc in range(8):
                pa = dc % 2
                for kc in range(8):
                    E.op("tensor", lambda e, dc=dc, kc=kc, b=b, n=n, pa=pa: e.matmul(self.ps[pa][:, 0:n], lhsT=wo[:, kc, dc * 128:(dc + 1) * 128], rhs=zb[b][:, kc, 0:n], start=(kc == 0), stop=(kc == 7)),
                         reads=["hwo", ("oz", b)], writes=[("ps", pa)])
                E.op("vector", lambda e, dc=dc, n=n, pa=pa: e.tensor_scalar(out=yv[:, 0:n], in0=self.ps[pa][:, 0:n], scalar1=self.V[:, boo + dc:boo + dc + 1], scalar2=None, op0=ALU.add),
                     reads=[("ps", pa), "vecs"], writes=["yv"])
                gate = self.AD[:, layer, 1, 2, dc, cond:cond + 1]
                E.op("vector", lambda e, dc=dc, n=n, b=b, gate=gate: e.scalar_tensor_tensor(out=hb[b][:, dc, 0:n], in0=yv[:, 0:n], scalar=gate, in1=hb[b][:, dc, 0:n], op0=ALU.mult, op1=ALU.add),
                     reads=["yv", ("oh", b), "AD"], writes=[("oho", b, dc)])
            E.dma("gpsimd", [lambda e, b=b, t0=t0, n=n: e.dma_start(out=hv[:, :, col0 + t0:col0 + t0 + n], in_=hb[b][:, :, 0:n])], ("ohs", b),
                  reads=[("oho", b, dc) for dc in range(8)] + [("oh", b)], writes=[("Ho", ti)])


VEC_LAYOUT = {}
NVEC = 0


def _vreg(name, shape):
    global NVEC
    VEC_LAYOUT[name] = (NVEC, tuple(shape))
    NVEC += int(np.prod(shape))


_vreg("cond", (8, 2))
_vreg("ada_b", (DEPTH, 72, 2))
_vreg("norm_g", (DEPTH, 3, 8))
_vreg("final_g", (8,))
_vreg("pool_b", (8,))
_vreg("pool_scale", (8,))
_vreg("ret_decay", (2, 8))
_vreg("hy_b_in", (24,))
_vreg("hy_w_short", (3, 24))
_vreg("hy_b_short", (24,))
_vreg("hy_b_out", (8,))
RETC_N = 2 * 128 + 2 * 128 + 2 * 128 + 2


def _pack_vecs(inp, b):
    V = np.zeros((128, NVEC), np.float32)

    def put(name, arr):
        off, shape = VEC_LAYOUT[name]
        n = int(np.prod(shape))
        V[:, off:off + n] = np.asarray(arr, np.float32).reshape(128, n)

    cond = np.stack([_fm(inp["c"][b]), _fm(inp["c_ctx"])], axis=-1)
    put("cond", cond)
    ab = _fm(inp["ada_b"])
    put("ada_b", np.repeat(ab[..., None], 2, axis=-1))
    put("norm_g", _fm(inp["norm_g"]))
    put("final_g", _fm(inp["final_g"]))
    put("pool_b", _fm(inp["pool_b"][0]))
    put("pool_scale", _fm(inp["pool_scale"][0]))
    put("hy_b_in", _fm(inp["hy_b_in"][0]))
    put("hy_w_short", _fm(inp["hy_w_short"][0]))
    put("hy_b_short", _fm(inp["hy_b_short"][0]))
    put("hy_b_out", _fm(inp["hy_b_out"][0]))
    put("ret_decay", np.broadcast_to(np.asarray(inp["ret_decay"], np.float32).reshape(1, 2, 8), (128, 2, 8)))
    return V


def _pool_rc():
    rc = np.zeros((4, 512 + 256), np.float32)
    for gi, win in enumerate((2, 4, 8, 16)):
        for w, off, reps in ((64, 0, 8), (256, 512, 1)):
            pos = np.arange(w)
            lo = np.clip(pos - win // 2, 0, w)
            hi = np.clip(pos - win // 2 + win, 0, w)
            r = (1.0 / (hi - lo)).astype(np.float32)
            rc[gi, off:off + w * reps] = np.tile(r, reps)
    return np.ascontiguousarray(np.broadcast_to(rc[None], (128, 4, 768)))


def _hy_consts(L):
    N = 2 * L
    nsc = L // 128
    s_ = np.arange(L, dtype=np.int64)
    r = np.arange(N, dtype=np.int64)
    f = np.where(r < L, r, r - L)
    ang = (2.0 * np.pi / N) * ((s_[:, None] * f[None, :]) % N)
    FT = np.where((r <= L)[None, :], np.cos(ang), -np.sin(ang))
    FT[:, L] = np.where(s_ % 2 == 0, 1.0, -1.0)
    cf = np.where((f == 0) | (r == L), 1.0, 2.0) / N
    GT = (FT * cf[None, :]).T
    FTt = FT.reshape(nsc, 128, 2 * nsc, 128).transpose(2, 1, 0, 3)
    GTt = GT.reshape(2 * nsc, 128, nsc, 128).transpose(2, 1, 0, 3)
    t = np.linspace(0.0, 1.0, L, dtype=np.float32)
    bands = np.linspace(1e-4, 15.0, 16, dtype=np.float32)
    a2 = (np.float32(2.0 * math.pi / L) * np.arange(L, dtype=np.float32)[:, None] * bands[None, :]).astype(np.float32)
    feat = np.concatenate([t[:, None], np.cos(a2), -np.sin(a2)], axis=-1).astype(np.float32)
    tneg = (-t).reshape(nsc, 128).T
    return dict(FT=np.ascontiguousarray(FTt).astype(ml_dtypes.bfloat16), GT=np.ascontiguousarray(GTt).astype(ml_dtypes.bfloat16),
                featT=np.ascontiguousarray(feat.T), tneg=np.ascontiguousarray(tneg, np.float32))


_CACHE = {}


def _get_prog(stop_after=None):
    key = stop_after
    if key not in _CACHE:
        p = Prog(stop_after)
        with contextlib.ExitStack() as st:
            p.build()
            p.E.emit(st)
        _CACHE[key] = p
    return _CACHE[key]


def kernel(stop_after=None, **inp):
    inp = {k: np.asarray(v) for k, v in inp.items()}
    p = _get_prog(stop_after)
    cosT, sinT = _rot_tables()
    rc = _ret_consts()
    retc = np.concatenate([rc["rel"][0], rc["rel"][1], rc["mask"][0], rc["mask"][1], rc["qexp"][0], rc["qexp"][1],
                           rc["kexp"].T], axis=1).astype(np.float32)
    shared = {
        "ada_w": np.ascontiguousarray(inp["ada_w"], np.float32),
        "ffn_w1": np.ascontiguousarray(inp["ffn_w1"], np.float32),
        "ffn_w3": np.ascontiguousarray(inp["ffn_w3"], np.float32),
        "ffn_w2": np.ascontiguousarray(inp["ffn_w2"], np.float32),
        "ret_w_in": np.ascontiguousarray(inp["ret_w_in"], np.float32),
        "ret_w_out": np.ascontiguousarray(inp["ret_w_out"], np.float32),
        "pool_w": np.ascontiguousarray(inp["pool_w"][0], np.float32),
        "rot": np.stack([cosT, sinT]),
        "retc": np.ascontiguousarray(retc),
        "poolrc": _pool_rc(),
        "identb": np.eye(128, dtype=np.float32).astype(ml_dtypes.bfloat16),
        "hy_w_in": np.ascontiguousarray(inp["hy_w_in"][0], np.float32),
        "hy_w_out": np.ascontiguousarray(inp["hy_w_out"][0], np.float32),
        "hy_w_pos": np.ascontiguousarray(inp["hy_w_pos"][0], np.float32),
        "hy_w_mid": np.ascontiguousarray(inp["hy_w_mid"][0], np.float32),
        "hy_w_filt": np.ascontiguousarray(inp["hy_w_filt"][0], np.float32),
    }
    hv = np.zeros((64, 8), np.float32)
    hv[:, 0] = inp["hy_b_pos"][0]
    hv[:, 1] = inp["hy_b_mid"][0]
    hv[:, 2] = inp["hy_freq"][0]
    shared["hyv64"] = hv
    min_decay = math.log(1e-2) / 1.5
    max_decay = math.log(1e-2) / 0.3
    shared["hy_delta"] = np.ascontiguousarray(np.broadcast_to(np.abs(np.linspace(min_decay, max_decay, D, dtype=np.float32))[None], (128, D)))
    shared["hyb_bc"] = np.ascontiguousarray(np.broadcast_to(np.asarray(inp["hy_bias"][0], np.float32).reshape(1, 2 * D), (128, 2 * D)))
    for nm, L in (("l", SEQ), ("c", CTX)):
        if ("FT_" + nm) not in p.inputs:
            continue
        if "hyc_" + nm not in _CACHE:
            _CACHE["hyc_" + nm] = _hy_consts(L)
        for k, v in _CACHE["hyc_" + nm].items():
            shared[k + "_" + nm] = v
    in_maps = []
    for b in range(NCORES):
        m = dict(shared)
        m["h0"] = np.ascontiguousarray(np.concatenate([inp["ctx"][b].T, inp["x"][b].T], axis=1), np.float32)
        m["vecs"] = _pack_vecs(inp, b)
        in_maps.append({k: v for k, v in m.items() if k in p.inputs})
    if TEST_CORES is not None:
        res = run_bass_kernel_spmd(p.nc, in_maps[:TEST_CORES], core_ids=list(range(TEST_CORES)))
        DEBUG_OUT.update({k: np.asarray(v) for k, v in res.results[0].items()})
        return None
    if SPREAD:
        zero_map = {k: np.zeros_like(v) for k, v in in_maps[0].items()}
        full = [zero_map] * 8
        for b, cidx in enumerate(ACTIVE_CORES):
            full[cidx] = in_maps[b]
        res = run_bass_kernel_spmd(p.nc, full, core_ids=list(range(8)))
        out = np.stack([np.ascontiguousarray(res.results[cidx]["outT"].T) for cidx in ACTIVE_CORES])
        return out.astype(np.float32)
    res = run_bass_kernel_spmd(p.nc, in_maps, core_ids=list(range(NCORES)))
    if DEBUG_DUMP:
        DEBUG_OUT.update({k: np.asarray(v) for k, v in res.results[0].items()})
    out = np.stack([np.ascontiguousarray(res.results[b]["outT"].T) for b in range(NCORES)])
    return out.astype(np.float32)
```
